# Optimizing a Trainium2 kernel written in Bass

```python
import math
import jax, jax.numpy as jnp
from jax import lax
import numpy as np

D_MODEL = 1024
BATCH = 16
SEQ = 2048
DEPTH = 2

N_META = 16
Q_BLOCK = 128
ROPE_THETA = 10000.0
EPS = 1e-6

MLA_NOPE_DIM = 128
MLA_ROPE_DIM = 64
MLA_V_DIM = 128
MLA_QK_DIM = MLA_NOPE_DIM + MLA_ROPE_DIM
MLA_HEADS = (D_MODEL // 2) // MLA_V_DIM
MLA_Q_RANK = D_MODEL // 4
MLA_KV_RANK = D_MODEL // 8

DIFF_HEAD_DIM = 64
DIFF_V_DIM = 2 * DIFF_HEAD_DIM
DIFF_HEADS = (D_MODEL // 2) // DIFF_V_DIM

MLA_OUT = MLA_HEADS * MLA_V_DIM
DIFF_OUT = DIFF_HEADS * DIFF_V_DIM
MIX_WIDTH = MLA_OUT + DIFF_OUT

IN_SIZES = (MLA_Q_RANK, MLA_KV_RANK, MLA_ROPE_DIM,
            DIFF_HEADS * 2 * DIFF_HEAD_DIM, DIFF_HEADS * 2 * DIFF_HEAD_DIM, DIFF_OUT)
IN_WIDTH = sum(IN_SIZES)

D_FF = -(-8 * D_MODEL // (3 * 256)) * 256

kernel_name = 'hymba_mla_diffattn_swiglu'


def rms_norm(x, g):
    xf = x.astype(jnp.float32)
    y = xf * lax.rsqrt(jnp.mean(xf * xf, axis=-1, keepdims=True) + EPS)
    return (y * g.astype(jnp.float32)).astype(x.dtype)


def rope_tables(n, dim):
    pos = jnp.arange(n, dtype=jnp.float32)
    inv = 1.0 / (ROPE_THETA ** (jnp.arange(0, dim, 2, dtype=jnp.float32) / dim))
    ang = pos[:, None] * inv[None, :]
    emb = jnp.concatenate([ang, ang], axis=-1)
    return jnp.cos(emb), jnp.sin(emb)


def apply_rope(x, cos, sin):
    n, dim = cos.shape
    shape = (1, n) + (1,) * (x.ndim - 3) + (dim,)
    c = cos.reshape(shape).astype(x.dtype)
    s = sin.reshape(shape).astype(x.dtype)
    half = dim // 2
    rot = jnp.concatenate([-x[..., half:], x[..., :half]], axis=-1)
    return x * c + rot * s


def query_blocks(n_total):
    bounds = [(0, N_META)]
    for s in range(N_META, n_total, Q_BLOCK):
        bounds.append((s, min(s + Q_BLOCK, n_total)))
    return bounds


def causal_probs(scores, q_start):
    tq, tk = scores.shape[-2], scores.shape[-1]
    q_idx = q_start + jnp.arange(tq)
    k_idx = jnp.arange(tk)
    mask = k_idx[None, :] <= q_idx[:, None]
    scores = jnp.where(mask, scores, jnp.finfo(jnp.float32).min)
    return jax.nn.softmax(scores, axis=-1)


def mla_attention(q, k, v):
    scale = MLA_QK_DIM ** -0.5
    outs = []
    for s, e in query_blocks(q.shape[1]):
        sc = jnp.einsum('bqhd,bkhd->bhqk', q[:, s:e], k[:, :e],
                        preferred_element_type=jnp.float32) * scale
        p = causal_probs(sc, s)
        outs.append(jnp.einsum('bhqk,bkhd->bqhd', p.astype(v.dtype), v[:, :e]))
    return jnp.concatenate(outs, axis=1)


def diff_attention(q, k, v, lam):
    scale = DIFF_HEAD_DIM ** -0.5
    outs = []
    for s, e in query_blocks(q.shape[1]):
        sc = jnp.einsum('bqhcd,bkhcd->bchqk', q[:, s:e], k[:, :e],
                        preferred_element_type=jnp.float32) * scale
        p = causal_probs(sc, s)
        attn = p[:, 0] - lam * p[:, 1]
        outs.append(jnp.einsum('bhqk,bkhd->bqhd', attn.astype(v.dtype), v[:, :e]))
    return jnp.concatenate(outs, axis=1)


def setup_inputs(seed: int = 0) -> dict:
    key = jax.random.key(seed)
    ks = jax.random.split(key, 24)
    f32 = jnp.float32

    def w(k, shape, fan_in):
        return jax.random.normal(k, shape, f32) * (fan_in ** -0.5)

    def gain(k, shape):
        return 1.0 + 0.02 * jax.random.normal(k, shape, f32)

    return {
        'x': jax.random.normal(ks[0], (BATCH, SEQ, D_MODEL), f32),
        'meta_tokens': jax.random.normal(ks[1], (N_META, D_MODEL), f32),
        'attn_norm': gain(ks[2], (DEPTH, D_MODEL)),
        'w_in': w(ks[3], (DEPTH, D_MODEL, IN_WIDTH), D_MODEL),
        'mla_q_a_norm': gain(ks[4], (DEPTH, MLA_Q_RANK)),
        'w_q_up': w(ks[5], (DEPTH, MLA_Q_RANK, MLA_HEADS * MLA_QK_DIM), MLA_Q_RANK),
        'mla_kv_a_norm': gain(ks[6], (DEPTH, MLA_KV_RANK)),
        'w_kv_up': w(ks[7], (DEPTH, MLA_KV_RANK, MLA_HEADS * (MLA_NOPE_DIM + MLA_V_DIM)), MLA_KV_RANK),
        'mla_q_norm': gain(ks[8], (DEPTH, MLA_QK_DIM)),
        'mla_k_norm': gain(ks[9], (DEPTH, MLA_QK_DIM)),
        'diff_q_norm': gain(ks[10], (DEPTH, DIFF_HEAD_DIM)),
        'diff_k_norm': gain(ks[11], (DEPTH, DIFF_HEAD_DIM)),
        'lambda_q1': 0.1 * jax.random.normal(ks[12], (DEPTH, DIFF_HEAD_DIM), f32),
        'lambda_k1': 0.1 * jax.random.normal(ks[13], (DEPTH, DIFF_HEAD_DIM), f32),
        'lambda_q2': 0.1 * jax.random.normal(ks[14], (DEPTH, DIFF_HEAD_DIM), f32),
        'lambda_k2': 0.1 * jax.random.normal(ks[15], (DEPTH, DIFF_HEAD_DIM), f32),
        'diff_subln': gain(ks[16], (DEPTH, DIFF_V_DIM)),
        'w_o': w(ks[17], (DEPTH, MIX_WIDTH, D_MODEL), MIX_WIDTH),
        'ffn_norm': gain(ks[18], (DEPTH, D_MODEL)),
        'w_gate_up': w(ks[19], (DEPTH, D_MODEL, 2 * D_FF), D_MODEL),
        'w_down': w(ks[20], (DEPTH, D_FF, D_MODEL), D_FF),
    }


def reference(x, meta_tokens, attn_norm, w_in, mla_q_a_norm, w_q_up, mla_kv_a_norm,
              w_kv_up, mla_q_norm, mla_k_norm, diff_q_norm, diff_k_norm,
              lambda_q1, lambda_k1, lambda_q2, lambda_k2, diff_subln, w_o,
              ffn_norm, w_gate_up, w_down):
    b = x.shape[0]
    meta = jnp.broadcast_to(meta_tokens[None].astype(x.dtype), (b, N_META, x.shape[2]))
    h_res = jnp.concatenate([meta, x], axis=1)
    n = h_res.shape[1]
    cos_a, sin_a = rope_tables(n, MLA_ROPE_DIM)
    cos_b, sin_b = rope_tables(n, DIFF_HEAD_DIM)
    split_pts = [sum(IN_SIZES[:i + 1]) for i in range(len(IN_SIZES) - 1)]

    for l in range(DEPTH):
        hn = rms_norm(h_res, attn_norm[l])
        proj = hn @ w_in[l]
        cq, ckv, kr, dq, dk, dv = jnp.split(proj, split_pts, axis=-1)

        q = (rms_norm(cq, mla_q_a_norm[l]) @ w_q_up[l]).reshape(b, n, MLA_HEADS, MLA_QK_DIM)
        kv = (rms_norm(ckv, mla_kv_a_norm[l]) @ w_kv_up[l]).reshape(
            b, n, MLA_HEADS, MLA_NOPE_DIM + MLA_V_DIM)
        k_nope, v_a = kv[..., :MLA_NOPE_DIM], kv[..., MLA_NOPE_DIM:]
        k_rope = jnp.broadcast_to(kr[:, :, None, :], (b, n, MLA_HEADS, MLA_ROPE_DIM))
        k = jnp.concatenate([k_nope, k_rope], axis=-1)
        q = rms_norm(q, mla_q_norm[l])
        k = rms_norm(k, mla_k_norm[l])
        q = jnp.concatenate([q[..., :MLA_NOPE_DIM],
                             apply_rope(q[..., MLA_NOPE_DIM:], cos_a, sin_a)], axis=-1)
        k = jnp.concatenate([k[..., :MLA_NOPE_DIM],
                             apply_rope(k[..., MLA_NOPE_DIM:], cos_a, sin_a)], axis=-1)
        o_a = mla_attention(q, k, v_a).reshape(b, n, MLA_OUT)

        qd = rms_norm(dq.reshape(b, n, DIFF_HEADS, 2, DIFF_HEAD_DIM), diff_q_norm[l])
        kd = rms_norm(dk.reshape(b, n, DIFF_HEADS, 2, DIFF_HEAD_DIM), diff_k_norm[l])
        qd = apply_rope(qd, cos_b, sin_b)
        kd = apply_rope(kd, cos_b, sin_b)
        vd = dv.reshape(b, n, DIFF_HEADS, DIFF_V_DIM)
        lam_init = 0.8 - 0.6 * math.exp(-0.3 * l)
        lam = (jnp.exp(jnp.sum(lambda_q1[l].astype(jnp.float32) * lambda_k1[l].astype(jnp.float32)))
               - jnp.exp(jnp.sum(lambda_q2[l].astype(jnp.float32) * lambda_k2[l].astype(jnp.float32)))
               + lam_init)
        o_b = diff_attention(qd, kd, vd, lam)
        o_b = (rms_norm(o_b, diff_subln[l]) * (1.0 - lam_init)).reshape(b, n, DIFF_OUT)

        h_res = h_res + jnp.concatenate([o_a, o_b], axis=-1) @ w_o[l]

        hn = rms_norm(h_res, ffn_norm[l])
        gu = hn @ w_gate_up[l]
        g, u = gu[..., :D_FF], gu[..., D_FF:]
        h_res = h_res + (jax.nn.silu(g) * u) @ w_down[l]

    return h_res[:, N_META:]
```

```python
import math
import os
import numpy as np
import concourse.bass as bass
import concourse.mybir as mybir
from concourse.bass_utils import run_bass_kernel_spmd

F32 = mybir.dt.float32
BF16 = mybir.dt.bfloat16
AF = mybir.ActivationFunctionType
ALU = mybir.AluOpType
AX = mybir.AxisListType

D = 1024
SEQ = 2048
NMETA = 16
LTOK = SEQ + NMETA
NT = 17
NG = 5
DFF = 2816
DEPTH = 2
EPS = 1e-6
NPV = 788
NEG = -30000.0
KCUT = int(os.environ.get("KCUT", "99"))
N_CORES = 8
SEQ_PER_CORE = 2


def tsz(i):
    return 128 if i < 16 else 16


def gsz(g):
    return 512 if g < 4 else 16


def gtiles(g):
    return list(range(4 * g, min(4 * g + 4, NT)))


class Op:
    __slots__ = ("eng", "fn", "r", "w", "dma", "deps", "needs_inc", "inc_val", "waits")

    def __init__(self, eng, fn, r, w, dma):
        self.eng = eng
        self.fn = fn
        self.r = r
        self.w = w
        self.dma = dma
        self.deps = ()
        self.needs_inc = False
        self.inc_val = 0
        self.waits = ()


class Sched:
    def __init__(self):
        self.ops = []

    def add(self, eng, fn, r=(), w=(), dma=None):
        w = tuple(w) + tuple(k for k in r if isinstance(k, tuple) and k[0] == "ps" and k not in w)
        self.ops.append(Op(eng, fn, tuple(r), w, dma))

    def resolve(self):
        ops = self.ops
        last_w = {}
        readers = {}
        for i, op in enumerate(ops):
            deps = set()
            for k in op.r:
                j = last_w.get(k)
                if j is not None:
                    deps.add(j)
            for k in op.w:
                j = last_w.get(k)
                if j is not None:
                    deps.add(j)
                rd = readers.get(k)
                if rd:
                    deps.update(rd.values())
            deps.discard(i)
            op.deps = deps
            src = ("dma", op.dma) if op.dma is not None else op.eng
            for k in op.r:
                readers.setdefault(k, {})[src] = i
            for k in op.w:
                last_w[k] = i
                readers[k] = {}

        def skip(pj, op):
            if pj.dma is None and op.dma is None:
                return pj.eng == "pe" and op.eng == "pe"
            return pj.dma is not None and op.dma is not None and pj.dma == op.dma

        for op in ops:
            for j in op.deps:
                pj = ops[j]
                if pj.dma is None and not skip(pj, op):
                    pj.needs_inc = True
        cnt = {}
        for op in ops:
            if op.dma is None and op.needs_inc:
                cnt[op.eng] = cnt.get(op.eng, 0) + 1
                op.inc_val = cnt[op.eng]
        dma_cnt = {}
        waited = {}
        for op in ops:
            waits = {}
            for j in op.deps:
                pj = ops[j]
                if skip(pj, op):
                    continue
                if pj.dma is None:
                    key = ("eng", pj.eng)
                    val = pj.inc_val
                else:
                    key = ("dma", pj.dma)
                    val = 16 * dma_cnt[pj.dma]
                if waits.get(key, 0) < val:
                    waits[key] = val
            we = waited.setdefault(op.eng, {})
            op.waits = [(k, v) for k, v in waits.items() if we.get(k, 0) < v]
            for k, v in op.waits:
                we[k] = v
            if op.dma is not None:
                dma_cnt[op.dma] = dma_cnt.get(op.dma, 0) + 1
        self.dma_cnt = dma_cnt
        keys = set(("eng", e) for e in cnt)
        keys.update(("dma", g) for g in dma_cnt)
        return sorted(keys, key=str)


def build(n_seq=SEQ_PER_CORE, layers=(0, 1), dbg=15, dbg_groups=NG, dbg_attn=True):
    nc = bass.Bass("TRN2", target_bir_lowering=False)
    x_d = nc.dram_tensor("x", [n_seq, SEQ, D], F32, kind="ExternalInput").ap()
    meta_d = nc.dram_tensor("meta", [NMETA, D], F32, kind="ExternalInput").ap()
    pv_d = nc.dram_tensor("pv", [DEPTH, 128, NPV], F32, kind="ExternalInput").ap()
    cst_d = nc.dram_tensor("cst", [128, 384], F32, kind="ExternalInput").ap()
    rope_d = nc.dram_tensor("rope", [128, 2 * NT * 64], F32, kind="ExternalInput").ap()
    win_d = nc.dram_tensor("w_in", [DEPTH, D, 1984], F32, kind="ExternalInput").ap()
    wq_d = nc.dram_tensor("w_qup", [DEPTH, 256, 768], F32, kind="ExternalInput").ap()
    wkv_d = nc.dram_tensor("w_kvup", [DEPTH, 128, 1024], F32, kind="ExternalInput").ap()
    wo_d = nc.dram_tensor("w_o", [DEPTH, D, D], F32, kind="ExternalInput").ap()
    wgu_d = nc.dram_tensor("w_gu", [DEPTH, D, 2 * DFF], F32, kind="ExternalInput").ap()
    wdn_d = nc.dram_tensor("w_dn", [DEPTH, DFF, D], F32, kind="ExternalInput").ap()
    out_d = nc.dram_tensor("out", [n_seq, SEQ, D], F32, kind="ExternalOutput").ap()

    S = Sched()

    def PE(fn, r=(), w=()):
        S.add("pe", fn, r, w)

    def ACT(fn, r=(), w=()):
        S.add("act", fn, r, w)

    def DVE(fn, r=(), w=()):
        S.add("dve", fn, r, w)

    def DMA(eng, grp, fn, r=(), w=()):
        S.add(eng, fn, r, w, dma=grp)

    import contextlib
    with contextlib.ExitStack() as es:
        def sb(name, shape, dt):
            return es.enter_context(nc.sbuf_tensor(name, shape, dt))

        h = sb("h", [128, NT, D], F32)
        XT = sb("XT", [128, 8, LTOK], BF16)
        WA = sb("WA", [128, 8, 1024], BF16)
        WS = sb("WS", [128, 2560], BF16)
        B = sb("B", [128, 12288], BF16)
        pv = sb("pv_sb", [128, DEPTH, NPV], F32)
        cst = sb("cst_bf", [128, 384], BF16)
        ones_f = sb("ones_f", [128, 128], F32)
        rope = sb("rope_sb", [128, 2 * NT * 64], F32)
        junk = [sb(f"junk{p}", [128, 1024], BF16) for p in range(2)]
        fence_t = sb("fence_t", [128, 8], F32)
        lamt = sb("lamt", [128, 8], F32)
        hs = [sb(f"hs{p}", [128, 1024], BF16) for p in range(2)]
        RS = sb("RS", [128, NT, 2], F32)
        OG = [sb(f"OG{p}", [128, 2, 512], BF16) for p in range(2)]
        latn = [sb(f"latn{p}", [128, 384], BF16) for p in range(2)]
        latT = [sb(f"latT{p}", [128, 3, 128], BF16) for p in range(2)]
        kr_sb = [sb(f"kr{p}", [128, 64], F32) for p in range(2)]
        qb = [sb(f"qb{p}", [128, 512], BF16) for p in range(2)]
        kb_ = [sb(f"kb{p}", [128, 384], BF16) for p in range(2)]
        st = [sb(f"st{p}", [128, 64], F32) for p in range(2)]
        Ft = [sb(f"F{n}", [128, 512], F32) for n in range(8)] + [sb(f"F{n}", [128, 384 if n in (8, 11) else 64], F32) for n in range(8, 14)]
        Ht = [sb(f"H{n}", [128, 512], BF16) for n in range(4)]
        ps = [es.enter_context(nc.psum_tensor(f"ps{b}", [128, 512], F32)) for b in range(8)]
        psb = [p.bitcast(BF16) for p in ps]

        ident = cst[:, 0:128]
        maskb = cst[:, 128:256]
        ones_b = cst[:, 256:384]

        def b3(off, a, b):
            return B[:, off:off + a * b].rearrange("p (a b) -> p a b", a=a)

        KnT = b3(0, 2, LTOK)
        KrT = B[:, 4128:6192]
        Vv = b3(6192, NT, 256)
        QnT = b3(10544, 2, 512)
        QrT = B[:, 11568:12080]

        def ffn_views(slot):
            base = slot * 6144
            return (b3(base, 8, 256), b3(base + 2048, 8, 256), b3(base + 4096, 2, 1024))

        DMA("sp", "c0", lambda e: e.dma_start(out=pv[:], in_=pv_d.rearrange("l p n -> p l n")), w=["pv"])
        DMA("sp", "c0", lambda e: e.dma_start(out=rope[:], in_=rope_d), w=["rope"])
        DMA("pool", "c1", lambda e: e.dma_start(out=cst[:], in_=cst_d), w=["cst"])
        DVE(lambda e: e.memset(ones_f[:], 1.0), w=["onesf"])
        DVE(lambda e: e.memset(fence_t[:], 0.0), w=["B"])

        def fenceB():
            DVE(lambda e: e.memset(fence_t[:], 0.0), w=["B"])

        cosv = lambda i, ts: rope[:ts, i * 64:(i + 1) * 64]
        sinv = lambda i, ts: rope[:ts, NT * 64 + i * 64: NT * 64 + (i + 1) * 64]

        def norm_to_T(l, i, par, gcol, bank, dst_ap, dst_keys, jx=0, defer=False, rs_ap=None, rs_key=None):
            ts = tsz(i)
            s_ = st[par]
            if defer:
                ACT(lambda e: e.copy(out=hs[par][:ts, :], in_=h[:ts, i, :]), r=[("h", i)], w=[("hs", par)])
            ACT(lambda e: e.activation(out=junk[jx][:ts, :], in_=h[:ts, i, :], func=AF.Square,
                                       accum_out=s_[:ts, 0:1]), r=[("h", i)], w=[("junk", jx), ("st", par, 0)])
            ACT(lambda e: e.activation(out=s_[:ts, 1:2], in_=s_[:ts, 0:1], func=AF.Ln, bias=EPS, scale=1.0 / D),
                r=[("st", par, 0)], w=[("st", par, 1)])
            if defer:
                ACT(lambda e: e.activation(out=rs_ap[:, 0:1], in_=s_[:ts, 1:2], func=AF.Exp, scale=-0.5),
                    r=[("st", par, 1)], w=[rs_key])
                ACT(lambda e: e.activation(out=rs_ap[:, 1:2], in_=s_[:ts, 1:2], func=AF.Exp, bias=math.log(EPS), scale=1.0),
                    r=[("st", par, 1)], w=[rs_key])
            else:
                ACT(lambda e: e.activation(out=s_[:ts, 2:3], in_=s_[:ts, 1:2], func=AF.Exp, scale=-0.5),
                    r=[("st", par, 1)], w=[("st", par, 2)])
                ACT(lambda e: e.mul(out=hs[par][:ts, :], in_=h[:ts, i, :], mul=s_[:ts, 2:3]),
                    r=[("h", i), ("st", par, 2)], w=[("hs", par)])
            for c in range(8):
                PE(lambda e, c=c: e.transpose(psb[bank][:, c * 128:c * 128 + ts], hs[par][:ts, c * 128:(c + 1) * 128],
                                              ident[:ts, :ts]),
                   r=[("hs", par), "cst"], w=[("ps", bank)])
            src = psb[bank][:, :].rearrange("p (c t) -> p c t", c=8)[:, :, :ts]
            gains = pv[:, l, gcol:gcol + 8].unsqueeze(2).broadcast_to([128, 8, ts])
            DVE(lambda e: e.tensor_tensor(out=dst_ap, in0=src, in1=gains, op=ALU.mult),
                r=[("ps", bank), "pv"], w=dst_keys)

        def rope_apply(src3, nh, i, ts, fa, fb, out_ap, rkeys, wkeys, tagk):
            a3 = Ft[fa][:ts, 0:nh * 64].rearrange("p (a d) -> p a d", a=nh)
            b3_ = Ft[fb][:ts, 0:nh * 64].rearrange("p (a d) -> p a d", a=nh)
            cb = cosv(i, ts).unsqueeze(1).broadcast_to([ts, nh, 64])
            s_lo = sinv(i, ts)[:, 0:32].unsqueeze(1).broadcast_to([ts, nh, 32])
            s_hi = sinv(i, ts)[:, 32:64].unsqueeze(1).broadcast_to([ts, nh, 32])
            DVE(lambda e: e.tensor_tensor(out=a3, in0=src3, in1=cb, op=ALU.mult),
                r=rkeys + ["rope"], w=[("F", fa)])
            DVE(lambda e: e.tensor_tensor(out=b3_[:, :, 0:32], in0=src3[:, :, 32:64], in1=s_lo, op=ALU.mult),
                r=rkeys + ["rope"], w=[("F", fb)])
            DVE(lambda e: e.tensor_tensor(out=b3_[:, :, 32:64], in0=src3[:, :, 0:32], in1=s_hi, op=ALU.mult),
                r=rkeys + ["rope"], w=[("F", fb)])
            DVE(lambda e: e.tensor_tensor(out=out_ap, in0=Ft[fa][:ts, 0:nh * 64], in1=Ft[fb][:ts, 0:nh * 64], op=ALU.add),
                r=[("F", fa), ("F", fb), ("F", fb)], w=wkeys)

        def mla_prep(l, P, g, i, stage="all", pk=0):
            ts = tsz(i)
            j = i - 4 * g
            par = i % 2
            s_ = st[par]
            Bk = ["B"]
            X0, X1, X2, X3 = (4 * par + k for k in range(4))
            Fa, Fb, Fc, Fd = (4 * par + k for k in range(4))
            Fk, Fk2, Fk3 = 8 + 3 * par, 9 + 3 * par, 10 + 3 * par
            jk = junk[par]
            jkey = ("junk", par)
            WB = X0 if pk == 0 else X1
            if stage in ("all", "win"):
                for c in range(8):
                    PE(lambda e, c=c: e.matmul(ps[WB][:ts, 0:448], XT[:, c, i * 128:i * 128 + ts], WA[:, c, 0:448],
                                               start=(c == 0), stop=(c == 7)),
                       r=[("XT", c, i), ("WA", 0)], w=[("ps", WB)])
            if stage == "win":
                return
            ACT(lambda e: e.activation(out=jk[:ts, 0:256], in_=ps[WB][:ts, 0:256], func=AF.Square,
                                       accum_out=s_[:ts, 3:4]), r=[("ps", WB)], w=[jkey, ("st", par, 3)])
            ACT(lambda e: e.activation(out=jk[:ts, 0:128], in_=ps[WB][:ts, 256:384], func=AF.Square,
                                       accum_out=s_[:ts, 4:5]), r=[("ps", WB)], w=[jkey, ("st", par, 4)])
            ACT(lambda e: e.activation(out=s_[:ts, 5:6], in_=s_[:ts, 3:4], func=AF.Ln, bias=RS[:ts, i, 1:2], scale=1.0 / 256),
                r=[("st", par, 3), ("RS", i)], w=[("st", par, 5)])
            ACT(lambda e: e.activation(out=s_[:ts, 6:7], in_=s_[:ts, 4:5], func=AF.Ln, bias=RS[:ts, i, 1:2], scale=1.0 / 128),
                r=[("st", par, 4), ("RS", i)], w=[("st", par, 6)])
            ACT(lambda e: e.activation(out=s_[:ts, 7:9], in_=s_[:ts, 5:7], func=AF.Exp, scale=-0.5),
                r=[("st", par, 5), ("st", par, 6)], w=[("st", par, 7)])
            ACT(lambda e: e.mul(out=latn[par][:ts, 0:256], in_=ps[WB][:ts, 0:256], mul=s_[:ts, 7:8]),
                r=[("ps", WB), ("st", par, 7)], w=[("latn", par, 0)])
            ACT(lambda e: e.mul(out=latn[par][:ts, 256:384], in_=ps[WB][:ts, 256:384], mul=s_[:ts, 8:9]),
                r=[("ps", WB), ("st", par, 7)], w=[("latn", par, 1)])
            ACT(lambda e: e.mul(out=kr_sb[par][:ts, :], in_=ps[WB][:ts, 384:448], mul=RS[:ts, i, 0:1]),
                r=[("ps", WB), ("RS", i)], w=[("kr", par)])
            for cb in range(3):
                PE(lambda e, cb=cb: e.transpose(psb[X2][:, cb * 128:cb * 128 + ts], latn[par][:ts, cb * 128:(cb + 1) * 128],
                                                ident[:ts, :ts]),
                   r=[("latn", par, 0), ("latn", par, 1), "cst"], w=[("ps", X2)])
            DVE(lambda e: e.tensor_tensor(out=latT[par][:, :, :ts],
                                          in0=psb[X2][:, 0:384].rearrange("p (c t) -> p c t", c=3)[:, :, :ts],
                                          in1=pv[:, l, 16:19].unsqueeze(2).broadcast_to([128, 3, ts]), op=ALU.mult),
                r=[("ps", X2), "pv"], w=[("latT", par)])
            Wq = WS[:, 0:1536].rearrange("p (c n) -> p c n", c=2)
            Wkv = WS[:, 1536:2560]
            for c in range(2):
                PE(lambda e, c=c: e.matmul(ps[X2][:ts, 0:256], latT[par][:, c, :ts], Wq[:, c, 256 * P:256 * P + 256],
                                           start=(c == 0), stop=(c == 1)),
                   r=[("latT", par), "WS"], w=[("ps", X2)])
            for c in range(2):
                PE(lambda e, c=c: e.matmul(ps[X2][:ts, 256:384], latT[par][:, c, :ts],
                                           Wq[:, c, 512 + 128 * P:512 + 128 * P + 128],
                                           start=(c == 0), stop=(c == 1)),
                   r=[("latT", par), "WS"], w=[("ps", X2)])
            PE(lambda e: e.matmul(ps[X3][:ts, 0:256], latT[par][:, 2, :ts], Wkv[:, 256 * P:256 * P + 256],
                                  start=True, stop=True), r=[("latT", par), "WS"], w=[("ps", X3)])
            PE(lambda e: e.matmul(ps[X3][:ts, 256:512], latT[par][:, 2, :ts], Wkv[:, 512 + 256 * P:512 + 256 * P + 256],
                                  start=True, stop=True), r=[("latT", par), "WS"], w=[("ps", X3)])
            def qpath():
                ACT(lambda e: e.activation(out=Ft[Fa][:ts, 0:384], in_=ps[X2][:ts, 0:384], func=AF.Square),
                    r=[("ps", X2)], w=[("F", Fa)])
                DVE(lambda e: e.reduce_sum(out=s_[:ts, 9:11], in_=Ft[Fa][:ts, 0:256].rearrange("p (a d) -> p a d", a=2),
                                           axis=AX.X), r=[("F", Fa)], w=[("st", par, 9)])
                DVE(lambda e: e.reduce_sum(out=s_[:ts, 11:13], in_=Ft[Fa][:ts, 256:384].rearrange("p (a d) -> p a d", a=2),
                                           axis=AX.X), r=[("F", Fa)], w=[("st", par, 11)])
                DVE(lambda e: e.tensor_tensor(out=s_[:ts, 13:15], in0=s_[:ts, 9:11], in1=s_[:ts, 11:13], op=ALU.add),
                    r=[("st", par, 9), ("st", par, 11)], w=[("st", par, 13)])
                ACT(lambda e: e.activation(out=s_[:ts, 15:17], in_=s_[:ts, 13:15], func=AF.Ln, bias=EPS, scale=1.0 / 192),
                    r=[("st", par, 13)], w=[("st", par, 15)])
                ACT(lambda e: e.activation(out=s_[:ts, 17:19], in_=s_[:ts, 15:17], func=AF.Exp, scale=-0.5),
                    r=[("st", par, 15)], w=[("st", par, 17)])
                for hh in range(2):
                    DVE(lambda e, hh=hh: e.scalar_tensor_tensor(
                        out=qb[par][:ts, hh * 128:(hh + 1) * 128], in0=ps[X2][:ts, hh * 128:(hh + 1) * 128],
                        scalar=s_[:ts, 17 + hh:18 + hh], in1=pv[:ts, l, 20:148], op0=ALU.mult, op1=ALU.mult),
                        r=[("ps", X2), ("st", par, 17), "pv"], w=[("qb", par, hh)])
                    DVE(lambda e, hh=hh: e.scalar_tensor_tensor(
                        out=Ft[Fb][:ts, hh * 64:(hh + 1) * 64], in0=ps[X2][:ts, 256 + hh * 64:256 + (hh + 1) * 64],
                        scalar=s_[:ts, 17 + hh:18 + hh], in1=pv[:ts, l, 148:212], op0=ALU.mult, op1=ALU.mult),
                        r=[("ps", X2), ("st", par, 17), "pv"], w=[("F", Fb)])
                rope_apply(Ft[Fb][:ts, 0:128].rearrange("p (a d) -> p a d", a=2), 2, i, ts, Fc, Fd,
                           qb[par][:ts, 256:384], [("F", Fb)], [("qb", par, 2)], "q")

            def kpath():
                ACT(lambda e: e.activation(out=Ft[Fk][:ts, 0:256], in_=ps[X3][:ts, 0:256], func=AF.Square),
                    r=[("ps", X3)], w=[("F", Fk)])
                ACT(lambda e: e.activation(out=jk[:ts, 0:64], in_=kr_sb[par][:ts, :], func=AF.Square,
                                           accum_out=s_[:ts, 19:20]), r=[("kr", par)], w=[jkey, ("st", par, 19)])
                DVE(lambda e: e.reduce_sum(out=s_[:ts, 20:22], in_=Ft[Fk][:ts, 0:256].rearrange("p (a d) -> p a d", a=2),
                                           axis=AX.X), r=[("F", Fk)], w=[("st", par, 20)])
                DVE(lambda e: e.tensor_scalar_add(out=s_[:ts, 22:24], in0=s_[:ts, 20:22], scalar1=s_[:ts, 19:20]),
                    r=[("st", par, 20), ("st", par, 19)], w=[("st", par, 22)])
                ACT(lambda e: e.activation(out=s_[:ts, 24:26], in_=s_[:ts, 22:24], func=AF.Ln, bias=EPS, scale=1.0 / 192),
                    r=[("st", par, 22)], w=[("st", par, 24)])
                ACT(lambda e: e.activation(out=s_[:ts, 26:28], in_=s_[:ts, 24:26], func=AF.Exp, scale=-0.5),
                    r=[("st", par, 24)], w=[("st", par, 26)])
                for hh in range(2):
                    DVE(lambda e, hh=hh: e.scalar_tensor_tensor(
                        out=kb_[par][:ts, hh * 128:(hh + 1) * 128], in0=ps[X3][:ts, hh * 128:(hh + 1) * 128],
                        scalar=s_[:ts, 26 + hh:27 + hh], in1=pv[:ts, l, 212:340], op0=ALU.mult, op1=ALU.mult),
                        r=[("ps", X3), ("st", par, 26), "pv"], w=[("kb", par, hh)])
                DVE(lambda e: e.tensor_tensor(out=Ft[Fk][:ts, 256:320], in0=kr_sb[par][:ts, :], in1=pv[:ts, l, 340:404], op=ALU.mult),
                    r=[("kr", par), "pv"], w=[("F", Fk)])
                rope_apply(Ft[Fk][:ts, 256:320].rearrange("p (a d) -> p a d", a=1), 1, i, ts, Fk2, Fk3,
                           Ft[Fk][:ts, 320:384], [("F", Fk)], [("F", Fk)], "k")
                for hh in range(2):
                    DVE(lambda e, hh=hh: e.tensor_scalar_mul(out=kb_[par][:ts, 256 + hh * 64:256 + (hh + 1) * 64],
                                                             in0=Ft[Fk][:ts, 320:384], scalar1=s_[:ts, 26 + hh:27 + hh]),
                        r=[("F", Fk), ("st", par, 26)], w=[("kb", par, 2 + hh)])

            interleaved([qpath, kpath])
            ACT(lambda e: e.copy(out=Vv[:ts, i, :], in_=ps[X3][:ts, 256:512]), r=[("ps", X3)] + Bk, w=[("V", i)])
            for blk in range(3):
                PE(lambda e, blk=blk: e.transpose(psb[X2][:, blk * 128:blk * 128 + ts], qb[par][:ts, blk * 128:(blk + 1) * 128],
                                                  ident[:ts, :ts]),
                   r=[("qb", par, 0), ("qb", par, 1), ("qb", par, 2), "cst"], w=[("ps", X2)])
            ACT(lambda e: e.copy(out=QnT[:, :, j * 128:j * 128 + ts],
                                 in_=psb[X2][:, 0:256].rearrange("p (c t) -> p c t", c=2)[:, :, :ts]),
                r=[("ps", X2)] + Bk, w=[("Qn", j)])
            ACT(lambda e: e.copy(out=QrT[:, j * 128:j * 128 + ts], in_=psb[X2][:, 256:256 + ts]),
                r=[("ps", X2)] + Bk, w=[("Qr", j)])
            for blk in range(3):
                PE(lambda e, blk=blk: e.transpose(psb[X3][:, blk * 128:blk * 128 + ts], kb_[par][:ts, blk * 128:(blk + 1) * 128],
                                                  ident[:ts, :ts]),
                   r=[("kb", par, 0), ("kb", par, 1), ("kb", par, 2), ("kb", par, 3), "cst"], w=[("ps", X3)])
            DVE(lambda e: e.tensor_copy(out=KnT[:, :, i * 128:i * 128 + ts],
                                        in_=psb[X3][:, 0:256].rearrange("p (c t) -> p c t", c=2)[:, :, :ts]),
                r=[("ps", X3)] + Bk, w=[("Kn", i)])
            DVE(lambda e: e.tensor_copy(out=KrT[:, i * 128:i * 128 + ts], in_=psb[X3][:, 256:256 + ts]),
                r=[("ps", X3)] + Bk, w=[("Kr", i)])

        cnt_att = [0]

        def attn_head(kind, l, P, g, hh, c2=0):
            gs = gsz(g)
            nkb = 4 * g + 4 if g < 4 else NT
            cidx = cnt_att[0]
            cnt_att[0] += 1
            o_ps = 3 + (cidx % 2)
            d_ps = 5 + (cidx % 2)
            acc = 4 + (cidx % 2)
            fr = 6 + (cidx % 2)
            if kind == "mla":
                scale = 192.0 ** -0.5
                qkeys = [("Qn", jj) for jj in range(len(gtiles(g)))] + [("Qr", jj) for jj in range(len(gtiles(g)))] + ["B"]
            else:
                scale = 0.125
                qkeys = [("Qn", jj) for jj in range(len(gtiles(g)))] + ["B"]

            SB = (0, 1, 2, 7)

            def qlo(kb):
                return 128 * (kb - 4 * g) if (g < 4 and kb >= 4 * g) else 0

            def s_mm(kb):
                ks = tsz(kb)
                bank = SB[kb % 4]
                lo = qlo(kb)
                diag = kb >= 4 * g
                dw = min(128, gs - lo)
                if kind == "mla":
                    PE(lambda e: e.matmul(ps[bank][:ks, lo:gs], KnT[:, hh, kb * 128:kb * 128 + ks], QnT[:, hh, lo:gs],
                                          start=True, stop=False),
                       r=[("Kn", kb)] + qkeys, w=[("ps", bank)])
                    if diag:
                        PE(lambda e: e.matmul(ps[bank][:ks, lo:lo + dw], ident[:ks, :ks], maskb[:ks, 0:dw],
                                              start=False, stop=False), r=["cst"], w=[("ps", bank)])
                    PE(lambda e: e.matmul(ps[bank][:ks, lo:gs], KrT[64 * hh:64 * hh + 64, kb * 128:kb * 128 + ks],
                                          QrT[64 * hh:64 * hh + 64, lo:gs], start=False, stop=True),
                       r=[("Kr", kb)] + qkeys, w=[("ps", bank)])
                else:
                    PE(lambda e: e.matmul(ps[bank][:ks, lo:gs], KnT[64 * c2:64 * c2 + 64, hh, kb * 128:kb * 128 + ks],
                                          QnT[64 * c2:64 * c2 + 64, hh, lo:gs], start=True, stop=(not diag)),
                       r=[("Kn", kb)] + qkeys, w=[("ps", bank)])
                    if diag:
                        PE(lambda e: e.matmul(ps[bank][:ks, lo:lo + dw], ident[:ks, :ks], maskb[:ks, 0:dw],
                                              start=False, stop=True), r=["cst"], w=[("ps", bank)])

            def rest(kb):
                ks = tsz(kb)
                bank = SB[kb % 4]
                lo = qlo(kb)
                pt = kb % 4
                ACT(lambda e: e.activation(out=Ht[pt][:ks, lo:gs], in_=ps[bank][:ks, lo:gs], func=AF.Exp, scale=scale),
                    r=[("ps", bank)], w=[("H", pt)])
                PE(lambda e: e.matmul(ps[o_ps][:, lo:gs], Vv[:ks, kb, hh * 128:(hh + 1) * 128], Ht[pt][:ks, lo:gs],
                                      start=(kb == 0), stop=(kb == nkb - 1)),
                   r=[("V", kb), ("H", pt), "B"], w=[("ps", o_ps)])
                PE(lambda e: e.matmul(ps[d_ps][:, lo:gs], ones_b[:ks, :], Ht[pt][:ks, lo:gs],
                                      start=(kb == 0), stop=(kb == nkb - 1)),
                   r=["cst", ("H", pt)], w=[("ps", d_ps)])

            TAIL_UNITS = [[0, 1, 2, 3], [4, 5, 6, 7], [8, 9, 10, 11], [12, 13, 14, 15], [16]]

            def s_unit(u):
                bank = SB[u % 4]
                for jj, kb in enumerate(TAIL_UNITS[u]):
                    ks = tsz(kb)
                    c0 = 16 * jj
                    PE(lambda e, kb=kb, ks=ks, c0=c0: e.matmul(ps[bank][:ks, c0:c0 + 16], KnT[:, hh, kb * 128:kb * 128 + ks],
                                                               QnT[:, hh, 0:16], start=True, stop=False),
                       r=[("Kn", kb)] + qkeys, w=[("ps", bank)])
                    if kb == 16:
                        PE(lambda e, c0=c0: e.matmul(ps[bank][:16, c0:c0 + 16], ident[:16, :16], maskb[:16, 0:16],
                                                     start=False, stop=False), r=["cst"], w=[("ps", bank)])
                    PE(lambda e, kb=kb, ks=ks, c0=c0: e.matmul(ps[bank][:ks, c0:c0 + 16],
                                                               KrT[64 * hh:64 * hh + 64, kb * 128:kb * 128 + ks],
                                                               QrT[64 * hh:64 * hh + 64, 0:16], start=False, stop=True),
                       r=[("Kr", kb)] + qkeys, w=[("ps", bank)])

            def rest_unit(u):
                bank = SB[u % 4]
                pt = u % 4
                kbs = TAIL_UNITS[u]
                rows = tsz(kbs[0])
                width = 16 * len(kbs)
                ACT(lambda e: e.activation(out=Ht[pt][:rows, 0:width], in_=ps[bank][:rows, 0:width], func=AF.Exp, scale=scale),
                    r=[("ps", bank)], w=[("H", pt)])
                for jj, kb in enumerate(kbs):
                    ks = tsz(kb)
                    c0 = 16 * jj
                    PE(lambda e, kb=kb, ks=ks, c0=c0: e.matmul(ps[o_ps][:, 0:16], Vv[:ks, kb, hh * 128:(hh + 1) * 128],
                                                               Ht[pt][:ks, c0:c0 + 16], start=(kb == 0), stop=(kb == 16)),
                       r=[("V", kb), ("H", pt), "B"], w=[("ps", o_ps)])
                    PE(lambda e, ks=ks, c0=c0, kb=kb: e.matmul(ps[d_ps][:, 0:16], ones_b[:ks, :], Ht[pt][:ks, c0:c0 + 16],
                                                               start=(kb == 0), stop=(kb == 16)),
                       r=["cst", ("H", pt)], w=[("ps", d_ps)])

            LA = 3
            if g == 4:
                nun = len(TAIL_UNITS)
                for u in range(min(LA, nun)):
                    s_unit(u)
                for u in range(nun):
                    if u + LA < nun:
                        s_unit(u + LA)
                    rest_unit(u)
            else:
                for kb in range(min(LA, nkb)):
                    s_mm(kb)
                for kb in range(nkb):
                    if kb + LA < nkb:
                        s_mm(kb + LA)
                    rest(kb)
            ACT(lambda e: e.activation(out=Ft[fr][:, 0:gs], in_=ps[d_ps][:, 0:gs], func=AF.Ln), r=[("ps", d_ps)], w=[("F", fr)])
            ACT(lambda e: e.activation(out=Ft[fr][:, 0:gs], in_=Ft[fr][:, 0:gs], func=AF.Exp, scale=-1.0),
                r=[("F", fr)], w=[("F", fr)])
            if kind == "mla":
                Hh = 2 * P + hh
                DVE(lambda e: e.tensor_tensor(out=OG[g % 2][:, hh, 0:gs], in0=ps[o_ps][:, 0:gs], in1=Ft[fr][:, 0:gs],
                                              op=ALU.mult),
                    r=[("ps", o_ps), ("F", fr)], w=[("OG", g % 2, hh)])
                return
            Hd = 2 * (P - 2) + hh
            if c2 == 0:
                DVE(lambda e: e.tensor_tensor(out=Ft[2][:, 0:gs], in0=ps[o_ps][:, 0:gs], in1=Ft[fr][:, 0:gs], op=ALU.mult),
                    r=[("ps", o_ps), ("F", fr)], w=[("F", 2)])
                return
            DVE(lambda e: e.tensor_tensor(out=Ft[3][:, 0:gs], in0=ps[o_ps][:, 0:gs], in1=Ft[fr][:, 0:gs], op=ALU.mult),
                r=[("ps", o_ps), ("F", fr)], w=[("F", 3)])
            DVE(lambda e: e.scalar_tensor_tensor(out=Ft[1][:, 0:gs], in0=Ft[3][:, 0:gs], scalar=lamt[:, 1:2],
                                                 in1=Ft[2][:, 0:gs], op0=ALU.mult, op1=ALU.add),
                r=[("F", 2), ("F", 3), ("lamt", 1)], w=[("F", 1)])
            ACT(lambda e: e.activation(out=Ft[3][:, 0:gs], in_=Ft[1][:, 0:gs], func=AF.Square),
                r=[("F", 1)], w=[("F", 3)])
            PE(lambda e: e.matmul(ps[d_ps][:, 0:gs], ones_f[:, :], Ft[3][:, 0:gs], start=True, stop=True),
               r=["onesf", ("F", 3)], w=[("ps", d_ps)])
            ACT(lambda e: e.activation(out=Ft[2][:, 0:gs], in_=ps[d_ps][:, 0:gs], func=AF.Ln, bias=EPS, scale=1.0 / 128),
                r=[("ps", d_ps)], w=[("F", 2)])
            ACT(lambda e: e.activation(out=Ft[2][:, 0:gs], in_=Ft[2][:, 0:gs], func=AF.Exp, scale=-0.5),
                r=[("F", 2)], w=[("F", 2)])
            DVE(lambda e: e.scalar_tensor_tensor(out=XT[:, 4 + Hd, g * 512:g * 512 + gs], in0=Ft[1][:, 0:gs],
                                                 scalar=lamt[:, 2:3], in1=Ft[2][:, 0:gs], op0=ALU.mult, op1=ALU.mult),
                r=[("F", 1), ("F", 2), ("lamt", 2)], w=[("XT", 4 + Hd, i) for i in gtiles(g)])

        def diff_prep(l, P, g, i, stage="all", pk=0):
            ts = tsz(i)
            j = i - 4 * g
            par = i % 2
            s_ = st[par]
            Bk = ["B"]
            X0, X1, X2, X3 = (4 * par + k for k in range(4))
            Fa, Fb, Fc, Fd = (4 * par + k for k in range(4))
            WQ, WV = (X0, X1) if pk == 0 else (X2, X3)
            if stage in ("all", "win"):
                for c in range(8):
                    PE(lambda e, c=c: e.matmul(ps[WQ][:ts, 0:512], XT[:, c, i * 128:i * 128 + ts], WA[:, c, 0:512],
                                               start=(c == 0), stop=(c == 7)),
                       r=[("XT", c, i), ("WA", 0), ("WA", 1)], w=[("ps", WQ)])
                for c in range(8):
                    PE(lambda e, c=c: e.matmul(ps[WV][:ts, 0:256], XT[:, c, i * 128:i * 128 + ts], WA[:, c, 512:768],
                                               start=(c == 0), stop=(c == 7)),
                       r=[("XT", c, i), ("WA", 2)], w=[("ps", WV)])
            if stage == "win":
                return
            ACT(lambda e: e.activation(out=Ft[Fa][:ts, :], in_=ps[WQ][:ts, :], func=AF.Square),
                r=[("ps", WQ)], w=[("F", Fa)])
            DVE(lambda e: e.reduce_sum(out=s_[:ts, 30:38], in_=Ft[Fa][:ts, :].rearrange("p (a d) -> p a d", a=8), axis=AX.X),
                r=[("F", Fa)], w=[("st", par, 30)])
            ACT(lambda e: e.activation(out=s_[:ts, 38:46], in_=s_[:ts, 30:38], func=AF.Ln, bias=RS[:ts, i, 1:2], scale=1.0 / 64),
                r=[("st", par, 30), ("RS", i)], w=[("st", par, 38)])
            ACT(lambda e: e.activation(out=s_[:ts, 46:54], in_=s_[:ts, 38:46], func=AF.Exp, scale=-0.5),
                r=[("st", par, 38)], w=[("st", par, 46)])
            DVE(lambda e: e.tensor_tensor(out=Ft[Fb][:ts, :].rearrange("p (a d) -> p a d", a=8),
                                          in0=ps[WQ][:ts, :].rearrange("p (a d) -> p a d", a=8),
                                          in1=s_[:ts, 46:54].unsqueeze(2).broadcast_to([ts, 8, 64]), op=ALU.mult),
                r=[("ps", WQ), ("st", par, 46)], w=[("F", Fb)])
            DVE(lambda e: e.tensor_tensor(out=Ft[Fa][:ts, :].rearrange("p (q a d) -> p q a d", q=2, a=4),
                                          in0=Ft[Fb][:ts, :].rearrange("p (q a d) -> p q a d", q=2, a=4),
                                          in1=pv[:ts, l, 404:532].rearrange("p (q d) -> p q d", q=2).unsqueeze(2)
                                          .broadcast_to([ts, 2, 4, 64]), op=ALU.mult),
                r=[("F", Fb), "pv"], w=[("F", Fa)])
            rope_apply(Ft[Fa][:ts, :].rearrange("p (a d) -> p a d", a=8), 8, i, ts, Fc, Fd,
                       qb[par][:ts, 0:512], [("F", Fa)],
                       [("qb", par, 0), ("qb", par, 1), ("qb", par, 2)], "d")
            ACT(lambda e: e.mul(out=Vv[:ts, i, :], in_=ps[WV][:ts, 0:256], mul=RS[:ts, i, 0:1]),
                r=[("ps", WV), ("RS", i)] + Bk, w=[("V", i)])
            for blk in range(2):
                PE(lambda e, blk=blk: e.transpose(psb[WQ][:, blk * 128:blk * 128 + ts], qb[par][:ts, blk * 128:(blk + 1) * 128],
                                                  ident[:ts, :ts]),
                   r=[("qb", par, 0), ("qb", par, 1), ("qb", par, 2), "cst"], w=[("ps", WQ)])
            ACT(lambda e: e.copy(out=QnT[:, :, j * 128:j * 128 + ts],
                                 in_=psb[WQ][:, 0:256].rearrange("p (c t) -> p c t", c=2)[:, :, :ts]),
                r=[("ps", WQ)] + Bk, w=[("Qn", j)])
            for blk in range(2):
                PE(lambda e, blk=blk: e.transpose(psb[WV][:, blk * 128:blk * 128 + ts],
                                                  qb[par][:ts, (2 + blk) * 128:(3 + blk) * 128], ident[:ts, :ts]),
                   r=[("qb", par, 0), ("qb", par, 1), ("qb", par, 2), "cst"], w=[("ps", WV)])
            DVE(lambda e: e.tensor_copy(out=KnT[:, :, i * 128:i * 128 + ts],
                                        in_=psb[WV][:, 0:256].rearrange("p (c t) -> p c t", c=2)[:, :, :ts]),
                r=[("ps", WV)] + Bk, w=[("Kn", i)])

        def diff_attn(l, P, g, hh, lam_init):
            gs = gsz(g)
            nkb = 4 * g + 4 if g < 4 else NT
            Hd = 2 * (P - 2) + hh
            scale = 0.125
            qkeys = [("Qn", jj) for jj in range(len(gtiles(g)))] + ["B"]

            def qlo(kb):
                return 128 * (kb - 4 * g) if (g < 4 and kb >= 4 * g) else 0

            def s_mm(kb):
                ks = tsz(kb)
                lo = qlo(kb)
                diag = kb >= 4 * g
                for c2 in range(2):
                    bank = 2 * c2 + (kb % 2)
                    PE(lambda e, c2=c2, bank=bank: e.matmul(
                        ps[bank][:ks, lo:gs], KnT[64 * c2:64 * c2 + 64, hh, kb * 128:kb * 128 + ks],
                        QnT[64 * c2:64 * c2 + 64, hh, lo:gs], start=True, stop=(not diag)),
                        r=[("Kn", kb)] + qkeys, w=[("ps", bank)])
                    if diag:
                        dw = min(128, gs - lo)
                        PE(lambda e, bank=bank, dw=dw: e.matmul(ps[bank][:ks, lo:lo + dw], ident[:ks, :ks], maskb[:ks, 0:dw],
                                                                start=False, stop=True), r=["cst"], w=[("ps", bank)])

            def rest(kb):
                ks = tsz(kb)
                lo = qlo(kb)
                for c2 in range(2):
                    bank = 2 * c2 + (kb % 2)
                    pt = 2 * c2 + (kb % 2)
                    ACT(lambda e, bank=bank, pt=pt: e.activation(out=Ht[pt][:ks, lo:gs], in_=ps[bank][:ks, lo:gs],
                                                                 func=AF.Exp, scale=scale),
                        r=[("ps", bank)], w=[("H", pt)])
                for c2 in range(2):
                    pt = 2 * c2 + (kb % 2)
                    PE(lambda e, c2=c2, pt=pt: e.matmul(ps[4 + c2][:, lo:gs], Vv[:ks, kb, hh * 128:(hh + 1) * 128],
                                                        Ht[pt][:ks, lo:gs], start=(kb == 0), stop=(kb == nkb - 1)),
                       r=[("V", kb), ("H", pt), "B"], w=[("ps", 4 + c2)])
                    PE(lambda e, c2=c2, pt=pt: e.matmul(ps[6 + c2][:, lo:gs], ones_b[:ks, :], Ht[pt][:ks, lo:gs],
                                                        start=(kb == 0), stop=(kb == nkb - 1)),
                       r=["cst", ("H", pt)], w=[("ps", 6 + c2)])

            TAIL_UNITS = [[0, 1, 2, 3], [4, 5, 6, 7], [8, 9, 10, 11], [12, 13, 14, 15], [16]]

            def s_unit(u):
                for c2 in range(2):
                    bank = 2 * c2 + (u % 2)
                    for jj, kb in enumerate(TAIL_UNITS[u]):
                        ks = tsz(kb)
                        c0 = 16 * jj
                        PE(lambda e, c2=c2, bank=bank, kb=kb, ks=ks, c0=c0: e.matmul(
                            ps[bank][:ks, c0:c0 + 16], KnT[64 * c2:64 * c2 + 64, hh, kb * 128:kb * 128 + ks],
                            QnT[64 * c2:64 * c2 + 64, hh, 0:16], start=True, stop=(kb != 16)),
                            r=[("Kn", kb)] + qkeys, w=[("ps", bank)])
                        if kb == 16:
                            PE(lambda e, bank=bank, c0=c0: e.matmul(ps[bank][:16, c0:c0 + 16], ident[:16, :16], maskb[:16, 0:16],
                                                                    start=False, stop=True), r=["cst"], w=[("ps", bank)])

            def rest_unit(u):
                kbs = TAIL_UNITS[u]
                rows = tsz(kbs[0])
                width = 16 * len(kbs)
                for c2 in range(2):
                    bank = 2 * c2 + (u % 2)
                    pt = 2 * c2 + (u % 2)
                    ACT(lambda e, bank=bank, pt=pt: e.activation(out=Ht[pt][:rows, 0:width], in_=ps[bank][:rows, 0:width],
                                                                 func=AF.Exp, scale=scale),
                        r=[("ps", bank)], w=[("H", pt)])
                for c2 in range(2):
                    pt = 2 * c2 + (u % 2)
                    for jj, kb in enumerate(kbs):
                        ks = tsz(kb)
                        c0 = 16 * jj
                        PE(lambda e, c2=c2, pt=pt, kb=kb, ks=ks, c0=c0: e.matmul(
                            ps[4 + c2][:, 0:16], Vv[:ks, kb, hh * 128:(hh + 1) * 128], Ht[pt][:ks, c0:c0 + 16],
                            start=(kb == 0), stop=(kb == 16)),
                            r=[("V", kb), ("H", pt), "B"], w=[("ps", 4 + c2)])
                        PE(lambda e, c2=c2, pt=pt, kb=kb, ks=ks, c0=c0: e.matmul(
                            ps[6 + c2][:, 0:16], ones_b[:ks, :], Ht[pt][:ks, c0:c0 + 16],
                            start=(kb == 0), stop=(kb == 16)),
                            r=["cst", ("H", pt)], w=[("ps", 6 + c2)])

            if g == 4:
                nun = len(TAIL_UNITS)
                s_unit(0)
                for u in range(nun):
                    if u + 1 < nun:
                        s_unit(u + 1)
                    rest_unit(u)
            else:
                s_mm(0)
                for kb in range(nkb):
                    if kb + 1 < nkb:
                        s_mm(kb + 1)
                    rest(kb)
            comb = 4 + hh
            for c2 in range(2):
                ACT(lambda e, c2=c2: e.activation(out=Ft[c2][:, 0:gs], in_=ps[6 + c2][:, 0:gs], func=AF.Ln),
                    r=[("ps", 6 + c2)], w=[("F", c2)])
                DVE(lambda e, c2=c2: e.tensor_copy(out=Ft[2 + c2][:, 0:gs], in_=ps[4 + c2][:, 0:gs]),
                    r=[("ps", 4 + c2)], w=[("F", 2 + c2)])
            for c2 in range(2):
                ACT(lambda e, c2=c2: e.activation(out=Ft[c2][:, 0:gs], in_=Ft[c2][:, 0:gs], func=AF.Exp, scale=-1.0),
                    r=[("F", c2)], w=[("F", c2)])
                DVE(lambda e, c2=c2: e.tensor_tensor(out=Ft[2 + c2][:, 0:gs], in0=Ft[2 + c2][:, 0:gs], in1=Ft[c2][:, 0:gs],
                                                     op=ALU.mult),
                    r=[("F", 2 + c2), ("F", c2)], w=[("F", 2 + c2)])
            DVE(lambda e: e.scalar_tensor_tensor(out=Ft[comb][:, 0:gs], in0=Ft[3][:, 0:gs], scalar=lamt[:, 1:2],
                                                 in1=Ft[2][:, 0:gs], op0=ALU.mult, op1=ALU.add),
                r=[("F", 2), ("F", 3), ("lamt", 1)], w=[("F", comb)])

            def part2():
                sq = 6 + hh
                ACT(lambda e: e.activation(out=Ft[sq][:, 0:gs], in_=Ft[comb][:, 0:gs], func=AF.Square),
                    r=[("F", comb)], w=[("F", sq)])
                PE(lambda e: e.matmul(ps[hh][:, 0:gs], ones_f[:, :], Ft[sq][:, 0:gs], start=True, stop=True),
                   r=["onesf", ("F", sq)], w=[("ps", hh)])
                ACT(lambda e: e.activation(out=Ft[sq][:, 0:gs], in_=ps[hh][:, 0:gs], func=AF.Ln, bias=EPS, scale=1.0 / 128),
                    r=[("ps", hh)], w=[("F", sq)])
                ACT(lambda e: e.activation(out=Ft[sq][:, 0:gs], in_=Ft[sq][:, 0:gs], func=AF.Exp, scale=-0.5),
                    r=[("F", sq)], w=[("F", sq)])
                DVE(lambda e: e.scalar_tensor_tensor(out=OG[g % 2][:, hh, 0:gs], in0=Ft[comb][:, 0:gs],
                                                     scalar=lamt[:, 2:3], in1=Ft[sq][:, 0:gs], op0=ALU.mult, op1=ALU.mult),
                    r=[("F", comb), ("F", sq), ("lamt", 2)], w=[("OG", g % 2, hh)])
            return part2

        def load_WA_mla(l):
            for c in range(8):
                DMA("pool", "WA", lambda e, c=c: e.dma_start(out=WA[:, c, 0:448], in_=win_d[l, c * 128:(c + 1) * 128, 0:448]),
                    w=[("WA", 0)])

        def load_WA_diff(l, Pd):
            for part, base in enumerate((448, 960, 1472)):
                for c in range(8):
                    DMA("pool", "WA", lambda e, c=c, part=part, base=base: e.dma_start(
                        out=WA[:, c, part * 256:(part + 1) * 256],
                        in_=win_d[l, c * 128:(c + 1) * 128, base + 256 * Pd:base + 256 * Pd + 256]),
                        w=[("WA", part)])

        def load_wo_rows(l, P):
            for kc in range(2):
                for q4 in range(4):
                    DMA("pool", "WA", lambda e, kc=kc, q4=q4: e.dma_start(
                        out=WA[:, kc * 4 + q4, 768:1024],
                        in_=wo_d[l, 256 * P + 128 * kc:256 * P + 128 * kc + 128, 256 * q4:256 * q4 + 256]),
                        w=[("WA", 3)])

        def wo_partial(l, P, g):
            og = OG[g % 2]
            for i in gtiles(g):
                ts = tsz(i)
                j = i - 4 * g
                for nb in range(2):
                    bank = (2 * i + nb) % 3
                    for half in range(2):
                        q4 = 2 * nb + half
                        for kc in range(2):
                            PE(lambda e, kc=kc, q4=q4, half=half, bank=bank, ts=ts, j=j: e.matmul(
                                ps[bank][:ts, half * 256:(half + 1) * 256], og[:, kc, j * 128:j * 128 + ts],
                                WA[:, kc * 4 + q4, 768:1024], start=(kc == 0), stop=(kc == 1)),
                                r=[("OG", g % 2, 0), ("OG", g % 2, 1), ("WA", 3)], w=[("ps", bank)])
                    DVE(lambda e, nb=nb, bank=bank, i=i, ts=ts: e.tensor_tensor(
                        out=h[:ts, i, nb * 512:(nb + 1) * 512], in0=ps[bank][:ts, :], in1=h[:ts, i, nb * 512:(nb + 1) * 512],
                        op=ALU.add), r=[("ps", bank), ("h", i)], w=[("h", i)])

        def load_WS(l):
            for c in range(2):
                DMA("pool", "WS", lambda e, c=c: e.dma_start(out=WS[:, c * 768:(c + 1) * 768],
                                                             in_=wq_d[l, c * 128:(c + 1) * 128, :]),
                    w=["WS"])
            DMA("pool", "WS", lambda e: e.dma_start(out=WS[:, 1536:2560], in_=wkv_d[l, :, :]), w=["WS"])

        def load_ffn(l, sl, slot):
            Wg, Wu, Wd = ffn_views(slot)
            for c in range(8):
                DMA("pool", ("ffn", slot), lambda e, c=c: e.dma_start(
                    out=Wg[:, c, :], in_=wgu_d[l, c * 128:(c + 1) * 128, 256 * sl:256 * sl + 256]),
                    r=["B"], w=[("fw", slot)])
                DMA("pool", ("ffn", slot), lambda e, c=c: e.dma_start(
                    out=Wu[:, c, :], in_=wgu_d[l, c * 128:(c + 1) * 128, DFF + 256 * sl:DFF + 256 * sl + 256]),
                    r=["B"], w=[("fw", slot)])
            for jc in range(2):
                DMA("pool", ("ffn", slot), lambda e, jc=jc: e.dma_start(
                    out=Wd[:, jc, :], in_=wdn_d[l, 256 * sl + 128 * jc:256 * sl + 128 * jc + 128, :]),
                    r=["B"], w=[("fw", slot)])


        def interleaved(fns):
            main = S.ops
            chains = []
            for fn in fns:
                S.ops = []
                fn()
                chains.append(S.ops)
            S.ops = main
            for k in range(max(len(c) for c in chains)):
                for c in chains:
                    if k < len(c):
                        main.append(c[k])

        def prep_group(fn, l, P, g):
            tl = gtiles(g)
            pairs = [tl[a:a + 2] for a in range(0, len(tl), 2)]
            for pk, pair in enumerate(pairs):
                interleaved([(lambda i=i, pk=pk: fn(l, P, g, i, "win", pk)) for i in pair])
            for pk, pair in enumerate(pairs):
                interleaved([(lambda i=i, pk=pk: fn(l, P, g, i, "rest", pk)) for i in pair])

        for s in range(n_seq):
            DMA("pool", ("x", 0), lambda e: e.dma_start(out=h[0:16, 0, :], in_=meta_d), w=[("h", 0)])
            DMA("pool", ("x", 0), lambda e, s=s: e.dma_start(out=h[16:128, 0, :], in_=x_d[s, 0:112, :]), w=[("h", 0)])
            for i in range(1, 16):
                DMA("pool", ("x", i), lambda e, s=s, i=i: e.dma_start(out=h[:, i, :], in_=x_d[s, 128 * i - 16:128 * i + 112, :]),
                    w=[("h", i)])
            DMA("pool", ("x", 16), lambda e, s=s: e.dma_start(out=h[0:16, 16, :], in_=x_d[s, 2032:2048, :]), w=[("h", 16)])

            for l in layers:
                lam_init = 0.8 - 0.6 * math.exp(-0.3 * l)
                DVE(lambda e, l=l: e.tensor_tensor(out=Ft[0][:, 0:64], in0=pv[:, l, 532:596], in1=pv[:, l, 596:660], op=ALU.mult),
                    r=["pv"], w=[("F", 0)])
                DVE(lambda e, l=l: e.tensor_tensor(out=Ft[0][:, 256:320], in0=pv[:, l, 660:724], in1=pv[:, l, 724:788], op=ALU.mult),
                    r=["pv"], w=[("F", 0)])
                DVE(lambda e: e.reduce_sum(out=lamt[:, 4:5], in_=Ft[0][:, 0:64], axis=AX.X), r=[("F", 0)], w=[("lamt", 4)])
                DVE(lambda e: e.reduce_sum(out=lamt[:, 5:6], in_=Ft[0][:, 256:320], axis=AX.X), r=[("F", 0)], w=[("lamt", 5)])
                ACT(lambda e: e.activation(out=lamt[:, 6:8], in_=lamt[:, 4:6], func=AF.Exp),
                    r=[("lamt", 4), ("lamt", 5)], w=[("lamt", 6)])
                DVE(lambda e: e.tensor_tensor(out=lamt[:, 0:1], in0=lamt[:, 7:8], in1=lamt[:, 6:7], op=ALU.subtract),
                    r=[("lamt", 6)], w=[("lamt", 0)])
                DVE(lambda e, li=lam_init: e.tensor_scalar_add(out=lamt[:, 1:2], in0=lamt[:, 0:1], scalar1=-li),
                    r=[("lamt", 0)], w=[("lamt", 1)])
                DVE(lambda e, li=lam_init, l=l: e.tensor_scalar_mul(out=lamt[:, 2:3], in0=pv[:, l, 19:20], scalar1=1.0 - li),
                    r=["pv"], w=[("lamt", 2)])

                fenceB()
                load_WS(l)
                load_WA_mla(l)
                load_wo_rows(l, 0)

                def norm1(i):
                    ts = tsz(i)
                    norm_to_T(l, i, i % 2, 0, 4 * (i % 2), XT[:, :, i * 128:i * 128 + ts], [("XT", c, i) for c in range(8)],
                              jx=i % 2, defer=True, rs_ap=RS[:ts, i, :], rs_key=("RS", i))

                for a in range(0, NT, 2):
                    interleaved([(lambda i=i: norm1(i)) for i in range(a, min(a + 2, NT))])

                for P in range(2 if (dbg & 1) else 0):
                    if P > 0:
                        load_wo_rows(l, P)
                    pend = None
                    for g in range(dbg_groups):
                        prep_group(mla_prep, l, P, g)
                        if pend is not None:
                            wo_partial(l, P, pend)
                        for hh in range(2 if dbg_attn else 0):
                            attn_head("mla", l, P, g, hh)
                        pend = g if dbg_attn else None
                    if pend is not None:
                        wo_partial(l, P, pend)
                for P in range(2, 4 if (dbg & 2) else 2):
                    load_WA_diff(l, P - 2)
                    load_wo_rows(l, P)
                    pend = None
                    for g in range(dbg_groups):
                        prep_group(diff_prep, l, P, g)
                        if pend is not None:
                            wo_partial(l, P, pend)
                        tails = [diff_attn(l, P, g, hh, lam_init) for hh in range(2 if dbg_attn else 0)]
                        if tails:
                            interleaved(tails)
                        pend = g if dbg_attn else None
                    if pend is not None:
                        wo_partial(l, P, pend)
                fenceB()
                load_ffn(l, 0, 0)
                load_ffn(l, 1, 1)

                def norm2(i):
                    ts = tsz(i)
                    norm_to_T(l, i, i % 2, 8, 4 * (i % 2), XT[:, :, i * 128:i * 128 + ts], [("XT", c, i) for c in range(8)],
                              jx=i % 2)

                for a in range(0, NT if (dbg & 4) else 0, 2):
                    interleaved([(lambda i=i: norm2(i)) for i in range(a, min(a + 2, NT))])
                NSL = DFF // 256
                steps = [(sl, g) for sl in range(NSL if (dbg & 8) else 0) for g in range(NG)]
                cnt_gu = [0]

                def ffn_gu(sl, g):
                    slot = sl % 2
                    Wg, Wu, Wd = ffn_views(slot)
                    gs = gsz(g)
                    xkeys = [("XT", c, i) for c in range(8) for i in gtiles(g)]
                    aset = (sl * NG + g) % 2
                    for jc in range(2):
                        gb = 2 * (cnt_gu[0] % 2)
                        ub = gb + 1
                        fa = cnt_gu[0] % 4
                        cnt_gu[0] += 1
                        at = Ht[aset * 2 + jc]
                        for c in range(8):
                            PE(lambda e, c=c, jc=jc, gb=gb: e.matmul(
                                ps[gb][:, 0:gs], Wg[:, c, jc * 128:(jc + 1) * 128], XT[:, c, g * 512:g * 512 + gs],
                                start=(c == 0), stop=(c == 7)), r=xkeys + [("fw", slot), "B"], w=[("ps", gb)])
                        for c in range(8):
                            PE(lambda e, c=c, jc=jc, ub=ub: e.matmul(
                                ps[ub][:, 0:gs], Wu[:, c, jc * 128:(jc + 1) * 128], XT[:, c, g * 512:g * 512 + gs],
                                start=(c == 0), stop=(c == 7)), r=xkeys + [("fw", slot), "B"], w=[("ps", ub)])
                        ACT(lambda e, gb=gb, fa=fa: e.activation(out=Ft[fa][:, 0:gs], in_=ps[gb][:, 0:gs], func=AF.Silu),
                            r=[("ps", gb)], w=[("F", fa)])
                        DVE(lambda e, ub=ub, fa=fa, at=at: e.tensor_tensor(out=at[:, 0:gs], in0=ps[ub][:, 0:gs],
                                                                           in1=Ft[fa][:, 0:gs], op=ALU.mult),
                            r=[("ps", ub), ("F", fa)], w=[("H", aset * 2 + jc)])

                def ffn_down(sl, g):
                    slot = sl % 2
                    Wg, Wu, Wd = ffn_views(slot)
                    aset = (sl * NG + g) % 2
                    for i in gtiles(g):
                        ts = tsz(i)
                        j = i - 4 * g
                        for nb in range(2):
                            bank = 4 + nb + 2 * (i % 2)
                            for jc in range(2):
                                at = Ht[aset * 2 + jc]
                                PE(lambda e, jc=jc, nb=nb, bank=bank, ts=ts, j=j, at=at: e.matmul(
                                    ps[bank][:ts, :], at[:, j * 128:j * 128 + ts], Wd[:, jc, nb * 512:(nb + 1) * 512],
                                    start=(jc == 0), stop=(jc == 1)),
                                    r=[("H", aset * 2 + jc), ("fw", slot), "B"], w=[("ps", bank)])
                            DVE(lambda e, nb=nb, bank=bank, i=i, ts=ts: e.tensor_tensor(
                                out=h[:ts, i, nb * 512:(nb + 1) * 512], in0=ps[bank][:ts, :],
                                in1=h[:ts, i, nb * 512:(nb + 1) * 512], op=ALU.add),
                                r=[("ps", bank), ("h", i)], w=[("h", i)])

                if steps:
                    ffn_gu(*steps[0])
                for k, (sl, g) in enumerate(steps):
                    if k + 1 < len(steps):
                        ffn_gu(*steps[k + 1])
                    ffn_down(sl, g)
                    if g == NG - 1 and sl + 2 < NSL:
                        load_ffn(l, sl + 2, sl % 2)

            DMA("sp", ("out", 0), lambda e, s=s: e.dma_start(out=out_d[s, 0:112, :], in_=h[16:128, 0, :]), r=[("h", 0)])
            for i in range(1, 16):
                DMA("sp", ("out", i), lambda e, s=s, i=i: e.dma_start(out=out_d[s, 128 * i - 16:128 * i + 112, :], in_=h[:, i, :]),
                    r=[("h", i)])
            DMA("sp", ("out", 16), lambda e, s=s: e.dma_start(out=out_d[s, 2032:2048, :], in_=h[0:16, 16, :]), r=[("h", 16)])

        sem_keys = S.resolve()
        sems = {k: es.enter_context(nc.semaphore("s_" + str(i))) for i, k in enumerate(sem_keys)}
        streams = {}
        for op in S.ops:
            streams.setdefault(op.eng, []).append(op)

        def runner(name):
            def f(e):
                for op in streams.get(name, []):
                    for k, v in op.waits:
                        e.wait_ge(sems[k], v)
                    ins = op.fn(e)
                    if op.dma is not None:
                        ins.then_inc(sems[("dma", op.dma)], 16)
                    elif op.needs_inc:
                        ins.then_inc(sems[("eng", op.eng)], 1)
                if name == "sp":
                    for grp, n in S.dma_cnt.items():
                        if isinstance(grp, tuple) and grp[0] == "out":
                            e.wait_ge(sems[("dma", grp)], 16 * n)
            return f

        with nc.Block() as block:
            block.tensor(runner("pe"))
            block.scalar(runner("act"))
            block.vector(runner("dve"))
            block.gpsimd(runner("pool"))
            block.sync(runner("sp"))
    return nc


def _host_consts():
    cst = np.zeros((128, 384), np.float32)
    cst[:, 0:128] = np.eye(128, dtype=np.float32)
    k = np.arange(128)[:, None]
    q = np.arange(128)[None, :]
    cst[:, 128:256] = np.where(k <= q, 0.0, NEG).astype(np.float32)
    cst[:, 256:384] = 1.0
    pos = (np.arange(NT)[None, :] * 128 + np.arange(128)[:, None]).astype(np.float32)
    inv = (1.0 / (np.float32(10000.0) ** (np.arange(0, 64, 2, dtype=np.float32) / np.float32(64)))).astype(np.float32)
    ang = pos[:, :, None] * inv[None, None, :]
    emb = np.concatenate([ang, ang], axis=-1).astype(np.float32)
    cos = np.cos(emb).astype(np.float32)
    sin = np.sin(emb).astype(np.float32)
    sinr = sin.copy()
    sinr[:, :, 0:32] = -sin[:, :, 0:32]
    rope = np.concatenate([cos.reshape(128, NT * 64), sinr.reshape(128, NT * 64)], axis=1).astype(np.float32)
    return cst, np.ascontiguousarray(rope)


def _pack_pv(inp):
    pv = np.zeros((DEPTH, 128, NPV), np.float32)
    bc = lambda v: np.broadcast_to(np.asarray(v, np.float32)[None, :], (128, len(v)))
    for l in range(DEPTH):
        pv[l, :, 0:8] = np.asarray(inp["attn_norm"][l]).reshape(8, 128).T
        pv[l, :, 8:16] = np.asarray(inp["ffn_norm"][l]).reshape(8, 128).T
        pv[l, :, 16:18] = np.asarray(inp["mla_q_a_norm"][l]).reshape(2, 128).T
        pv[l, :, 18:19] = np.asarray(inp["mla_kv_a_norm"][l]).reshape(1, 128).T
        pv[l, :, 19:20] = np.asarray(inp["diff_subln"][l]).reshape(1, 128).T
        pv[l, :, 20:212] = bc(inp["mla_q_norm"][l])
        pv[l, :, 212:404] = bc(inp["mla_k_norm"][l])
        pv[l, :, 404:468] = bc(inp["diff_q_norm"][l])
        pv[l, :, 468:532] = bc(inp["diff_k_norm"][l])
        pv[l, :, 532:596] = bc(inp["lambda_q1"][l])
        pv[l, :, 596:660] = bc(inp["lambda_k1"][l])
        pv[l, :, 660:724] = bc(inp["lambda_q2"][l])
        pv[l, :, 724:788] = bc(inp["lambda_k2"][l])
    return pv


def _prep_shared(inp):
    cst, rope = _host_consts()
    wq = np.asarray(inp["w_q_up"], np.float32).reshape(DEPTH, 256, 4, 192)
    wq_p = np.concatenate([wq[..., :128].reshape(DEPTH, 256, 512), wq[..., 128:].reshape(DEPTH, 256, 256)], axis=-1)
    wkv = np.asarray(inp["w_kv_up"], np.float32).reshape(DEPTH, 128, 4, 256)
    wkv_p = np.concatenate([wkv[..., :128].reshape(DEPTH, 128, 512), wkv[..., 128:].reshape(DEPTH, 128, 512)], axis=-1)
    return {
        "meta": np.ascontiguousarray(np.asarray(inp["meta_tokens"], np.float32)),
        "pv": _pack_pv(inp),
        "cst": cst,
        "rope": rope,
        "w_in": np.ascontiguousarray(np.asarray(inp["w_in"], np.float32)),
        "w_qup": np.ascontiguousarray(wq_p),
        "w_kvup": np.ascontiguousarray(wkv_p),
        "w_o": np.ascontiguousarray(np.asarray(inp["w_o"], np.float32)),
        "w_gu": np.ascontiguousarray(np.asarray(inp["w_gate_up"], np.float32)),
        "w_dn": np.ascontiguousarray(np.asarray(inp["w_down"], np.float32)),
    }


_NC_CACHE = {}


def kernel(**inputs):
    x = np.asarray(inputs["x"], np.float32)
    shared = _prep_shared(inputs)
    key = (SEQ_PER_CORE, (0, 1))
    if key not in _NC_CACHE:
        _NC_CACHE[key] = build(SEQ_PER_CORE, (0, 1))
    nc = _NC_CACHE[key]
    in_maps = []
    for c in range(N_CORES):
        m = dict(shared)
        m["x"] = np.ascontiguousarray(x[c * SEQ_PER_CORE:(c + 1) * SEQ_PER_CORE])
        in_maps.append(m)
    res = run_bass_kernel_spmd(nc, in_maps, core_ids=list(range(N_CORES)))
    out = np.concatenate([np.asarray(r["out"], np.float32) for r in res.results], axis=0)
    return out
```

```python
import math
import os
import numpy as np
import concourse.bass as bass
import concourse.mybir as mybir
from concourse.bass_utils import run_bass_kernel_spmd

F32 = mybir.dt.float32
BF16 = mybir.dt.bfloat16
AF = mybir.ActivationFunctionType
ALU = mybir.AluOpType
AX = mybir.AxisListType

D = 1024
SEQ = 2048
NMETA = 16
LTOK = SEQ + NMETA
NT = 17
NG = 5
DFF = 2816
DEPTH = 2
EPS = 1e-6
NPV = 788
NEG = -30000.0
KCUT = int(os.environ.get("KCUT", "99"))
N_CORES = 8
SEQ_PER_CORE = 2


def tsz(i):
    return 128 if i < 16 else 16


def gsz(g):
    return 512 if g < 4 else 16


def gtiles(g):
    return list(range(4 * g, min(4 * g + 4, NT)))


class Op:
    __slots__ = ("eng", "fn", "r", "w", "dma", "deps", "needs_inc", "inc_val", "waits")

    def __init__(self, eng, fn, r, w, dma):
        self.eng = eng
        self.fn = fn
        self.r = r
        self.w = w
        self.dma = dma
        self.deps = ()
        self.needs_inc = False
        self.inc_val = 0
        self.waits = ()


class Sched:
    def __init__(self):
        self.ops = []

    def add(self, eng, fn, r=(), w=(), dma=None):
        w = tuple(w) + tuple(k for k in r if isinstance(k, tuple) and k[0] == "ps" and k not in w)
        self.ops.append(Op(eng, fn, tuple(r), w, dma))

    def resolve(self):
        ops = self.ops
        last_w = {}
        readers = {}
        for i, op in enumerate(ops):
            deps = set()
            for k in op.r:
                j = last_w.get(k)
                if j is not None:
                    deps.add(j)
            for k in op.w:
                j = last_w.get(k)
                if j is not None:
                    deps.add(j)
                rd = readers.get(k)
                if rd:
                    deps.update(rd.values())
            deps.discard(i)
            op.deps = deps
            src = ("dma", op.dma) if op.dma is not None else op.eng
            for k in op.r:
                readers.setdefault(k, {})[src] = i
            for k in op.w:
                last_w[k] = i
                readers[k] = {}

        def skip(pj, op):
            if pj.dma is None and op.dma is None:
                return pj.eng == "pe" and op.eng == "pe"
            return pj.dma is not None and op.dma is not None and pj.dma == op.dma

        for op in ops:
            for j in op.deps:
                pj = ops[j]
                if pj.dma is None and not skip(pj, op):
                    pj.needs_inc = True
        cnt = {}
        for op in ops:
            if op.dma is None and op.needs_inc:
                cnt[op.eng] = cnt.get(op.eng, 0) + 1
                op.inc_val = cnt[op.eng]
        dma_cnt = {}
        waited = {}
        for op in ops:
            waits = {}
            for j in op.deps:
                pj = ops[j]
                if skip(pj, op):
                    continue
                if pj.dma is None:
                    key = ("eng", pj.eng)
                    val = pj.inc_val
                else:
                    key = ("dma", pj.dma)
                    val = 16 * dma_cnt[pj.dma]
                if waits.get(key, 0) < val:
                    waits[key] = val
            we = waited.setdefault(op.eng, {})
            op.waits = [(k, v) for k, v in waits.items() if we.get(k, 0) < v]
            for k, v in op.waits:
                we[k] = v
            if op.dma is not None:
                dma_cnt[op.dma] = dma_cnt.get(op.dma, 0) + 1
        self.dma_cnt = dma_cnt
        keys = set(("eng", e) for e in cnt)
        keys.update(("dma", g) for g in dma_cnt)
        return sorted(keys, key=str)


def build(n_seq=SEQ_PER_CORE, layers=(0, 1), dbg=15, dbg_groups=NG, dbg_attn=True):
    nc = bass.Bass("TRN2", target_bir_lowering=False)
    x_d = nc.dram_tensor("x", [n_seq, SEQ, D], F32, kind="ExternalInput").ap()
    meta_d = nc.dram_tensor("meta", [NMETA, D], F32, kind="ExternalInput").ap()
    pv_d = nc.dram_tensor("pv", [DEPTH, 128, NPV], F32, kind="ExternalInput").ap()
    cst_d = nc.dram_tensor("cst", [128, 384], F32, kind="ExternalInput").ap()
    rope_d = nc.dram_tensor("rope", [128, 2 * NT * 64], F32, kind="ExternalInput").ap()
    win_d = nc.dram_tensor("w_in", [DEPTH, D, 1984], F32, kind="ExternalInput").ap()
    wq_d = nc.dram_tensor("w_qup", [DEPTH, 256, 768], F32, kind="ExternalInput").ap()
    wkv_d = nc.dram_tensor("w_kvup", [DEPTH, 128, 1024], F32, kind="ExternalInput").ap()
    wo_d = nc.dram_tensor("w_o", [DEPTH, D, D], F32, kind="ExternalInput").ap()
    wgu_d = nc.dram_tensor("w_gu", [DEPTH, D, 2 * DFF], F32, kind="ExternalInput").ap()
    wdn_d = nc.dram_tensor("w_dn", [DEPTH, DFF, D], F32, kind="ExternalInput").ap()
    out_d = nc.dram_tensor("out", [n_seq, SEQ, D], F32, kind="ExternalOutput").ap()

    S = Sched()

    def PE(fn, r=(), w=()):
        S.add("pe", fn, r, w)

    def ACT(fn, r=(), w=()):
        S.add("act", fn, r, w)

    def DVE(fn, r=(), w=()):
        S.add("dve", fn, r, w)

    def DMA(eng, grp, fn, r=(), w=()):
        S.add(eng, fn, r, w, dma=grp)

    import contextlib
    with contextlib.ExitStack() as es:
        def sb(name, shape, dt):
            return es.enter_context(nc.sbuf_tensor(name, shape, dt))

        h = sb("h", [128, NT, D], F32)
        XT = sb("XT", [128, 8, LTOK], BF16)
        WA = sb("WA", [128, 8, 1024], BF16)
        WS = sb("WS", [128, 2560], BF16)
        B = sb("B", [128, 12288], BF16)
        pv = sb("pv_sb", [128, DEPTH, NPV], F32)
        cst = sb("cst_bf", [128, 384], BF16)
        ones_f = sb("ones_f", [128, 128], F32)
        rope = sb("rope_sb", [128, 2 * NT * 64], F32)
        junk = [sb(f"junk{p}", [128, 1024], BF16) for p in range(2)]
        fence_t = sb("fence_t", [128, 8], F32)
        lamt = sb("lamt", [128, 8], F32)
        hs = [sb(f"hs{p}", [128, 1024], BF16) for p in range(2)]
        RS = sb("RS", [128, NT, 2], F32)
        OG = [sb(f"OG{p}", [128, 2, 512], BF16) for p in range(2)]
        latn = [sb(f"latn{p}", [128, 384], BF16) for p in range(2)]
        latT = [sb(f"latT{p}", [128, 3, 128], BF16) for p in range(2)]
        kr_sb = [sb(f"kr{p}", [128, 64], F32) for p in range(2)]
        qb = [sb(f"qb{p}", [128, 512], BF16) for p in range(2)]
        kb_ = [sb(f"kb{p}", [128, 384], BF16) for p in range(2)]
        st = [sb(f"st{p}", [128, 64], F32) for p in range(2)]
        Ft = [sb(f"F{n}", [128, 512], F32) for n in range(8)] + [sb(f"F{n}", [128, 384 if n in (8, 11) else 64], F32) for n in range(8, 14)]
        Ht = [sb(f"H{n}", [128, 512], BF16) for n in range(4)]
        ps = [es.enter_context(nc.psum_tensor(f"ps{b}", [128, 512], F32)) for b in range(8)]
        psb = [p.bitcast(BF16) for p in ps]

        ident = cst[:, 0:128]
        maskb = cst[:, 128:256]
        ones_b = cst[:, 256:384]

        def b3(off, a, b):
            return B[:, off:off + a * b].rearrange("p (a b) -> p a b", a=a)

        KnT = b3(0, 2, LTOK)
        KrT = B[:, 4128:6192]
        Vv = b3(6192, NT, 256)
        QnT = b3(10544, 2, 512)
        QrT = B[:, 11568:12080]

        def ffn_views(slot):
            base = slot * 6144
            return (b3(base, 8, 256), b3(base + 2048, 8, 256), b3(base + 4096, 2, 1024))

        DMA("sp", "c0", lambda e: e.dma_start(out=pv[:], in_=pv_d.rearrange("l p n -> p l n")), w=["pv"])
        DMA("sp", "c0", lambda e: e.dma_start(out=rope[:], in_=rope_d), w=["rope"])
        DMA("pool", "c1", lambda e: e.dma_start(out=cst[:], in_=cst_d), w=["cst"])
        DVE(lambda e: e.memset(ones_f[:], 1.0), w=["onesf"])
        DVE(lambda e: e.memset(fence_t[:], 0.0), w=["B"])

        def fenceB():
            DVE(lambda e: e.memset(fence_t[:], 0.0), w=["B"])

        cosv = lambda i, ts: rope[:ts, i * 64:(i + 1) * 64]
        sinv = lambda i, ts: rope[:ts, NT * 64 + i * 64: NT * 64 + (i + 1) * 64]

        def norm_to_T(l, i, par, gcol, bank, dst_ap, dst_keys, jx=0, defer=False, rs_ap=None, rs_key=None):
            ts = tsz(i)
            s_ = st[par]
            if defer:
                ACT(lambda e: e.copy(out=hs[par][:ts, :], in_=h[:ts, i, :]), r=[("h", i)], w=[("hs", par)])
            ACT(lambda e: e.activation(out=junk[jx][:ts, :], in_=h[:ts, i, :], func=AF.Square,
                                       accum_out=s_[:ts, 0:1]), r=[("h", i)], w=[("junk", jx), ("st", par, 0)])
            ACT(lambda e: e.activation(out=s_[:ts, 1:2], in_=s_[:ts, 0:1], func=AF.Ln, bias=EPS, scale=1.0 / D),
                r=[("st", par, 0)], w=[("st", par, 1)])
            if defer:
                ACT(lambda e: e.activation(out=rs_ap[:, 0:1], in_=s_[:ts, 1:2], func=AF.Exp, scale=-0.5),
                    r=[("st", par, 1)], w=[rs_key])
                ACT(lambda e: e.activation(out=rs_ap[:, 1:2], in_=s_[:ts, 1:2], func=AF.Exp, bias=math.log(EPS), scale=1.0),
                    r=[("st", par, 1)], w=[rs_key])
            else:
                ACT(lambda e: e.activation(out=s_[:ts, 2:3], in_=s_[:ts, 1:2], func=AF.Exp, scale=-0.5),
                    r=[("st", par, 1)], w=[("st", par, 2)])
                ACT(lambda e: e.mul(out=hs[par][:ts, :], in_=h[:ts, i, :], mul=s_[:ts, 2:3]),
                    r=[("h", i), ("st", par, 2)], w=[("hs", par)])
            for c in range(8):
                PE(lambda e, c=c: e.transpose(psb[bank][:, c * 128:c * 128 + ts], hs[par][:ts, c * 128:(c + 1) * 128],
                                              ident[:ts, :ts]),
                   r=[("hs", par), "cst"], w=[("ps", bank)])
            src = psb[bank][:, :].rearrange("p (c t) -> p c t", c=8)[:, :, :ts]
            gains = pv[:, l, gcol:gcol + 8].unsqueeze(2).broadcast_to([128, 8, ts])
            DVE(lambda e: e.tensor_tensor(out=dst_ap, in0=src, in1=gains, op=ALU.mult),
                r=[("ps", bank), "pv"], w=dst_keys)

        def rope_apply(src3, nh, i, ts, fa, fb, out_ap, rkeys, wkeys, tagk):
            a3 = Ft[fa][:ts, 0:nh * 64].rearrange("p (a d) -> p a d", a=nh)
            b3_ = Ft[fb][:ts, 0:nh * 64].rearrange("p (a d) -> p a d", a=nh)
            cb = cosv(i, ts).unsqueeze(1).broadcast_to([ts, nh, 64])
            s_lo = sinv(i, ts)[:, 0:32].unsqueeze(1).broadcast_to([ts, nh, 32])
            s_hi = sinv(i, ts)[:, 32:64].unsqueeze(1).broadcast_to([ts, nh, 32])
            DVE(lambda e: e.tensor_tensor(out=a3, in0=src3, in1=cb, op=ALU.mult),
                r=rkeys + ["rope"], w=[("F", fa)])
            DVE(lambda e: e.tensor_tensor(out=b3_[:, :, 0:32], in0=src3[:, :, 32:64], in1=s_lo, op=ALU.mult),
                r=rkeys + ["rope"], w=[("F", fb)])
            DVE(lambda e: e.tensor_tensor(out=b3_[:, :, 32:64], in0=src3[:, :, 0:32], in1=s_hi, op=ALU.mult),
                r=rkeys + ["rope"], w=[("F", fb)])
            DVE(lambda e: e.tensor_tensor(out=out_ap, in0=Ft[fa][:ts, 0:nh * 64], in1=Ft[fb][:ts, 0:nh * 64], op=ALU.add),
                r=[("F", fa), ("F", fb), ("F", fb)], w=wkeys)

        def mla_prep(l, P, g, i, stage="all", pk=0):
            ts = tsz(i)
            j = i - 4 * g
            par = i % 2
            s_ = st[par]
            Bk = ["B"]
            X0, X1, X2, X3 = (4 * par + k for k in range(4))
            Fa, Fb, Fc, Fd = (4 * par + k for k in range(4))
            Fk, Fk2, Fk3 = 8 + 3 * par, 9 + 3 * par, 10 + 3 * par
            jk = junk[par]
            jkey = ("junk", par)
            WB = X0 if pk == 0 else X1
            if stage in ("all", "win"):
                for c in range(8):
                    PE(lambda e, c=c: e.matmul(ps[WB][:ts, 0:448], XT[:, c, i * 128:i * 128 + ts], WA[:, c, 0:448],
                                               start=(c == 0), stop=(c == 7)),
                       r=[("XT", c, i), ("WA", 0)], w=[("ps", WB)])
            if stage == "win":
                return
            ACT(lambda e: e.activation(out=jk[:ts, 0:256], in_=ps[WB][:ts, 0:256], func=AF.Square,
                                       accum_out=s_[:ts, 3:4]), r=[("ps", WB)], w=[jkey, ("st", par, 3)])
            ACT(lambda e: e.activation(out=jk[:ts, 0:128], in_=ps[WB][:ts, 256:384], func=AF.Square,
                                       accum_out=s_[:ts, 4:5]), r=[("ps", WB)], w=[jkey, ("st", par, 4)])
            ACT(lambda e: e.activation(out=s_[:ts, 5:6], in_=s_[:ts, 3:4], func=AF.Ln, bias=RS[:ts, i, 1:2], scale=1.0 / 256),
                r=[("st", par, 3), ("RS", i)], w=[("st", par, 5)])
            ACT(lambda e: e.activation(out=s_[:ts, 6:7], in_=s_[:ts, 4:5], func=AF.Ln, bias=RS[:ts, i, 1:2], scale=1.0 / 128),
                r=[("st", par, 4), ("RS", i)], w=[("st", par, 6)])
            ACT(lambda e: e.activation(out=s_[:ts, 7:9], in_=s_[:ts, 5:7], func=AF.Exp, scale=-0.5),
                r=[("st", par, 5), ("st", par, 6)], w=[("st", par, 7)])
            ACT(lambda e: e.mul(out=latn[par][:ts, 0:256], in_=ps[WB][:ts, 0:256], mul=s_[:ts, 7:8]),
                r=[("ps", WB), ("st", par, 7)], w=[("latn", par, 0)])
            ACT(lambda e: e.mul(out=latn[par][:ts, 256:384], in_=ps[WB][:ts, 256:384], mul=s_[:ts, 8:9]),
                r=[("ps", WB), ("st", par, 7)], w=[("latn", par, 1)])
            ACT(lambda e: e.mul(out=kr_sb[par][:ts, :], in_=ps[WB][:ts, 384:448], mul=RS[:ts, i, 0:1]),
                r=[("ps", WB), ("RS", i)], w=[("kr", par)])
            for cb in range(3):
                PE(lambda e, cb=cb: e.transpose(psb[X2][:, cb * 128:cb * 128 + ts], latn[par][:ts, cb * 128:(cb + 1) * 128],
                                                ident[:ts, :ts]),
                   r=[("latn", par, 0), ("latn", par, 1), "cst"], w=[("ps", X2)])
            DVE(lambda e: e.tensor_tensor(out=latT[par][:, :, :ts],
                                          in0=psb[X2][:, 0:384].rearrange("p (c t) -> p c t", c=3)[:, :, :ts],
                                          in1=pv[:, l, 16:19].unsqueeze(2).broadcast_to([128, 3, ts]), op=ALU.mult),
                r=[("ps", X2), "pv"], w=[("latT", par)])
            Wq = WS[:, 0:1536].rearrange("p (c n) -> p c n", c=2)
            Wkv = WS[:, 1536:2560]
            for c in range(2):
                PE(lambda e, c=c: e.matmul(ps[X2][:ts, 0:256], latT[par][:, c, :ts], Wq[:, c, 256 * P:256 * P + 256],
                                           start=(c == 0), stop=(c == 1)),
                   r=[("latT", par), "WS"], w=[("ps", X2)])
            for c in range(2):
                PE(lambda e, c=c: e.matmul(ps[X2][:ts, 256:384], latT[par][:, c, :ts],
                                           Wq[:, c, 512 + 128 * P:512 + 128 * P + 128],
                                           start=(c == 0), stop=(c == 1)),
                   r=[("latT", par), "WS"], w=[("ps", X2)])
            PE(lambda e: e.matmul(ps[X3][:ts, 0:256], latT[par][:, 2, :ts], Wkv[:, 256 * P:256 * P + 256],
                                  start=True, stop=True), r=[("latT", par), "WS"], w=[("ps", X3)])
            PE(lambda e: e.matmul(ps[X3][:ts, 256:512], latT[par][:, 2, :ts], Wkv[:, 512 + 256 * P:512 + 256 * P + 256],
                                  start=True, stop=True), r=[("latT", par), "WS"], w=[("ps", X3)])
            def qpath():
                ACT(lambda e: e.activation(out=Ft[Fa][:ts, 0:384], in_=ps[X2][:ts, 0:384], func=AF.Square),
                    r=[("ps", X2)], w=[("F", Fa)])
                DVE(lambda e: e.reduce_sum(out=s_[:ts, 9:11], in_=Ft[Fa][:ts, 0:256].rearrange("p (a d) -> p a d", a=2),
                                           axis=AX.X), r=[("F", Fa)], w=[("st", par, 9)])
                DVE(lambda e: e.reduce_sum(out=s_[:ts, 11:13], in_=Ft[Fa][:ts, 256:384].rearrange("p (a d) -> p a d", a=2),
                                           axis=AX.X), r=[("F", Fa)], w=[("st", par, 11)])
                DVE(lambda e: e.tensor_tensor(out=s_[:ts, 13:15], in0=s_[:ts, 9:11], in1=s_[:ts, 11:13], op=ALU.add),
                    r=[("st", par, 9), ("st", par, 11)], w=[("st", par, 13)])
                ACT(lambda e: e.activation(out=s_[:ts, 15:17], in_=s_[:ts, 13:15], func=AF.Ln, bias=EPS, scale=1.0 / 192),
                    r=[("st", par, 13)], w=[("st", par, 15)])
                ACT(lambda e: e.activation(out=s_[:ts, 17:19], in_=s_[:ts, 15:17], func=AF.Exp, scale=-0.5),
                    r=[("st", par, 15)], w=[("st", par, 17)])
                for hh in range(2):
                    DVE(lambda e, hh=hh: e.scalar_tensor_tensor(
                        out=qb[par][:ts, hh * 128:(hh + 1) * 128], in0=ps[X2][:ts, hh * 128:(hh + 1) * 128],
                        scalar=s_[:ts, 17 + hh:18 + hh], in1=pv[:ts, l, 20:148], op0=ALU.mult, op1=ALU.mult),
                        r=[("ps", X2), ("st", par, 17), "pv"], w=[("qb", par, hh)])
                    DVE(lambda e, hh=hh: e.scalar_tensor_tensor(
                        out=Ft[Fb][:ts, hh * 64:(hh + 1) * 64], in0=ps[X2][:ts, 256 + hh * 64:256 + (hh + 1) * 64],
                        scalar=s_[:ts, 17 + hh:18 + hh], in1=pv[:ts, l, 148:212], op0=ALU.mult, op1=ALU.mult),
                        r=[("ps", X2), ("st", par, 17), "pv"], w=[("F", Fb)])
                rope_apply(Ft[Fb][:ts, 0:128].rearrange("p (a d) -> p a d", a=2), 2, i, ts, Fc, Fd,
                           qb[par][:ts, 256:384], [("F", Fb)], [("qb", par, 2)], "q")

            def kpath():
                ACT(lambda e: e.activation(out=Ft[Fk][:ts, 0:256], in_=ps[X3][:ts, 0:256], func=AF.Square),
                    r=[("ps", X3)], w=[("F", Fk)])
                ACT(lambda e: e.activation(out=jk[:ts, 0:64], in_=kr_sb[par][:ts, :], func=AF.Square,
                                           accum_out=s_[:ts, 19:20]), r=[("kr", par)], w=[jkey, ("st", par, 19)])
                DVE(lambda e: e.reduce_sum(out=s_[:ts, 20:22], in_=Ft[Fk][:ts, 0:256].rearrange("p (a d) -> p a d", a=2),
                                           axis=AX.X), r=[("F", Fk)], w=[("st", par, 20)])
                DVE(lambda e: e.tensor_scalar_add(out=s_[:ts, 22:24], in0=s_[:ts, 20:22], scalar1=s_[:ts, 19:20]),
                    r=[("st", par, 20), ("st", par, 19)], w=[("st", par, 22)])
                ACT(lambda e: e.activation(out=s_[:ts, 24:26], in_=s_[:ts, 22:24], func=AF.Ln, bias=EPS, scale=1.0 / 192),
                    r=[("st", par, 22)], w=[("st", par, 24)])
                ACT(lambda e: e.activation(out=s_[:ts, 26:28], in_=s_[:ts, 24:26], func=AF.Exp, scale=-0.5),
                    r=[("st", par, 24)], w=[("st", par, 26)])
                for hh in range(2):
                    DVE(lambda e, hh=hh: e.scalar_tensor_tensor(
                        out=kb_[par][:ts, hh * 128:(hh + 1) * 128], in0=ps[X3][:ts, hh * 128:(hh + 1) * 128],
                        scalar=s_[:ts, 26 + hh:27 + hh], in1=pv[:ts, l, 212:340], op0=ALU.mult, op1=ALU.mult),
                        r=[("ps", X3), ("st", par, 26), "pv"], w=[("kb", par, hh)])
                DVE(lambda e: e.tensor_tensor(out=Ft[Fk][:ts, 256:320], in0=kr_sb[par][:ts, :], in1=pv[:ts, l, 340:404], op=ALU.mult),
                    r=[("kr", par), "pv"], w=[("F", Fk)])
                rope_apply(Ft[Fk][:ts, 256:320].rearrange("p (a d) -> p a d", a=1), 1, i, ts, Fk2, Fk3,
                           Ft[Fk][:ts, 320:384], [("F", Fk)], [("F", Fk)], "k")
                for hh in range(2):
                    DVE(lambda e, hh=hh: e.tensor_scalar_mul(out=kb_[par][:ts, 256 + hh * 64:256 + (hh + 1) * 64],
                                                             in0=Ft[Fk][:ts, 320:384], scalar1=s_[:ts, 26 + hh:27 + hh]),
                        r=[("F", Fk), ("st", par, 26)], w=[("kb", par, 2 + hh)])

            interleaved([qpath, kpath])
            ACT(lambda e: e.copy(out=Vv[:ts, i, :], in_=ps[X3][:ts, 256:512]), r=[("ps", X3)] + Bk, w=[("V", i)])
            for blk in range(3):
                PE(lambda e, blk=blk: e.transpose(psb[X2][:, blk * 128:blk * 128 + ts], qb[par][:ts, blk * 128:(blk + 1) * 128],
                                                  ident[:ts, :ts]),
                   r=[("qb", par, 0), ("qb", par, 1), ("qb", par, 2), "cst"], w=[("ps", X2)])
            ACT(lambda e: e.copy(out=QnT[:, :, j * 128:j * 128 + ts],
                                 in_=psb[X2][:, 0:256].rearrange("p (c t) -> p c t", c=2)[:, :, :ts]),
                r=[("ps", X2)] + Bk, w=[("Qn", j)])
            ACT(lambda e: e.copy(out=QrT[:, j * 128:j * 128 + ts], in_=psb[X2][:, 256:256 + ts]),
                r=[("ps", X2)] + Bk, w=[("Qr", j)])
            for blk in range(3):
                PE(lambda e, blk=blk: e.transpose(psb[X3][:, blk * 128:blk * 128 + ts], kb_[par][:ts, blk * 128:(blk + 1) * 128],
                                                  ident[:ts, :ts]),
                   r=[("kb", par, 0), ("kb", par, 1), ("kb", par, 2), ("kb", par, 3), "cst"], w=[("ps", X3)])
            DVE(lambda e: e.tensor_copy(out=KnT[:, :, i * 128:i * 128 + ts],
                                        in_=psb[X3][:, 0:256].rearrange("p (c t) -> p c t", c=2)[:, :, :ts]),
                r=[("ps", X3)] + Bk, w=[("Kn", i)])
            DVE(lambda e: e.tensor_copy(out=KrT[:, i * 128:i * 128 + ts], in_=psb[X3][:, 256:256 + ts]),
                r=[("ps", X3)] + Bk, w=[("Kr", i)])

        cnt_att = [0]

        def attn_head(kind, l, P, g, hh, c2=0):
            gs = gsz(g)
            nkb = 4 * g + 4 if g < 4 else NT
            cidx = cnt_att[0]
            cnt_att[0] += 1
            o_ps = 3 + (cidx % 2)
            d_ps = 5 + (cidx % 2)
            acc = 4 + (cidx % 2)
            fr = 6 + (cidx % 2)
            if kind == "mla":
                scale = 192.0 ** -0.5
                qkeys = [("Qn", jj) for jj in range(len(gtiles(g)))] + [("Qr", jj) for jj in range(len(gtiles(g)))] + ["B"]
            else:
                scale = 0.125
                qkeys = [("Qn", jj) for jj in range(len(gtiles(g)))] + ["B"]

            SB = (0, 1, 2, 7)

            def qlo(kb):
                return 128 * (kb - 4 * g) if (g < 4 and kb >= 4 * g) else 0

            def s_mm(kb):
                ks = tsz(kb)
                bank = SB[kb % 4]
                lo = qlo(kb)
                diag = kb >= 4 * g
                dw = min(128, gs - lo)
                if kind == "mla":
                    PE(lambda e: e.matmul(ps[bank][:ks, lo:gs], KnT[:, hh, kb * 128:kb * 128 + ks], QnT[:, hh, lo:gs],
                                          start=True, stop=False),
                       r=[("Kn", kb)] + qkeys, w=[("ps", bank)])
                    if diag:
                        PE(lambda e: e.matmul(ps[bank][:ks, lo:lo + dw], ident[:ks, :ks], maskb[:ks, 0:dw],
                                              start=False, stop=False), r=["cst"], w=[("ps", bank)])
                    PE(lambda e: e.matmul(ps[bank][:ks, lo:gs], KrT[64 * hh:64 * hh + 64, kb * 128:kb * 128 + ks],
                                          QrT[64 * hh:64 * hh + 64, lo:gs], start=False, stop=True),
                       r=[("Kr", kb)] + qkeys, w=[("ps", bank)])
                else:
                    PE(lambda e: e.matmul(ps[bank][:ks, lo:gs], KnT[64 * c2:64 * c2 + 64, hh, kb * 128:kb * 128 + ks],
                                          QnT[64 * c2:64 * c2 + 64, hh, lo:gs], start=True, stop=(not diag)),
                       r=[("Kn", kb)] + qkeys, w=[("ps", bank)])
                    if diag:
                        PE(lambda e: e.matmul(ps[bank][:ks, lo:lo + dw], ident[:ks, :ks], maskb[:ks, 0:dw],
                                              start=False, stop=True), r=["cst"], w=[("ps", bank)])

            def rest(kb):
                ks = tsz(kb)
                bank = SB[kb % 4]
                lo = qlo(kb)
                pt = kb % 4
                ACT(lambda e: e.activation(out=Ht[pt][:ks, lo:gs], in_=ps[bank][:ks, lo:gs], func=AF.Exp, scale=scale),
                    r=[("ps", bank)], w=[("H", pt)])
                PE(lambda e: e.matmul(ps[o_ps][:, lo:gs], Vv[:ks, kb, hh * 128:(hh + 1) * 128], Ht[pt][:ks, lo:gs],
                                      start=(kb == 0), stop=(kb == nkb - 1)),
                   r=[("V", kb), ("H", pt), "B"], w=[("ps", o_ps)])
                PE(lambda e: e.matmul(ps[d_ps][:, lo:gs], ones_b[:ks, :], Ht[pt][:ks, lo:gs],
                                      start=(kb == 0), stop=(kb == nkb - 1)),
                   r=["cst", ("H", pt)], w=[("ps", d_ps)])

            TAIL_UNITS = [[0, 1, 2, 3], [4, 5, 6, 7], [8, 9, 10, 11], [12, 13, 14, 15], [16]]

            def s_unit(u):
                bank = SB[u % 4]
                for jj, kb in enumerate(TAIL_UNITS[u]):
                    ks = tsz(kb)
                    c0 = 16 * jj
                    PE(lambda e, kb=kb, ks=ks, c0=c0: e.matmul(ps[bank][:ks, c0:c0 + 16], KnT[:, hh, kb * 128:kb * 128 + ks],
                                                               QnT[:, hh, 0:16], start=True, stop=False),
                       r=[("Kn", kb)] + qkeys, w=[("ps", bank)])
                    if kb == 16:
                        PE(lambda e, c0=c0: e.matmul(ps[bank][:16, c0:c0 + 16], ident[:16, :16], maskb[:16, 0:16],
                                                     start=False, stop=False), r=["cst"], w=[("ps", bank)])
                    PE(lambda e, kb=kb, ks=ks, c0=c0: e.matmul(ps[bank][:ks, c0:c0 + 16],
                                                               KrT[64 * hh:64 * hh + 64, kb * 128:kb * 128 + ks],
                                                               QrT[64 * hh:64 * hh + 64, 0:16], start=False, stop=True),
                       r=[("Kr", kb)] + qkeys, w=[("ps", bank)])

            def rest_unit(u):
                bank = SB[u % 4]
                pt = u % 4
                kbs = TAIL_UNITS[u]
                rows = tsz(kbs[0])
                width = 16 * len(kbs)
                ACT(lambda e: e.activation(out=Ht[pt][:rows, 0:width], in_=ps[bank][:rows, 0:width], func=AF.Exp, scale=scale),
                    r=[("ps", bank)], w=[("H", pt)])
                for jj, kb in enumerate(kbs):
                    ks = tsz(kb)
                    c0 = 16 * jj
                    PE(lambda e, kb=kb, ks=ks, c0=c0: e.matmul(ps[o_ps][:, 0:16], Vv[:ks, kb, hh * 128:(hh + 1) * 128],
                                                               Ht[pt][:ks, c0:c0 + 16], start=(kb == 0), stop=(kb == 16)),
                       r=[("V", kb), ("H", pt), "B"], w=[("ps", o_ps)])
                    PE(lambda e, ks=ks, c0=c0, kb=kb: e.matmul(ps[d_ps][:, 0:16], ones_b[:ks, :], Ht[pt][:ks, c0:c0 + 16],
                                                               start=(kb == 0), stop=(kb == 16)),
                       r=["cst", ("H", pt)], w=[("ps", d_ps)])

            LA = 3
            if g == 4:
                nun = len(TAIL_UNITS)
                for u in range(min(LA, nun)):
                    s_unit(u)
                for u in range(nun):
                    if u + LA < nun:
                        s_unit(u + LA)
                    rest_unit(u)
            else:
                for kb in range(min(LA, nkb)):
                    s_mm(kb)
                for kb in range(nkb):
                    if kb + LA < nkb:
                        s_mm(kb + LA)
                    rest(kb)
            ACT(lambda e: e.activation(out=Ft[fr][:, 0:gs], in_=ps[d_ps][:, 0:gs], func=AF.Ln), r=[("ps", d_ps)], w=[("F", fr)])
            ACT(lambda e: e.activation(out=Ft[fr][:, 0:gs], in_=Ft[fr][:, 0:gs], func=AF.Exp, scale=-1.0),
                r=[("F", fr)], w=[("F", fr)])
            if kind == "mla":
                Hh = 2 * P + hh
                DVE(lambda e: e.tensor_tensor(out=OG[g % 2][:, hh, 0:gs], in0=ps[o_ps][:, 0:gs], in1=Ft[fr][:, 0:gs],
                                              op=ALU.mult),
                    r=[("ps", o_ps), ("F", fr)], w=[("OG", g % 2, hh)])
                return
            Hd = 2 * (P - 2) + hh
            if c2 == 0:
                DVE(lambda e: e.tensor_tensor(out=Ft[2][:, 0:gs], in0=ps[o_ps][:, 0:gs], in1=Ft[fr][:, 0:gs], op=ALU.mult),
                    r=[("ps", o_ps), ("F", fr)], w=[("F", 2)])
                return
            DVE(lambda e: e.tensor_tensor(out=Ft[3][:, 0:gs], in0=ps[o_ps][:, 0:gs], in1=Ft[fr][:, 0:gs], op=ALU.mult),
                r=[("ps", o_ps), ("F", fr)], w=[("F", 3)])
            DVE(lambda e: e.scalar_tensor_tensor(out=Ft[1][:, 0:gs], in0=Ft[3][:, 0:gs], scalar=lamt[:, 1:2],
                                                 in1=Ft[2][:, 0:gs], op0=ALU.mult, op1=ALU.add),
                r=[("F", 2), ("F", 3), ("lamt", 1)], w=[("F", 1)])
            ACT(lambda e: e.activation(out=Ft[3][:, 0:gs], in_=Ft[1][:, 0:gs], func=AF.Square),
                r=[("F", 1)], w=[("F", 3)])
            PE(lambda e: e.matmul(ps[d_ps][:, 0:gs], ones_f[:, :], Ft[3][:, 0:gs], start=True, stop=True),
               r=["onesf", ("F", 3)], w=[("ps", d_ps)])
            ACT(lambda e: e.activation(out=Ft[2][:, 0:gs], in_=ps[d_ps][:, 0:gs], func=AF.Ln, bias=EPS, scale=1.0 / 128),
                r=[("ps", d_ps)], w=[("F", 2)])
            ACT(lambda e: e.activation(out=Ft[2][:, 0:gs], in_=Ft[2][:, 0:gs], func=AF.Exp, scale=-0.5),
                r=[("F", 2)], w=[("F", 2)])
            DVE(lambda e: e.scalar_tensor_tensor(out=XT[:, 4 + Hd, g * 512:g * 512 + gs], in0=Ft[1][:, 0:gs],
                                                 scalar=lamt[:, 2:3], in1=Ft[2][:, 0:gs], op0=ALU.mult, op1=ALU.mult),
                r=[("F", 1), ("F", 2), ("lamt", 2)], w=[("XT", 4 + Hd, i) for i in gtiles(g)])

        def diff_prep(l, P, g, i, stage="all", pk=0):
            ts = tsz(i)
            j = i - 4 * g
            par = i % 2
            s_ = st[par]
            Bk = ["B"]
            X0, X1, X2, X3 = (4 * par + k for k in range(4))
            Fa, Fb, Fc, Fd = (4 * par + k for k in range(4))
            WQ, WV = (X0, X1) if pk == 0 else (X2, X3)
            if stage in ("all", "win"):
                for c in range(8):
                    PE(lambda e, c=c: e.matmul(ps[WQ][:ts, 0:512], XT[:, c, i * 128:i * 128 + ts], WA[:, c, 0:512],
                                               start=(c == 0), stop=(c == 7)),
                       r=[("XT", c, i), ("WA", 0), ("WA", 1)], w=[("ps", WQ)])
                for c in range(8):
                    PE(lambda e, c=c: e.matmul(ps[WV][:ts, 0:256], XT[:, c, i * 128:i * 128 + ts], WA[:, c, 512:768],
                                               start=(c == 0), stop=(c == 7)),
                       r=[("XT", c, i), ("WA", 2)], w=[("ps", WV)])
            if stage == "win":
                return
            ACT(lambda e: e.activation(out=Ft[Fa][:ts, :], in_=ps[WQ][:ts, :], func=AF.Square),
                r=[("ps", WQ)], w=[("F", Fa)])
            DVE(lambda e: e.reduce_sum(out=s_[:ts, 30:38], in_=Ft[Fa][:ts, :].rearrange("p (a d) -> p a d", a=8), axis=AX.X),
                r=[("F", Fa)], w=[("st", par, 30)])
            ACT(lambda e: e.activation(out=s_[:ts, 38:46], in_=s_[:ts, 30:38], func=AF.Ln, bias=RS[:ts, i, 1:2], scale=1.0 / 64),
                r=[("st", par, 30), ("RS", i)], w=[("st", par, 38)])
            ACT(lambda e: e.activation(out=s_[:ts, 46:54], in_=s_[:ts, 38:46], func=AF.Exp, scale=-0.5),
                r=[("st", par, 38)], w=[("st", par, 46)])
            DVE(lambda e: e.tensor_tensor(out=Ft[Fb][:ts, :].rearrange("p (a d) -> p a d", a=8),
                                          in0=ps[WQ][:ts, :].rearrange("p (a d) -> p a d", a=8),
                                          in1=s_[:ts, 46:54].unsqueeze(2).broadcast_to([ts, 8, 64]), op=ALU.mult),
                r=[("ps", WQ), ("st", par, 46)], w=[("F", Fb)])
            DVE(lambda e: e.tensor_tensor(out=Ft[Fa][:ts, :].rearrange("p (q a d) -> p q a d", q=2, a=4),
                                          in0=Ft[Fb][:ts, :].rearrange("p (q a d) -> p q a d", q=2, a=4),
                                          in1=pv[:ts, l, 404:532].rearrange("p (q d) -> p q d", q=2).unsqueeze(2)
                                          .broadcast_to([ts, 2, 4, 64]), op=ALU.mult),
                r=[("F", Fb), "pv"], w=[("F", Fa)])
            rope_apply(Ft[Fa][:ts, :].rearrange("p (a d) -> p a d", a=8), 8, i, ts, Fc, Fd,
                       qb[par][:ts, 0:512], [("F", Fa)],
                       [("qb", par, 0), ("qb", par, 1), ("qb", par, 2)], "d")
            ACT(lambda e: e.mul(out=Vv[:ts, i, :], in_=ps[WV][:ts, 0:256], mul=RS[:ts, i, 0:1]),
                r=[("ps", WV), ("RS", i)] + Bk, w=[("V", i)])
            for blk in range(2):
                PE(lambda e, blk=blk: e.transpose(psb[WQ][:, blk * 128:blk * 128 + ts], qb[par][:ts, blk * 128:(blk + 1) * 128],
                                                  ident[:ts, :ts]),
                   r=[("qb", par, 0), ("qb", par, 1), ("qb", par, 2), "cst"], w=[("ps", WQ)])
            ACT(lambda e: e.copy(out=QnT[:, :, j * 128:j * 128 + ts],
                                 in_=psb[WQ][:, 0:256].rearrange("p (c t) -> p c t", c=2)[:, :, :ts]),
                r=[("ps", WQ)] + Bk, w=[("Qn", j)])
            for blk in range(2):
                PE(lambda e, blk=blk: e.transpose(psb[WV][:, blk * 128:blk * 128 + ts],
                                                  qb[par][:ts, (2 + blk) * 128:(3 + blk) * 128], ident[:ts, :ts]),
                   r=[("qb", par, 0), ("qb", par, 1), ("qb", par, 2), "cst"], w=[("ps", WV)])
            DVE(lambda e: e.tensor_copy(out=KnT[:, :, i * 128:i * 128 + ts],
                                        in_=psb[WV][:, 0:256].rearrange("p (c t) -> p c t", c=2)[:, :, :ts]),
                r=[("ps", WV)] + Bk, w=[("Kn", i)])

        def diff_attn(l, P, g, hh, lam_init):
            gs = gsz(g)
            nkb = 4 * g + 4 if g < 4 else NT
            Hd = 2 * (P - 2) + hh
            scale = 0.125
            qkeys = [("Qn", jj) for jj in range(len(gtiles(g)))] + ["B"]

            def qlo(kb):
                return 128 * (kb - 4 * g) if (g < 4 and kb >= 4 * g) else 0

            def s_mm(kb):
                ks = tsz(kb)
                lo = qlo(kb)
                diag = kb >= 4 * g
                for c2 in range(2):
                    bank = 2 * c2 + (kb % 2)
                    PE(lambda e, c2=c2, bank=bank: e.matmul(
                        ps[bank][:ks, lo:gs], KnT[64 * c2:64 * c2 + 64, hh, kb * 128:kb * 128 + ks],
                        QnT[64 * c2:64 * c2 + 64, hh, lo:gs], start=True, stop=(not diag)),
                        r=[("Kn", kb)] + qkeys, w=[("ps", bank)])
                    if diag:
                        dw = min(128, gs - lo)
                        PE(lambda e, bank=bank, dw=dw: e.matmul(ps[bank][:ks, lo:lo + dw], ident[:ks, :ks], maskb[:ks, 0:dw],
                                                                start=False, stop=True), r=["cst"], w=[("ps", bank)])

            def rest(kb):
                ks = tsz(kb)
                lo = qlo(kb)
                for c2 in range(2):
                    bank = 2 * c2 + (kb % 2)
                    pt = 2 * c2 + (kb % 2)
                    ACT(lambda e, bank=bank, pt=pt: e.activation(out=Ht[pt][:ks, lo:gs], in_=ps[bank][:ks, lo:gs],
                                                                 func=AF.Exp, scale=scale),
                        r=[("ps", bank)], w=[("H", pt)])
                for c2 in range(2):
                    pt = 2 * c2 + (kb % 2)
                    PE(lambda e, c2=c2, pt=pt: e.matmul(ps[4 + c2][:, lo:gs], Vv[:ks, kb, hh * 128:(hh + 1) * 128],
                                                        Ht[pt][:ks, lo:gs], start=(kb == 0), stop=(kb == nkb - 1)),
                       r=[("V", kb), ("H", pt), "B"], w=[("ps", 4 + c2)])
                    PE(lambda e, c2=c2, pt=pt: e.matmul(ps[6 + c2][:, lo:gs], ones_b[:ks, :], Ht[pt][:ks, lo:gs],
                                                        start=(kb == 0), stop=(kb == nkb - 1)),
                       r=["cst", ("H", pt)], w=[("ps", 6 + c2)])

            TAIL_UNITS = [[0, 1, 2, 3], [4, 5, 6, 7], [8, 9, 10, 11], [12, 13, 14, 15], [16]]

            def s_unit(u):
                for c2 in range(2):
                    bank = 2 * c2 + (u % 2)
                    for jj, kb in enumerate(TAIL_UNITS[u]):
                        ks = tsz(kb)
                        c0 = 16 * jj
                        PE(lambda e, c2=c2, bank=bank, kb=kb, ks=ks, c0=c0: e.matmul(
                            ps[bank][:ks, c0:c0 + 16], KnT[64 * c2:64 * c2 + 64, hh, kb * 128:kb * 128 + ks],
                            QnT[64 * c2:64 * c2 + 64, hh, 0:16], start=True, stop=(kb != 16)),
                            r=[("Kn", kb)] + qkeys, w=[("ps", bank)])
                        if kb == 16:
                            PE(lambda e, bank=bank, c0=c0: e.matmul(ps[bank][:16, c0:c0 + 16], ident[:16, :16], maskb[:16, 0:16],
                                                                    start=False, stop=True), r=["cst"], w=[("ps", bank)])

            def rest_unit(u):
                kbs = TAIL_UNITS[u]
                rows = tsz(kbs[0])
                width = 16 * len(kbs)
                for c2 in range(2):
                    bank = 2 * c2 + (u % 2)
                    pt = 2 * c2 + (u % 2)
                    ACT(lambda e, bank=bank, pt=pt: e.activation(out=Ht[pt][:rows, 0:width], in_=ps[bank][:rows, 0:width],
                                                                 func=AF.Exp, scale=scale),
                        r=[("ps", bank)], w=[("H", pt)])
                for c2 in range(2):
                    pt = 2 * c2 + (u % 2)
                    for jj, kb in enumerate(kbs):
                        ks = tsz(kb)
                        c0 = 16 * jj
                        PE(lambda e, c2=c2, pt=pt, kb=kb, ks=ks, c0=c0: e.matmul(
                            ps[4 + c2][:, 0:16], Vv[:ks, kb, hh * 128:(hh + 1) * 128], Ht[pt][:ks, c0:c0 + 16],
                            start=(kb == 0), stop=(kb == 16)),
                            r=[("V", kb), ("H", pt), "B"], w=[("ps", 4 + c2)])
                        PE(lambda e, c2=c2, pt=pt, kb=kb, ks=ks, c0=c0: e.matmul(
                            ps[6 + c2][:, 0:16], ones_b[:ks, :], Ht[pt][:ks, c0:c0 + 16],
                            start=(kb == 0), stop=(kb == 16)),
                            r=["cst", ("H", pt)], w=[("ps", 6 + c2)])

            if g == 4:
                nun = len(TAIL_UNITS)
                s_unit(0)
                for u in range(nun):
                    if u + 1 < nun:
                        s_unit(u + 1)
                    rest_unit(u)
            else:
                s_mm(0)
                for kb in range(nkb):
                    if kb + 1 < nkb:
                        s_mm(kb + 1)
                    rest(kb)
            comb = 4 + hh
            for c2 in range(2):
                ACT(lambda e, c2=c2: e.activation(out=Ft[c2][:, 0:gs], in_=ps[6 + c2][:, 0:gs], func=AF.Ln),
                    r=[("ps", 6 + c2)], w=[("F", c2)])
                DVE(lambda e, c2=c2: e.tensor_copy(out=Ft[2 + c2][:, 0:gs], in_=ps[4 + c2][:, 0:gs]),
                    r=[("ps", 4 + c2)], w=[("F", 2 + c2)])
            for c2 in range(2):
                ACT(lambda e, c2=c2: e.activation(out=Ft[c2][:, 0:gs], in_=Ft[c2][:, 0:gs], func=AF.Exp, scale=-1.0),
                    r=[("F", c2)], w=[("F", c2)])
                DVE(lambda e, c2=c2: e.tensor_tensor(out=Ft[2 + c2][:, 0:gs], in0=Ft[2 + c2][:, 0:gs], in1=Ft[c2][:, 0:gs],
                                                     op=ALU.mult),
                    r=[("F", 2 + c2), ("F", c2)], w=[("F", 2 + c2)])
            DVE(lambda e: e.scalar_tensor_tensor(out=Ft[comb][:, 0:gs], in0=Ft[3][:, 0:gs], scalar=lamt[:, 1:2],
                                                 in1=Ft[2][:, 0:gs], op0=ALU.mult, op1=ALU.add),
                r=[("F", 2), ("F", 3), ("lamt", 1)], w=[("F", comb)])

            def part2():
                sq = 6 + hh
                ACT(lambda e: e.activation(out=Ft[sq][:, 0:gs], in_=Ft[comb][:, 0:gs], func=AF.Square),
                    r=[("F", comb)], w=[("F", sq)])
                PE(lambda e: e.matmul(ps[hh][:, 0:gs], ones_f[:, :], Ft[sq][:, 0:gs], start=True, stop=True),
                   r=["onesf", ("F", sq)], w=[("ps", hh)])
                ACT(lambda e: e.activation(out=Ft[sq][:, 0:gs], in_=ps[hh][:, 0:gs], func=AF.Ln, bias=EPS, scale=1.0 / 128),
                    r=[("ps", hh)], w=[("F", sq)])
                ACT(lambda e: e.activation(out=Ft[sq][:, 0:gs], in_=Ft[sq][:, 0:gs], func=AF.Exp, scale=-0.5),
                    r=[("F", sq)], w=[("F", sq)])
                DVE(lambda e: e.scalar_tensor_tensor(out=OG[g % 2][:, hh, 0:gs], in0=Ft[comb][:, 0:gs],
                                                     scalar=lamt[:, 2:3], in1=Ft[sq][:, 0:gs], op0=ALU.mult, op1=ALU.mult),
                    r=[("F", comb), ("F", sq), ("lamt", 2)], w=[("OG", g % 2, hh)])
            return part2

        def load_WA_mla(l):
            for c in range(8):
                DMA("pool", "WA", lambda e, c=c: e.dma_start(out=WA[:, c, 0:448], in_=win_d[l, c * 128:(c + 1) * 128, 0:448]),
                    w=[("WA", 0)])

        def load_WA_diff(l, Pd):
            for part, base in enumerate((448, 960, 1472)):
                for c in range(8):
                    DMA("pool", "WA", lambda e, c=c, part=part, base=base: e.dma_start(
                        out=WA[:, c, part * 256:(part + 1) * 256],
                        in_=win_d[l, c * 128:(c + 1) * 128, base + 256 * Pd:base + 256 * Pd + 256]),
                        w=[("WA", part)])

        def load_wo_rows(l, P):
            for kc in range(2):
                for q4 in range(4):
                    DMA("pool", "WA", lambda e, kc=kc, q4=q4: e.dma_start(
                        out=WA[:, kc * 4 + q4, 768:1024],
                        in_=wo_d[l, 256 * P + 128 * kc:256 * P + 128 * kc + 128, 256 * q4:256 * q4 + 256]),
                        w=[("WA", 3)])

        def wo_partial(l, P, g):
            og = OG[g % 2]
            for i in gtiles(g):
                ts = tsz(i)
                j = i - 4 * g
                for nb in range(2):
                    bank = (2 * i + nb) % 3
                    for half in range(2):
                        q4 = 2 * nb + half
                        for kc in range(2):
                            PE(lambda e, kc=kc, q4=q4, half=half, bank=bank, ts=ts, j=j: e.matmul(
                                ps[bank][:ts, half * 256:(half + 1) * 256], og[:, kc, j * 128:j * 128 + ts],
                                WA[:, kc * 4 + q4, 768:1024], start=(kc == 0), stop=(kc == 1)),
                                r=[("OG", g % 2, 0), ("OG", g % 2, 1), ("WA", 3)], w=[("ps", bank)])
                    DVE(lambda e, nb=nb, bank=bank, i=i, ts=ts: e.tensor_tensor(
                        out=h[:ts, i, nb * 512:(nb + 1) * 512], in0=ps[bank][:ts, :], in1=h[:ts, i, nb * 512:(nb + 1) * 512],
                        op=ALU.add), r=[("ps", bank), ("h", i)], w=[("h", i)])

        def load_WS(l):
            for c in range(2):
                DMA("pool", "WS", lambda e, c=c: e.dma_start(out=WS[:, c * 768:(c + 1) * 768],
                                                             in_=wq_d[l, c * 128:(c + 1) * 128, :]),
                    w=["WS"])
            DMA("pool", "WS", lambda e: e.dma_start(out=WS[:, 1536:2560], in_=wkv_d[l, :, :]), w=["WS"])

        def load_ffn(l, sl, slot):
            Wg, Wu, Wd = ffn_views(slot)
            for c in range(8):
                DMA("pool", ("ffn", slot), lambda e, c=c: e.dma_start(
                    out=Wg[:, c, :], in_=wgu_d[l, c * 128:(c + 1) * 128, 256 * sl:256 * sl + 256]),
                    r=["B"], w=[("fw", slot)])
                DMA("pool", ("ffn", slot), lambda e, c=c: e.dma_start(
                    out=Wu[:, c, :], in_=wgu_d[l, c * 128:(c + 1) * 128, DFF + 256 * sl:DFF + 256 * sl + 256]),
                    r=["B"], w=[("fw", slot)])
            for jc in range(2):
                DMA("pool", ("ffn", slot), lambda e, jc=jc: e.dma_start(
                    out=Wd[:, jc, :], in_=wdn_d[l, 256 * sl + 128 * jc:256 * sl + 128 * jc + 128, :]),
                    r=["B"], w=[("fw", slot)])


        def interleaved(fns):
            main = S.ops
            chains = []
            for fn in fns:
                S.ops = []
                fn()
                chains.append(S.ops)
            S.ops = main
            for k in range(max(len(c) for c in chains)):
                for c in chains:
                    if k < len(c):
                        main.append(c[k])

        def prep_group(fn, l, P, g):
            tl = gtiles(g)
            pairs = [tl[a:a + 2] for a in range(0, len(tl), 2)]
            for pk, pair in enumerate(pairs):
                interleaved([(lambda i=i, pk=pk: fn(l, P, g, i, "win", pk)) for i in pair])
            for pk, pair in enumerate(pairs):
                interleaved([(lambda i=i, pk=pk: fn(l, P, g, i, "rest", pk)) for i in pair])

        for s in range(n_seq):
            DMA("pool", ("x", 0), lambda e: e.dma_start(out=h[0:16, 0, :], in_=meta_d), w=[("h", 0)])
            DMA("pool", ("x", 0), lambda e, s=s: e.dma_start(out=h[16:128, 0, :], in_=x_d[s, 0:112, :]), w=[("h", 0)])
            for i in range(1, 16):
                DMA("pool", ("x", i), lambda e, s=s, i=i: e.dma_start(out=h[:, i, :], in_=x_d[s, 128 * i - 16:128 * i + 112, :]),
                    w=[("h", i)])
            DMA("pool", ("x", 16), lambda e, s=s: e.dma_start(out=h[0:16, 16, :], in_=x_d[s, 2032:2048, :]), w=[("h", 16)])

            for l in layers:
                lam_init = 0.8 - 0.6 * math.exp(-0.3 * l)
                DVE(lambda e, l=l: e.tensor_tensor(out=Ft[0][:, 0:64], in0=pv[:, l, 532:596], in1=pv[:, l, 596:660], op=ALU.mult),
                    r=["pv"], w=[("F", 0)])
                DVE(lambda e, l=l: e.tensor_tensor(out=Ft[0][:, 256:320], in0=pv[:, l, 660:724], in1=pv[:, l, 724:788], op=ALU.mult),
                    r=["pv"], w=[("F", 0)])
                DVE(lambda e: e.reduce_sum(out=lamt[:, 4:5], in_=Ft[0][:, 0:64], axis=AX.X), r=[("F", 0)], w=[("lamt", 4)])
                DVE(lambda e: e.reduce_sum(out=lamt[:, 5:6], in_=Ft[0][:, 256:320], axis=AX.X), r=[("F", 0)], w=[("lamt", 5)])
                ACT(lambda e: e.activation(out=lamt[:, 6:8], in_=lamt[:, 4:6], func=AF.Exp),
                    r=[("lamt", 4), ("lamt", 5)], w=[("lamt", 6)])
                DVE(lambda e: e.tensor_tensor(out=lamt[:, 0:1], in0=lamt[:, 7:8], in1=lamt[:, 6:7], op=ALU.subtract),
                    r=[("lamt", 6)], w=[("lamt", 0)])
                DVE(lambda e, li=lam_init: e.tensor_scalar_add(out=lamt[:, 1:2], in0=lamt[:, 0:1], scalar1=-li),
                    r=[("lamt", 0)], w=[("lamt", 1)])
                DVE(lambda e, li=lam_init, l=l: e.tensor_scalar_mul(out=lamt[:, 2:3], in0=pv[:, l, 19:20], scalar1=1.0 - li),
                    r=["pv"], w=[("lamt", 2)])

                fenceB()
                load_WS(l)
                load_WA_mla(l)
                load_wo_rows(l, 0)

                def norm1(i):
                    ts = tsz(i)
                    norm_to_T(l, i, i % 2, 0, 4 * (i % 2), XT[:, :, i * 128:i * 128 + ts], [("XT", c, i) for c in range(8)],
                              jx=i % 2, defer=True, rs_ap=RS[:ts, i, :], rs_key=("RS", i))

                for a in range(0, NT, 2):
                    interleaved([(lambda i=i: norm1(i)) for i in range(a, min(a + 2, NT))])

                for P in range(2 if (dbg & 1) else 0):
                    if P > 0:
                        load_wo_rows(l, P)
                    pend = None
                    for g in range(dbg_groups):
                        prep_group(mla_prep, l, P, g)
                        if pend is not None:
                            wo_partial(l, P, pend)
                        for hh in range(2 if dbg_attn else 0):
                            attn_head("mla", l, P, g, hh)
                        pend = g if dbg_attn else None
                    if pend is not None:
                        wo_partial(l, P, pend)
                for P in range(2, 4 if (dbg & 2) else 2):
                    load_WA_diff(l, P - 2)
                    load_wo_rows(l, P)
                    pend = None
                    for g in range(dbg_groups):
                        prep_group(diff_prep, l, P, g)
                        if pend is not None:
                            wo_partial(l, P, pend)
                        tails = [diff_attn(l, P, g, hh, lam_init) for hh in range(2 if dbg_attn else 0)]
                        if tails:
                            interleaved(tails)
                        pend = g if dbg_attn else None
                    if pend is not None:
                        wo_partial(l, P, pend)
                fenceB()
                load_ffn(l, 0, 0)
                load_ffn(l, 1, 1)

                def norm2(i):
                    ts = tsz(i)
                    norm_to_T(l, i, i % 2, 8, 4 * (i % 2), XT[:, :, i * 128:i * 128 + ts], [("XT", c, i) for c in range(8)],
                              jx=i % 2)

                for a in range(0, NT if (dbg & 4) else 0, 2):
                    interleaved([(lambda i=i: norm2(i)) for i in range(a, min(a + 2, NT))])
                NSL = DFF // 256
                steps = [(sl, g) for sl in range(NSL if (dbg & 8) else 0) for g in range(NG)]
                cnt_gu = [0]

                def ffn_gu(sl, g):
                    slot = sl % 2
                    Wg, Wu, Wd = ffn_views(slot)
                    gs = gsz(g)
                    xkeys = [("XT", c, i) for c in range(8) for i in gtiles(g)]
                    aset = (sl * NG + g) % 2
                    for jc in range(2):
                        gb = 2 * (cnt_gu[0] % 2)
                        ub = gb + 1
                        fa = cnt_gu[0] % 4
                        cnt_gu[0] += 1
                        at = Ht[aset * 2 + jc]
                        for c in range(8):
                            PE(lambda e, c=c, jc=jc, gb=gb: e.matmul(
                                ps[gb][:, 0:gs], Wg[:, c, jc * 128:(jc + 1) * 128], XT[:, c, g * 512:g * 512 + gs],
                                start=(c == 0), stop=(c == 7)), r=xkeys + [("fw", slot), "B"], w=[("ps", gb)])
                        for c in range(8):
                            PE(lambda e, c=c, jc=jc, ub=ub: e.matmul(
                                ps[ub][:, 0:gs], Wu[:, c, jc * 128:(jc + 1) * 128], XT[:, c, g * 512:g * 512 + gs],
                                start=(c == 0), stop=(c == 7)), r=xkeys + [("fw", slot), "B"], w=[("ps", ub)])
                        ACT(lambda e, gb=gb, fa=fa: e.activation(out=Ft[fa][:, 0:gs], in_=ps[gb][:, 0:gs], func=AF.Silu),
                            r=[("ps", gb)], w=[("F", fa)])
                        DVE(lambda e, ub=ub, fa=fa, at=at: e.tensor_tensor(out=at[:, 0:gs], in0=ps[ub][:, 0:gs],
                                                                           in1=Ft[fa][:, 0:gs], op=ALU.mult),
                            r=[("ps", ub), ("F", fa)], w=[("H", aset * 2 + jc)])

                def ffn_down(sl, g):
                    slot = sl % 2
                    Wg, Wu, Wd = ffn_views(slot)
                    aset = (sl * NG + g) % 2
                    for i in gtiles(g):
                        ts = tsz(i)
                        j = i - 4 * g
                        for nb in range(2):
                            bank = 4 + nb + 2 * (i % 2)
                            for jc in range(2):
                                at = Ht[aset * 2 + jc]
                                PE(lambda e, jc=jc, nb=nb, bank=bank, ts=ts, j=j, at=at: e.matmul(
                                    ps[bank][:ts, :], at[:, j * 128:j * 128 + ts], Wd[:, jc, nb * 512:(nb + 1) * 512],
                                    start=(jc == 0), stop=(jc == 1)),
                                    r=[("H", aset * 2 + jc), ("fw", slot), "B"], w=[("ps", bank)])
                            DVE(lambda e, nb=nb, bank=bank, i=i, ts=ts: e.tensor_tensor(
                                out=h[:ts, i, nb * 512:(nb + 1) * 512], in0=ps[bank][:ts, :],
                                in1=h[:ts, i, nb * 512:(nb + 1) * 512], op=ALU.add),
                                r=[("ps", bank), ("h", i)], w=[("h", i)])

                if steps:
                    ffn_gu(*steps[0])
                for k, (sl, g) in enumerate(steps):
                    if k + 1 < len(steps):
                        ffn_gu(*steps[k + 1])
                    ffn_down(sl, g)
                    if g == NG - 1 and sl + 2 < NSL:
                        load_ffn(l, sl + 2, sl % 2)

            DMA("sp", ("out", 0), lambda e, s=s: e.dma_start(out=out_d[s, 0:112, :], in_=h[16:128, 0, :]), r=[("h", 0)])
            for i in range(1, 16):
                DMA("sp", ("out", i), lambda e, s=s, i=i: e.dma_start(out=out_d[s, 128 * i - 16:128 * i + 112, :], in_=h[:, i, :]),
                    r=[("h", i)])
            DMA("sp", ("out", 16), lambda e, s=s: e.dma_start(out=out_d[s, 2032:2048, :], in_=h[0:16, 16, :]), r=[("h", 16)])

        sem_keys = S.resolve()
        sems = {k: es.enter_context(nc.semaphore("s_" + str(i))) for i, k in enumerate(sem_keys)}
        streams = {}
        for op in S.ops:
            streams.setdefault(op.eng, []).append(op)

        def runner(name):
            def f(e):
                for op in streams.get(name, []):
                    for k, v in op.waits:
                        e.wait_ge(sems[k], v)
                    ins = op.fn(e)
                    if op.dma is not None:
                        ins.then_inc(sems[("dma", op.dma)], 16)
                    elif op.needs_inc:
                        ins.then_inc(sems[("eng", op.eng)], 1)
                if name == "sp":
                    for grp, n in S.dma_cnt.items():
                        if isinstance(grp, tuple) and grp[0] == "out":
                            e.wait_ge(sems[("dma", grp)], 16 * n)
            return f

        with nc.Block() as block:
            block.tensor(runner("pe"))
            block.scalar(runner("act"))
            block.vector(runner("dve"))
            block.gpsimd(runner("pool"))
            block.sync(runner("sp"))
    return nc


def _host_consts():
    cst = np.zeros((128, 384), np.float32)
    cst[:, 0:128] = np.eye(128, dtype=np.float32)
    k = np.arange(128)[:, None]
    q = np.arange(128)[None, :]
    cst[:, 128:256] = np.where(k <= q, 0.0, NEG).astype(np.float32)
    cst[:, 256:384] = 1.0
    pos = (np.arange(NT)[None, :] * 128 + np.arange(128)[:, None]).astype(np.float64)
    inv = 1.0 / (10000.0 ** (np.arange(0, 64, 2, dtype=np.float64) / 64.0))
    ang = pos[:, :, None] * inv[None, None, :]
    emb = np.concatenate([ang, ang], axis=-1)
    cos = np.cos(emb).astype(np.float32)
    sin = np.sin(emb).astype(np.float32)
    sinr = sin.copy()
    sinr[:, :, 0:32] = -sin[:, :, 0:32]
    rope = np.concatenate([cos.reshape(128, NT * 64), sinr.reshape(128, NT * 64)], axis=1).astype(np.float32)
    return cst, np.ascontiguousarray(rope)


def _pack_pv(inp):
    pv = np.zeros((DEPTH, 128, NPV), np.float32)
    bc = lambda v: np.broadcast_to(np.asarray(v, np.float32)[None, :], (128, len(v)))
    for l in range(DEPTH):
        pv[l, :, 0:8] = np.asarray(inp["attn_norm"][l]).reshape(8, 128).T
        pv[l, :, 8:16] = np.asarray(inp["ffn_norm"][l]).reshape(8, 128).T
        pv[l, :, 16:18] = np.asarray(inp["mla_q_a_norm"][l]).reshape(2, 128).T
        pv[l, :, 18:19] = np.asarray(inp["mla_kv_a_norm"][l]).reshape(1, 128).T
        pv[l, :, 19:20] = np.asarray(inp["diff_subln"][l]).reshape(1, 128).T
        pv[l, :, 20:212] = bc(inp["mla_q_norm"][l])
        pv[l, :, 212:404] = bc(inp["mla_k_norm"][l])
        pv[l, :, 404:468] = bc(inp["diff_q_norm"][l])
        pv[l, :, 468:532] = bc(inp["diff_k_norm"][l])
        pv[l, :, 532:596] = bc(inp["lambda_q1"][l])
        pv[l, :, 596:660] = bc(inp["lambda_k1"][l])
        pv[l, :, 660:724] = bc(inp["lambda_q2"][l])
        pv[l, :, 724:788] = bc(inp["lambda_k2"][l])
    return pv


def _prep_shared(inp):
    cst, rope = _host_consts()
    wq = np.asarray(inp["w_q_up"], np.float32).reshape(DEPTH, 256, 4, 192)
    wq_p = np.concatenate([wq[..., :128].reshape(DEPTH, 256, 512), wq[..., 128:].reshape(DEPTH, 256, 256)], axis=-1)
    wkv = np.asarray(inp["w_kv_up"], np.float32).reshape(DEPTH, 128, 4, 256)
    wkv_p = np.concatenate([wkv[..., :128].reshape(DEPTH, 128, 512), wkv[..., 128:].reshape(DEPTH, 128, 512)], axis=-1)
    return {
        "meta": np.ascontiguousarray(np.asarray(inp["meta_tokens"], np.float32)),
        "pv": _pack_pv(inp),
        "cst": cst,
        "rope": rope,
        "w_in": np.ascontiguousarray(np.asarray(inp["w_in"], np.float32)),
        "w_qup": np.ascontiguousarray(wq_p),
        "w_kvup": np.ascontiguousarray(wkv_p),
        "w_o": np.ascontiguousarray(np.asarray(inp["w_o"], np.float32)),
        "w_gu": np.ascontiguousarray(np.asarray(inp["w_gate_up"], np.float32)),
        "w_dn": np.ascontiguousarray(np.asarray(inp["w_down"], np.float32)),
    }


_NC_CACHE = {}


def kernel(**inputs):
    x = np.asarray(inputs["x"], np.float32)
    shared = _prep_shared(inputs)
    key = (SEQ_PER_CORE, (0, 1))
    if key not in _NC_CACHE:
        _NC_CACHE[key] = build(SEQ_PER_CORE, (0, 1))
    nc = _NC_CACHE[key]
    in_maps = []
    for c in range(N_CORES):
        m = dict(shared)
        m["x"] = np.ascontiguousarray(x[c * SEQ_PER_CORE:(c + 1) * SEQ_PER_CORE])
        in_maps.append(m)
    res = run_bass_kernel_spmd(nc, in_maps, core_ids=list(range(N_CORES)))
    out = np.concatenate([np.asarray(r["out"], np.float32) for r in res.results], axis=0)
    return out
```

```python
import math
import os
import numpy as np
import concourse.bass as bass
import concourse.mybir as mybir
from concourse.bass_utils import run_bass_kernel_spmd

F32 = mybir.dt.float32
BF16 = mybir.dt.bfloat16
AF = mybir.ActivationFunctionType
ALU = mybir.AluOpType
AX = mybir.AxisListType

D = 1024
SEQ = 2048
NMETA = 16
LTOK = SEQ + NMETA
NT = 17
NG = 5
DFF = 2816
DEPTH = 2
EPS = 1e-6
NPV = 788
NEG = -30000.0
KCUT = int(os.environ.get("KCUT", "99"))
N_CORES = 8
SEQ_PER_CORE = 2


def tsz(i):
    return 128 if i < 16 else 16


def gsz(g):
    return 512 if g < 4 else 16


def gtiles(g):
    return list(range(4 * g, min(4 * g + 4, NT)))


class Op:
    __slots__ = ("eng", "fn", "r", "w", "dma", "deps", "needs_inc", "inc_val", "waits")

    def __init__(self, eng, fn, r, w, dma):
        self.eng = eng
        self.fn = fn
        self.r = r
        self.w = w
        self.dma = dma
        self.deps = ()
        self.needs_inc = False
        self.inc_val = 0
        self.waits = ()


class Sched:
    def __init__(self):
        self.ops = []

    def add(self, eng, fn, r=(), w=(), dma=None):
        w = tuple(w) + tuple(k for k in r if isinstance(k, tuple) and k[0] == "ps" and k not in w)
        self.ops.append(Op(eng, fn, tuple(r), w, dma))

    def resolve(self):
        ops = self.ops
        last_w = {}
        readers = {}
        for i, op in enumerate(ops):
            deps = set()
            for k in op.r:
                j = last_w.get(k)
                if j is not None:
                    deps.add(j)
            for k in op.w:
                j = last_w.get(k)
                if j is not None:
                    deps.add(j)
                rd = readers.get(k)
                if rd:
                    deps.update(rd.values())
            deps.discard(i)
            op.deps = deps
            src = ("dma", op.dma) if op.dma is not None else op.eng
            for k in op.r:
                readers.setdefault(k, {})[src] = i
            for k in op.w:
                last_w[k] = i
                readers[k] = {}

        def skip(pj, op):
            if pj.dma is None and op.dma is None:
                return pj.eng == "pe" and op.eng == "pe"
            return pj.dma is not None and op.dma is not None and pj.dma == op.dma

        for op in ops:
            for j in op.deps:
                pj = ops[j]
                if pj.dma is None and not skip(pj, op):
                    pj.needs_inc = True
        cnt = {}
        for op in ops:
            if op.dma is None and op.needs_inc:
                cnt[op.eng] = cnt.get(op.eng, 0) + 1
                op.inc_val = cnt[op.eng]
        dma_cnt = {}
        waited = {}
        for op in ops:
            waits = {}
            for j in op.deps:
                pj = ops[j]
                if skip(pj, op):
                    continue
                if pj.dma is None:
                    key = ("eng", pj.eng)
                    val = pj.inc_val
                else:
                    key = ("dma", pj.dma)
                    val = 16 * dma_cnt[pj.dma]
                if waits.get(key, 0) < val:
                    waits[key] = val
            we = waited.setdefault(op.eng, {})
            op.waits = [(k, v) for k, v in waits.items() if we.get(k, 0) < v]
            for k, v in op.waits:
                we[k] = v
            if op.dma is not None:
                dma_cnt[op.dma] = dma_cnt.get(op.dma, 0) + 1
        self.dma_cnt = dma_cnt
        keys = set(("eng", e) for e in cnt)
        keys.update(("dma", g) for g in dma_cnt)
        return sorted(keys, key=str)


def build(n_seq=SEQ_PER_CORE, layers=(0, 1), dbg=15, dbg_groups=NG, dbg_attn=True):
    nc = bass.Bass("TRN2", target_bir_lowering=False)
    x_d = nc.dram_tensor("x", [n_seq, SEQ, D], F32, kind="ExternalInput").ap()
    meta_d = nc.dram_tensor("meta", [NMETA, D], F32, kind="ExternalInput").ap()
    pv_d = nc.dram_tensor("pv", [DEPTH, 128, NPV], F32, kind="ExternalInput").ap()
    cst_d = nc.dram_tensor("cst", [128, 384], F32, kind="ExternalInput").ap()
    rope_d = nc.dram_tensor("rope", [128, 2 * NT * 64], F32, kind="ExternalInput").ap()
    win_d = nc.dram_tensor("w_in", [DEPTH, D, 1984], F32, kind="ExternalInput").ap()
    wq_d = nc.dram_tensor("w_qup", [DEPTH, 256, 768], F32, kind="ExternalInput").ap()
    wkv_d = nc.dram_tensor("w_kvup", [DEPTH, 128, 1024], F32, kind="ExternalInput").ap()
    wo_d = nc.dram_tensor("w_o", [DEPTH, D, D], F32, kind="ExternalInput").ap()
    wgu_d = nc.dram_tensor("w_gu", [DEPTH, D, 2 * DFF], F32, kind="ExternalInput").ap()
    wdn_d = nc.dram_tensor("w_dn", [DEPTH, DFF, D], F32, kind="ExternalInput").ap()
    out_d = nc.dram_tensor("out", [n_seq, SEQ, D], F32, kind="ExternalOutput").ap()

    S = Sched()

    def PE(fn, r=(), w=()):
        S.add("pe", fn, r, w)

    def ACT(fn, r=(), w=()):
        S.add("act", fn, r, w)

    def DVE(fn, r=(), w=()):
        S.add("dve", fn, r, w)

    def DMA(eng, grp, fn, r=(), w=()):
        S.add(eng, fn, r, w, dma=grp)

    import contextlib
    with contextlib.ExitStack() as es:
        def sb(name, shape, dt):
            return es.enter_context(nc.sbuf_tensor(name, shape, dt))

        h = sb("h", [128, NT, D], F32)
        XT = sb("XT", [128, 8, LTOK], BF16)
        WA = sb("WA", [128, 8, 1024], BF16)
        WS = sb("WS", [128, 2560], BF16)
        B = sb("B", [128, 12288], BF16)
        pv = sb("pv_sb", [128, DEPTH, NPV], F32)
        cst = sb("cst_bf", [128, 384], BF16)
        ones_f = sb("ones_f", [128, 128], F32)
        rope = sb("rope_sb", [128, 2 * NT * 64], F32)
        junk = [sb(f"junk{p}", [128, 1024], BF16) for p in range(2)]
        fence_t = sb("fence_t", [128, 8], F32)
        lamt = sb("lamt", [128, 8], F32)
        hs = [sb(f"hs{p}", [128, 1024], BF16) for p in range(2)]
        RS = sb("RS", [128, NT, 2], F32)
        OG = [sb(f"OG{p}", [128, 2, 512], BF16) for p in range(2)]
        latn = [sb(f"latn{p}", [128, 384], BF16) for p in range(2)]
        latT = [sb(f"latT{p}", [128, 3, 128], BF16) for p in range(2)]
        kr_sb = [sb(f"kr{p}", [128, 64], F32) for p in range(2)]
        qb = [sb(f"qb{p}", [128, 512], BF16) for p in range(2)]
        kb_ = [sb(f"kb{p}", [128, 384], BF16) for p in range(2)]
        st = [sb(f"st{p}", [128, 64], F32) for p in range(2)]
        Ft = [sb(f"F{n}", [128, 512], F32) for n in range(8)] + [sb(f"F{n}", [128, 384 if n in (8, 11) else 64], F32) for n in range(8, 14)]
        Ht = [sb(f"H{n}", [128, 512], BF16) for n in range(4)]
        ps = [es.enter_context(nc.psum_tensor(f"ps{b}", [128, 512], F32)) for b in range(8)]
        psb = [p.bitcast(BF16) for p in ps]

        ident = cst[:, 0:128]
        maskb = cst[:, 128:256]
        ones_b = cst[:, 256:384]

        def b3(off, a, b):
            return B[:, off:off + a * b].rearrange("p (a b) -> p a b", a=a)

        KnT = b3(0, 2, LTOK)
        KrT = B[:, 4128:6192]
        Vv = b3(6192, NT, 256)
        QnT = b3(10544, 2, 512)
        QrT = B[:, 11568:12080]

        def ffn_views(slot):
            base = slot * 6144
            return (b3(base, 8, 256), b3(base + 2048, 8, 256), b3(base + 4096, 2, 1024))

        DMA("sp", "c0", lambda e: e.dma_start(out=pv[:], in_=pv_d.rearrange("l p n -> p l n")), w=["pv"])
        DMA("sp", "c0", lambda e: e.dma_start(out=rope[:], in_=rope_d), w=["rope"])
        DMA("pool", "c1", lambda e: e.dma_start(out=cst[:], in_=cst_d), w=["cst"])
        DVE(lambda e: e.memset(ones_f[:], 1.0), w=["onesf"])
        DVE(lambda e: e.memset(fence_t[:], 0.0), w=["B"])

        def fenceB():
            DVE(lambda e: e.memset(fence_t[:], 0.0), w=["B"])

        cosv = lambda i, ts: rope[:ts, i * 64:(i + 1) * 64]
        sinv = lambda i, ts: rope[:ts, NT * 64 + i * 64: NT * 64 + (i + 1) * 64]

        def norm_to_T(l, i, par, gcol, bank, dst_ap, dst_keys, jx=0, defer=False, rs_ap=None, rs_key=None):
            ts = tsz(i)
            s_ = st[par]
            if defer:
                ACT(lambda e: e.copy(out=hs[par][:ts, :], in_=h[:ts, i, :]), r=[("h", i)], w=[("hs", par)])
            ACT(lambda e: e.activation(out=junk[jx][:ts, :], in_=h[:ts, i, :], func=AF.Square,
                                       accum_out=s_[:ts, 0:1]), r=[("h", i)], w=[("junk", jx), ("st", par, 0)])
            ACT(lambda e: e.activation(out=s_[:ts, 1:2], in_=s_[:ts, 0:1], func=AF.Ln, bias=EPS, scale=1.0 / D),
                r=[("st", par, 0)], w=[("st", par, 1)])
            if defer:
                ACT(lambda e: e.activation(out=rs_ap[:, 0:1], in_=s_[:ts, 1:2], func=AF.Exp, scale=-0.5),
                    r=[("st", par, 1)], w=[rs_key])
                ACT(lambda e: e.activation(out=rs_ap[:, 1:2], in_=s_[:ts, 1:2], func=AF.Exp, bias=math.log(EPS), scale=1.0),
                    r=[("st", par, 1)], w=[rs_key])
            else:
                ACT(lambda e: e.activation(out=s_[:ts, 2:3], in_=s_[:ts, 1:2], func=AF.Exp, scale=-0.5),
                    r=[("st", par, 1)], w=[("st", par, 2)])
                ACT(lambda e: e.mul(out=hs[par][:ts, :], in_=h[:ts, i, :], mul=s_[:ts, 2:3]),
                    r=[("h", i), ("st", par, 2)], w=[("hs", par)])
            for c in range(8):
                PE(lambda e, c=c: e.transpose(psb[bank][:, c * 128:c * 128 + ts], hs[par][:ts, c * 128:(c + 1) * 128],
                                              ident[:ts, :ts]),
                   r=[("hs", par), "cst"], w=[("ps", bank)])
            src = psb[bank][:, :].rearrange("p (c t) -> p c t", c=8)[:, :, :ts]
            gains = pv[:, l, gcol:gcol + 8].unsqueeze(2).broadcast_to([128, 8, ts])
            DVE(lambda e: e.tensor_tensor(out=dst_ap, in0=src, in1=gains, op=ALU.mult),
                r=[("ps", bank), "pv"], w=dst_keys)

        def rope_apply(src3, nh, i, ts, fa, fb, out_ap, rkeys, wkeys, tagk):
            a3 = Ft[fa][:ts, 0:nh * 64].rearrange("p (a d) -> p a d", a=nh)
            b3_ = Ft[fb][:ts, 0:nh * 64].rearrange("p (a d) -> p a d", a=nh)
            cb = cosv(i, ts).unsqueeze(1).broadcast_to([ts, nh, 64])
            s_lo = sinv(i, ts)[:, 0:32].unsqueeze(1).broadcast_to([ts, nh, 32])
            s_hi = sinv(i, ts)[:, 32:64].unsqueeze(1).broadcast_to([ts, nh, 32])
            DVE(lambda e: e.tensor_tensor(out=a3, in0=src3, in1=cb, op=ALU.mult),
                r=rkeys + ["rope"], w=[("F", fa)])
            DVE(lambda e: e.tensor_tensor(out=b3_[:, :, 0:32], in0=src3[:, :, 32:64], in1=s_lo, op=ALU.mult),
                r=rkeys + ["rope"], w=[("F", fb)])
            DVE(lambda e: e.tensor_tensor(out=b3_[:, :, 32:64], in0=src3[:, :, 0:32], in1=s_hi, op=ALU.mult),
                r=rkeys + ["rope"], w=[("F", fb)])
            DVE(lambda e: e.tensor_tensor(out=out_ap, in0=Ft[fa][:ts, 0:nh * 64], in1=Ft[fb][:ts, 0:nh * 64], op=ALU.add),
                r=[("F", fa), ("F", fb), ("F", fb)], w=wkeys)

        def mla_prep(l, P, g, i, stage="all", pk=0):
            ts = tsz(i)
            j = i - 4 * g
            par = i % 2
            s_ = st[par]
            Bk = ["B"]
            X0, X1, X2, X3 = (4 * par + k for k in range(4))
            Fa, Fb, Fc, Fd = (4 * par + k for k in range(4))
            Fk, Fk2, Fk3 = 8 + 3 * par, 9 + 3 * par, 10 + 3 * par
            jk = junk[par]
            jkey = ("junk", par)
            WB = X0 if pk == 0 else X1
            if stage in ("all", "win"):
                for c in range(8):
                    PE(lambda e, c=c: e.matmul(ps[WB][:ts, 0:448], XT[:, c, i * 128:i * 128 + ts], WA[:, c, 0:448],
                                               start=(c == 0), stop=(c == 7)),
                       r=[("XT", c, i), ("WA", 0)], w=[("ps", WB)])
            if stage == "win":
                return
            ACT(lambda e: e.activation(out=jk[:ts, 0:256], in_=ps[WB][:ts, 0:256], func=AF.Square,
                                       accum_out=s_[:ts, 3:4]), r=[("ps", WB)], w=[jkey, ("st", par, 3)])
            ACT(lambda e: e.activation(out=jk[:ts, 0:128], in_=ps[WB][:ts, 256:384], func=AF.Square,
                                       accum_out=s_[:ts, 4:5]), r=[("ps", WB)], w=[jkey, ("st", par, 4)])
            ACT(lambda e: e.activation(out=s_[:ts, 5:6], in_=s_[:ts, 3:4], func=AF.Ln, bias=RS[:ts, i, 1:2], scale=1.0 / 256),
                r=[("st", par, 3), ("RS", i)], w=[("st", par, 5)])
            ACT(lambda e: e.activation(out=s_[:ts, 6:7], in_=s_[:ts, 4:5], func=AF.Ln, bias=RS[:ts, i, 1:2], scale=1.0 / 128),
                r=[("st", par, 4), ("RS", i)], w=[("st", par, 6)])
            ACT(lambda e: e.activation(out=s_[:ts, 7:9], in_=s_[:ts, 5:7], func=AF.Exp, scale=-0.5),
                r=[("st", par, 5), ("st", par, 6)], w=[("st", par, 7)])
            ACT(lambda e: e.mul(out=latn[par][:ts, 0:256], in_=ps[WB][:ts, 0:256], mul=s_[:ts, 7:8]),
                r=[("ps", WB), ("st", par, 7)], w=[("latn", par, 0)])
            ACT(lambda e: e.mul(out=latn[par][:ts, 256:384], in_=ps[WB][:ts, 256:384], mul=s_[:ts, 8:9]),
                r=[("ps", WB), ("st", par, 7)], w=[("latn", par, 1)])
            ACT(lambda e: e.mul(out=kr_sb[par][:ts, :], in_=ps[WB][:ts, 384:448], mul=RS[:ts, i, 0:1]),
                r=[("ps", WB), ("RS", i)], w=[("kr", par)])
            for cb in range(3):
                PE(lambda e, cb=cb: e.transpose(psb[X2][:, cb * 128:cb * 128 + ts], latn[par][:ts, cb * 128:(cb + 1) * 128],
                                                ident[:ts, :ts]),
                   r=[("latn", par, 0), ("latn", par, 1), "cst"], w=[("ps", X2)])
            DVE(lambda e: e.tensor_tensor(out=latT[par][:, :, :ts],
                                          in0=psb[X2][:, 0:384].rearrange("p (c t) -> p c t", c=3)[:, :, :ts],
                                          in1=pv[:, l, 16:19].unsqueeze(2).broadcast_to([128, 3, ts]), op=ALU.mult),
                r=[("ps", X2), "pv"], w=[("latT", par)])
            Wq = WS[:, 0:1536].rearrange("p (c n) -> p c n", c=2)
            Wkv = WS[:, 1536:2560]
            for c in range(2):
                PE(lambda e, c=c: e.matmul(ps[X2][:ts, 0:256], latT[par][:, c, :ts], Wq[:, c, 256 * P:256 * P + 256],
                                           start=(c == 0), stop=(c == 1)),
                   r=[("latT", par), "WS"], w=[("ps", X2)])
            for c in range(2):
                PE(lambda e, c=c: e.matmul(ps[X2][:ts, 256:384], latT[par][:, c, :ts],
                                           Wq[:, c, 512 + 128 * P:512 + 128 * P + 128],
                                           start=(c == 0), stop=(c == 1)),
                   r=[("latT", par), "WS"], w=[("ps", X2)])
            PE(lambda e: e.matmul(ps[X3][:ts, 0:256], latT[par][:, 2, :ts], Wkv[:, 256 * P:256 * P + 256],
                                  start=True, stop=True), r=[("latT", par), "WS"], w=[("ps", X3)])
            PE(lambda e: e.matmul(ps[X3][:ts, 256:512], latT[par][:, 2, :ts], Wkv[:, 512 + 256 * P:512 + 256 * P + 256],
                                  start=True, stop=True), r=[("latT", par), "WS"], w=[("ps", X3)])
            def qpath():
                ACT(lambda e: e.activation(out=Ft[Fa][:ts, 0:384], in_=ps[X2][:ts, 0:384], func=AF.Square),
                    r=[("ps", X2)], w=[("F", Fa)])
                DVE(lambda e: e.reduce_sum(out=s_[:ts, 9:11], in_=Ft[Fa][:ts, 0:256].rearrange("p (a d) -> p a d", a=2),
                                           axis=AX.X), r=[("F", Fa)], w=[("st", par, 9)])
                DVE(lambda e: e.reduce_sum(out=s_[:ts, 11:13], in_=Ft[Fa][:ts, 256:384].rearrange("p (a d) -> p a d", a=2),
                                           axis=AX.X), r=[("F", Fa)], w=[("st", par, 11)])
                DVE(lambda e: e.tensor_tensor(out=s_[:ts, 13:15], in0=s_[:ts, 9:11], in1=s_[:ts, 11:13], op=ALU.add),
                    r=[("st", par, 9), ("st", par, 11)], w=[("st", par, 13)])
                ACT(lambda e: e.activation(out=s_[:ts, 15:17], in_=s_[:ts, 13:15], func=AF.Ln, bias=EPS, scale=1.0 / 192),
                    r=[("st", par, 13)], w=[("st", par, 15)])
                ACT(lambda e: e.activation(out=s_[:ts, 17:19], in_=s_[:ts, 15:17], func=AF.Exp, scale=-0.5),
                    r=[("st", par, 15)], w=[("st", par, 17)])
                for hh in range(2):
                    DVE(lambda e, hh=hh: e.scalar_tensor_tensor(
                        out=qb[par][:ts, hh * 128:(hh + 1) * 128], in0=ps[X2][:ts, hh * 128:(hh + 1) * 128],
                        scalar=s_[:ts, 17 + hh:18 + hh], in1=pv[:ts, l, 20:148], op0=ALU.mult, op1=ALU.mult),
                        r=[("ps", X2), ("st", par, 17), "pv"], w=[("qb", par, hh)])
                    DVE(lambda e, hh=hh: e.scalar_tensor_tensor(
                        out=Ft[Fb][:ts, hh * 64:(hh + 1) * 64], in0=ps[X2][:ts, 256 + hh * 64:256 + (hh + 1) * 64],
                        scalar=s_[:ts, 17 + hh:18 + hh], in1=pv[:ts, l, 148:212], op0=ALU.mult, op1=ALU.mult),
                        r=[("ps", X2), ("st", par, 17), "pv"], w=[("F", Fb)])
                rope_apply(Ft[Fb][:ts, 0:128].rearrange("p (a d) -> p a d", a=2), 2, i, ts, Fc, Fd,
                           qb[par][:ts, 256:384], [("F", Fb)], [("qb", par, 2)], "q")

            def kpath():
                ACT(lambda e: e.activation(out=Ft[Fk][:ts, 0:256], in_=ps[X3][:ts, 0:256], func=AF.Square),
                    r=[("ps", X3)], w=[("F", Fk)])
                ACT(lambda e: e.activation(out=jk[:ts, 0:64], in_=kr_sb[par][:ts, :], func=AF.Square,
                                           accum_out=s_[:ts, 19:20]), r=[("kr", par)], w=[jkey, ("st", par, 19)])
                DVE(lambda e: e.reduce_sum(out=s_[:ts, 20:22], in_=Ft[Fk][:ts, 0:256].rearrange("p (a d) -> p a d", a=2),
                                           axis=AX.X), r=[("F", Fk)], w=[("st", par, 20)])
                DVE(lambda e: e.tensor_scalar_add(out=s_[:ts, 22:24], in0=s_[:ts, 20:22], scalar1=s_[:ts, 19:20]),
                    r=[("st", par, 20), ("st", par, 19)], w=[("st", par, 22)])
                ACT(lambda e: e.activation(out=s_[:ts, 24:26], in_=s_[:ts, 22:24], func=AF.Ln, bias=EPS, scale=1.0 / 192),
                    r=[("st", par, 22)], w=[("st", par, 24)])
                ACT(lambda e: e.activation(out=s_[:ts, 26:28], in_=s_[:ts, 24:26], func=AF.Exp, scale=-0.5),
                    r=[("st", par, 24)], w=[("st", par, 26)])
                for hh in range(2):
                    DVE(lambda e, hh=hh: e.scalar_tensor_tensor(
                        out=kb_[par][:ts, hh * 128:(hh + 1) * 128], in0=ps[X3][:ts, hh * 128:(hh + 1) * 128],
                        scalar=s_[:ts, 26 + hh:27 + hh], in1=pv[:ts, l, 212:340], op0=ALU.mult, op1=ALU.mult),
                        r=[("ps", X3), ("st", par, 26), "pv"], w=[("kb", par, hh)])
                DVE(lambda e: e.tensor_tensor(out=Ft[Fk][:ts, 256:320], in0=kr_sb[par][:ts, :], in1=pv[:ts, l, 340:404], op=ALU.mult),
                    r=[("kr", par), "pv"], w=[("F", Fk)])
                rope_apply(Ft[Fk][:ts, 256:320].rearrange("p (a d) -> p a d", a=1), 1, i, ts, Fk2, Fk3,
                           Ft[Fk][:ts, 320:384], [("F", Fk)], [("F", Fk)], "k")
                for hh in range(2):
                    DVE(lambda e, hh=hh: e.tensor_scalar_mul(out=kb_[par][:ts, 256 + hh * 64:256 + (hh + 1) * 64],
                                                             in0=Ft[Fk][:ts, 320:384], scalar1=s_[:ts, 26 + hh:27 + hh]),
                        r=[("F", Fk), ("st", par, 26)], w=[("kb", par, 2 + hh)])

            interleaved([qpath, kpath])
            ACT(lambda e: e.copy(out=Vv[:ts, i, :], in_=ps[X3][:ts, 256:512]), r=[("ps", X3)] + Bk, w=[("V", i)])
            for blk in range(3):
                PE(lambda e, blk=blk: e.transpose(psb[X2][:, blk * 128:blk * 128 + ts], qb[par][:ts, blk * 128:(blk + 1) * 128],
                                                  ident[:ts, :ts]),
                   r=[("qb", par, 0), ("qb", par, 1), ("qb", par, 2), "cst"], w=[("ps", X2)])
            ACT(lambda e: e.copy(out=QnT[:, :, j * 128:j * 128 + ts],
                                 in_=psb[X2][:, 0:256].rearrange("p (c t) -> p c t", c=2)[:, :, :ts]),
                r=[("ps", X2)] + Bk, w=[("Qn", j)])
            ACT(lambda e: e.copy(out=QrT[:, j * 128:j * 128 + ts], in_=psb[X2][:, 256:256 + ts]),
                r=[("ps", X2)] + Bk, w=[("Qr", j)])
            for blk in range(3):
                PE(lambda e, blk=blk: e.transpose(psb[X3][:, blk * 128:blk * 128 + ts], kb_[par][:ts, blk * 128:(blk + 1) * 128],
                                                  ident[:ts, :ts]),
                   r=[("kb", par, 0), ("kb", par, 1), ("kb", par, 2), ("kb", par, 3), "cst"], w=[("ps", X3)])
            DVE(lambda e: e.tensor_copy(out=KnT[:, :, i * 128:i * 128 + ts],
                                        in_=psb[X3][:, 0:256].rearrange("p (c t) -> p c t", c=2)[:, :, :ts]),
                r=[("ps", X3)] + Bk, w=[("Kn", i)])
            DVE(lambda e: e.tensor_copy(out=KrT[:, i * 128:i * 128 + ts], in_=psb[X3][:, 256:256 + ts]),
                r=[("ps", X3)] + Bk, w=[("Kr", i)])

        cnt_att = [0]

        def attn_head(kind, l, P, g, hh, c2=0):
            gs = gsz(g)
            nkb = 4 * g + 4 if g < 4 else NT
            cidx = cnt_att[0]
            cnt_att[0] += 1
            o_ps = 3 + (cidx % 2)
            d_ps = 5 + (cidx % 2)
            acc = 4 + (cidx % 2)
            fr = 6 + (cidx % 2)
            if kind == "mla":
                scale = 192.0 ** -0.5
                qkeys = [("Qn", jj) for jj in range(len(gtiles(g)))] + [("Qr", jj) for jj in range(len(gtiles(g)))] + ["B"]
            else:
                scale = 0.125
                qkeys = [("Qn", jj) for jj in range(len(gtiles(g)))] + ["B"]

            SB = (0, 1, 2, 7)

            def qlo(kb):
                return 128 * (kb - 4 * g) if (g < 4 and kb >= 4 * g) else 0

            def s_mm(kb):
                ks = tsz(kb)
                bank = SB[kb % 4]
                lo = qlo(kb)
                diag = kb >= 4 * g
                dw = min(128, gs - lo)
                if kind == "mla":
                    PE(lambda e: e.matmul(ps[bank][:ks, lo:gs], KnT[:, hh, kb * 128:kb * 128 + ks], QnT[:, hh, lo:gs],
                                          start=True, stop=False),
                       r=[("Kn", kb)] + qkeys, w=[("ps", bank)])
                    if diag:
                        PE(lambda e: e.matmul(ps[bank][:ks, lo:lo + dw], ident[:ks, :ks], maskb[:ks, 0:dw],
                                              start=False, stop=False), r=["cst"], w=[("ps", bank)])
                    PE(lambda e: e.matmul(ps[bank][:ks, lo:gs], KrT[64 * hh:64 * hh + 64, kb * 128:kb * 128 + ks],
                                          QrT[64 * hh:64 * hh + 64, lo:gs], start=False, stop=True),
                       r=[("Kr", kb)] + qkeys, w=[("ps", bank)])
                else:
                    PE(lambda e: e.matmul(ps[bank][:ks, lo:gs], KnT[64 * c2:64 * c2 + 64, hh, kb * 128:kb * 128 + ks],
                                          QnT[64 * c2:64 * c2 + 64, hh, lo:gs], start=True, stop=(not diag)),
                       r=[("Kn", kb)] + qkeys, w=[("ps", bank)])
                    if diag:
                        PE(lambda e: e.matmul(ps[bank][:ks, lo:lo + dw], ident[:ks, :ks], maskb[:ks, 0:dw],
                                              start=False, stop=True), r=["cst"], w=[("ps", bank)])

            def rest(kb):
                ks = tsz(kb)
                bank = SB[kb % 4]
                lo = qlo(kb)
                pt = kb % 4
                ACT(lambda e: e.activation(out=Ht[pt][:ks, lo:gs], in_=ps[bank][:ks, lo:gs], func=AF.Exp, scale=scale),
                    r=[("ps", bank)], w=[("H", pt)])
                PE(lambda e: e.matmul(ps[o_ps][:, lo:gs], Vv[:ks, kb, hh * 128:(hh + 1) * 128], Ht[pt][:ks, lo:gs],
                                      start=(kb == 0), stop=(kb == nkb - 1)),
                   r=[("V", kb), ("H", pt), "B"], w=[("ps", o_ps)])
                PE(lambda e: e.matmul(ps[d_ps][:, lo:gs], ones_b[:ks, :], Ht[pt][:ks, lo:gs],
                                      start=(kb == 0), stop=(kb == nkb - 1)),
                   r=["cst", ("H", pt)], w=[("ps", d_ps)])

            TAIL_UNITS = [[0, 1, 2, 3], [4, 5, 6, 7], [8, 9, 10, 11], [12, 13, 14, 15], [16]]

            def s_unit(u):
                bank = SB[u % 4]
                for jj, kb in enumerate(TAIL_UNITS[u]):
                    ks = tsz(kb)
                    c0 = 16 * jj
                    PE(lambda e, kb=kb, ks=ks, c0=c0: e.matmul(ps[bank][:ks, c0:c0 + 16], KnT[:, hh, kb * 128:kb * 128 + ks],
                                                               QnT[:, hh, 0:16], start=True, stop=False),
                       r=[("Kn", kb)] + qkeys, w=[("ps", bank)])
                    if kb == 16:
                        PE(lambda e, c0=c0: e.matmul(ps[bank][:16, c0:c0 + 16], ident[:16, :16], maskb[:16, 0:16],
                                                     start=False, stop=False), r=["cst"], w=[("ps", bank)])
                    PE(lambda e, kb=kb, ks=ks, c0=c0: e.matmul(ps[bank][:ks, c0:c0 + 16],
                                                               KrT[64 * hh:64 * hh + 64, kb * 128:kb * 128 + ks],
                                                               QrT[64 * hh:64 * hh + 64, 0:16], start=False, stop=True),
                       r=[("Kr", kb)] + qkeys, w=[("ps", bank)])

            def rest_unit(u):
                bank = SB[u % 4]
                pt = u % 4
                kbs = TAIL_UNITS[u]
                rows = tsz(kbs[0])
                width = 16 * len(kbs)
                ACT(lambda e: e.activation(out=Ht[pt][:rows, 0:width], in_=ps[bank][:rows, 0:width], func=AF.Exp, scale=scale),
                    r=[("ps", bank)], w=[("H", pt)])
                for jj, kb in enumerate(kbs):
                    ks = tsz(kb)
                    c0 = 16 * jj
                    PE(lambda e, kb=kb, ks=ks, c0=c0: e.matmul(ps[o_ps][:, 0:16], Vv[:ks, kb, hh * 128:(hh + 1) * 128],
                                                               Ht[pt][:ks, c0:c0 + 16], start=(kb == 0), stop=(kb == 16)),
                       r=[("V", kb), ("H", pt), "B"], w=[("ps", o_ps)])
                    PE(lambda e, ks=ks, c0=c0, kb=kb: e.matmul(ps[d_ps][:, 0:16], ones_b[:ks, :], Ht[pt][:ks, c0:c0 + 16],
                                                               start=(kb == 0), stop=(kb == 16)),
                       r=["cst", ("H", pt)], w=[("ps", d_ps)])

            LA = 3
            if g == 4:
                nun = len(TAIL_UNITS)
                for u in range(min(LA, nun)):
                    s_unit(u)
                for u in range(nun):
                    if u + LA < nun:
                        s_unit(u + LA)
                    rest_unit(u)
            else:
                for kb in range(min(LA, nkb)):
                    s_mm(kb)
                for kb in range(nkb):
                    if kb + LA < nkb:
                        s_mm(kb + LA)
                    rest(kb)
            ACT(lambda e: e.activation(out=Ft[fr][:, 0:gs], in_=ps[d_ps][:, 0:gs], func=AF.Ln), r=[("ps", d_ps)], w=[("F", fr)])
            ACT(lambda e: e.activation(out=Ft[fr][:, 0:gs], in_=Ft[fr][:, 0:gs], func=AF.Exp, scale=-1.0),
                r=[("F", fr)], w=[("F", fr)])
            if kind == "mla":
                Hh = 2 * P + hh
                DVE(lambda e: e.tensor_tensor(out=OG[g % 2][:, hh, 0:gs], in0=ps[o_ps][:, 0:gs], in1=Ft[fr][:, 0:gs],
                                              op=ALU.mult),
                    r=[("ps", o_ps), ("F", fr)], w=[("OG", g % 2, hh)])
                return
            Hd = 2 * (P - 2) + hh
            if c2 == 0:
                DVE(lambda e: e.tensor_tensor(out=Ft[2][:, 0:gs], in0=ps[o_ps][:, 0:gs], in1=Ft[fr][:, 0:gs], op=ALU.mult),
                    r=[("ps", o_ps), ("F", fr)], w=[("F", 2)])
                return
            DVE(lambda e: e.tensor_tensor(out=Ft[3][:, 0:gs], in0=ps[o_ps][:, 0:gs], in1=Ft[fr][:, 0:gs], op=ALU.mult),
                r=[("ps", o_ps), ("F", fr)], w=[("F", 3)])
            DVE(lambda e: e.scalar_tensor_tensor(out=Ft[1][:, 0:gs], in0=Ft[3][:, 0:gs], scalar=lamt[:, 1:2],
                                                 in1=Ft[2][:, 0:gs], op0=ALU.mult, op1=ALU.add),
                r=[("F", 2), ("F", 3), ("lamt", 1)], w=[("F", 1)])
            ACT(lambda e: e.activation(out=Ft[3][:, 0:gs], in_=Ft[1][:, 0:gs], func=AF.Square),
                r=[("F", 1)], w=[("F", 3)])
            PE(lambda e: e.matmul(ps[d_ps][:, 0:gs], ones_f[:, :], Ft[3][:, 0:gs], start=True, stop=True),
               r=["onesf", ("F", 3)], w=[("ps", d_ps)])
            ACT(lambda e: e.activation(out=Ft[2][:, 0:gs], in_=ps[d_ps][:, 0:gs], func=AF.Ln, bias=EPS, scale=1.0 / 128),
                r=[("ps", d_ps)], w=[("F", 2)])
            ACT(lambda e: e.activation(out=Ft[2][:, 0:gs], in_=Ft[2][:, 0:gs], func=AF.Exp, scale=-0.5),
                r=[("F", 2)], w=[("F", 2)])
            DVE(lambda e: e.scalar_tensor_tensor(out=XT[:, 4 + Hd, g * 512:g * 512 + gs], in0=Ft[1][:, 0:gs],
                                                 scalar=lamt[:, 2:3], in1=Ft[2][:, 0:gs], op0=ALU.mult, op1=ALU.mult),
                r=[("F", 1), ("F", 2), ("lamt", 2)], w=[("XT", 4 + Hd, i) for i in gtiles(g)])

        def diff_prep(l, P, g, i, stage="all", pk=0):
            ts = tsz(i)
            j = i - 4 * g
            par = i % 2
            s_ = st[par]
            Bk = ["B"]
            X0, X1, X2, X3 = (4 * par + k for k in range(4))
            Fa, Fb, Fc, Fd = (4 * par + k for k in range(4))
            WQ, WV = (X0, X1) if pk == 0 else (X2, X3)
            if stage in ("all", "win"):
                for c in range(8):
                    PE(lambda e, c=c: e.matmul(ps[WQ][:ts, 0:512], XT[:, c, i * 128:i * 128 + ts], WA[:, c, 0:512],
                                               start=(c == 0), stop=(c == 7)),
                       r=[("XT", c, i), ("WA", 0), ("WA", 1)], w=[("ps", WQ)])
                for c in range(8):
                    PE(lambda e, c=c: e.matmul(ps[WV][:ts, 0:256], XT[:, c, i * 128:i * 128 + ts], WA[:, c, 512:768],
                                               start=(c == 0), stop=(c == 7)),
                       r=[("XT", c, i), ("WA", 2)], w=[("ps", WV)])
            if stage == "win":
                return
            ACT(lambda e: e.activation(out=Ft[Fa][:ts, :], in_=ps[WQ][:ts, :], func=AF.Square),
                r=[("ps", WQ)], w=[("F", Fa)])
            DVE(lambda e: e.reduce_sum(out=s_[:ts, 30:38], in_=Ft[Fa][:ts, :].rearrange("p (a d) -> p a d", a=8), axis=AX.X),
                r=[("F", Fa)], w=[("st", par, 30)])
            ACT(lambda e: e.activation(out=s_[:ts, 38:46], in_=s_[:ts, 30:38], func=AF.Ln, bias=RS[:ts, i, 1:2], scale=1.0 / 64),
                r=[("st", par, 30), ("RS", i)], w=[("st", par, 38)])
            ACT(lambda e: e.activation(out=s_[:ts, 46:54], in_=s_[:ts, 38:46], func=AF.Exp, scale=-0.5),
                r=[("st", par, 38)], w=[("st", par, 46)])
            DVE(lambda e: e.tensor_tensor(out=Ft[Fb][:ts, :].rearrange("p (a d) -> p a d", a=8),
                                          in0=ps[WQ][:ts, :].rearrange("p (a d) -> p a d", a=8),
                                          in1=s_[:ts, 46:54].unsqueeze(2).broadcast_to([ts, 8, 64]), op=ALU.mult),
                r=[("ps", WQ), ("st", par, 46)], w=[("F", Fb)])
            DVE(lambda e: e.tensor_tensor(out=Ft[Fa][:ts, :].rearrange("p (q a d) -> p q a d", q=2, a=4),
                                          in0=Ft[Fb][:ts, :].rearrange("p (q a d) -> p q a d", q=2, a=4),
                                          in1=pv[:ts, l, 404:532].rearrange("p (q d) -> p q d", q=2).unsqueeze(2)
                                          .broadcast_to([ts, 2, 4, 64]), op=ALU.mult),
                r=[("F", Fb), "pv"], w=[("F", Fa)])
            rope_apply(Ft[Fa][:ts, :].rearrange("p (a d) -> p a d", a=8), 8, i, ts, Fc, Fd,
                       qb[par][:ts, 0:512], [("F", Fa)],
                       [("qb", par, 0), ("qb", par, 1), ("qb", par, 2)], "d")
            ACT(lambda e: e.mul(out=Vv[:ts, i, :], in_=ps[WV][:ts, 0:256], mul=RS[:ts, i, 0:1]),
                r=[("ps", WV), ("RS", i)] + Bk, w=[("V", i)])
            for blk in range(2):
                PE(lambda e, blk=blk: e.transpose(psb[WQ][:, blk * 128:blk * 128 + ts], qb[par][:ts, blk * 128:(blk + 1) * 128],
                                                  ident[:ts, :ts]),
                   r=[("qb", par, 0), ("qb", par, 1), ("qb", par, 2), "cst"], w=[("ps", WQ)])
            ACT(lambda e: e.copy(out=QnT[:, :, j * 128:j * 128 + ts],
                                 in_=psb[WQ][:, 0:256].rearrange("p (c t) -> p c t", c=2)[:, :, :ts]),
                r=[("ps", WQ)] + Bk, w=[("Qn", j)])
            for blk in range(2):
                PE(lambda e, blk=blk: e.transpose(psb[WV][:, blk * 128:blk * 128 + ts],
                                                  qb[par][:ts, (2 + blk) * 128:(3 + blk) * 128], ident[:ts, :ts]),
                   r=[("qb", par, 0), ("qb", par, 1), ("qb", par, 2), "cst"], w=[("ps", WV)])
            DVE(lambda e: e.tensor_copy(out=KnT[:, :, i * 128:i * 128 + ts],
                                        in_=psb[WV][:, 0:256].rearrange("p (c t) -> p c t", c=2)[:, :, :ts]),
                r=[("ps", WV)] + Bk, w=[("Kn", i)])

        def diff_attn(l, P, g, hh, lam_init):
            gs = gsz(g)
            nkb = 4 * g + 4 if g < 4 else NT
            Hd = 2 * (P - 2) + hh
            scale = 0.125
            qkeys = [("Qn", jj) for jj in range(len(gtiles(g)))] + ["B"]

            def qlo(kb):
                return 128 * (kb - 4 * g) if (g < 4 and kb >= 4 * g) else 0

            def s_mm(kb):
                ks = tsz(kb)
                lo = qlo(kb)
                diag = kb >= 4 * g
                for c2 in range(2):
                    bank = 2 * c2 + (kb % 2)
                    PE(lambda e, c2=c2, bank=bank: e.matmul(
                        ps[bank][:ks, lo:gs], KnT[64 * c2:64 * c2 + 64, hh, kb * 128:kb * 128 + ks],
                        QnT[64 * c2:64 * c2 + 64, hh, lo:gs], start=True, stop=(not diag)),
                        r=[("Kn", kb)] + qkeys, w=[("ps", bank)])
                    if diag:
                        dw = min(128, gs - lo)
                        PE(lambda e, bank=bank, dw=dw: e.matmul(ps[bank][:ks, lo:lo + dw], ident[:ks, :ks], maskb[:ks, 0:dw],
                                                                start=False, stop=True), r=["cst"], w=[("ps", bank)])

            def rest(kb):
                ks = tsz(kb)
                lo = qlo(kb)
                for c2 in range(2):
                    bank = 2 * c2 + (kb % 2)
                    pt = 2 * c2 + (kb % 2)
                    ACT(lambda e, bank=bank, pt=pt: e.activation(out=Ht[pt][:ks, lo:gs], in_=ps[bank][:ks, lo:gs],
                                                                 func=AF.Exp, scale=scale),
                        r=[("ps", bank)], w=[("H", pt)])
                for c2 in range(2):
                    pt = 2 * c2 + (kb % 2)
                    PE(lambda e, c2=c2, pt=pt: e.matmul(ps[4 + c2][:, lo:gs], Vv[:ks, kb, hh * 128:(hh + 1) * 128],
                                                        Ht[pt][:ks, lo:gs], start=(kb == 0), stop=(kb == nkb - 1)),
                       r=[("V", kb), ("H", pt), "B"], w=[("ps", 4 + c2)])
                    PE(lambda e, c2=c2, pt=pt: e.matmul(ps[6 + c2][:, lo:gs], ones_b[:ks, :], Ht[pt][:ks, lo:gs],
                                                        start=(kb == 0), stop=(kb == nkb - 1)),
                       r=["cst", ("H", pt)], w=[("ps", 6 + c2)])

            TAIL_UNITS = [[0, 1, 2, 3], [4, 5, 6, 7], [8, 9, 10, 11], [12, 13, 14, 15], [16]]

            def s_unit(u):
                for c2 in range(2):
                    bank = 2 * c2 + (u % 2)
                    for jj, kb in enumerate(TAIL_UNITS[u]):
                        ks = tsz(kb)
                        c0 = 16 * jj
                        PE(lambda e, c2=c2, bank=bank, kb=kb, ks=ks, c0=c0: e.matmul(
                            ps[bank][:ks, c0:c0 + 16], KnT[64 * c2:64 * c2 + 64, hh, kb * 128:kb * 128 + ks],
                            QnT[64 * c2:64 * c2 + 64, hh, 0:16], start=True, stop=(kb != 16)),
                            r=[("Kn", kb)] + qkeys, w=[("ps", bank)])
                        if kb == 16:
                            PE(lambda e, bank=bank, c0=c0: e.matmul(ps[bank][:16, c0:c0 + 16], ident[:16, :16], maskb[:16, 0:16],
                                                                    start=False, stop=True), r=["cst"], w=[("ps", bank)])

            def rest_unit(u):
                kbs = TAIL_UNITS[u]
                rows = tsz(kbs[0])
                width = 16 * len(kbs)
                for c2 in range(2):
                    bank = 2 * c2 + (u % 2)
                    pt = 2 * c2 + (u % 2)
                    ACT(lambda e, bank=bank, pt=pt: e.activation(out=Ht[pt][:rows, 0:width], in_=ps[bank][:rows, 0:width],
                                                                 func=AF.Exp, scale=scale),
                        r=[("ps", bank)], w=[("H", pt)])
                for c2 in range(2):
                    pt = 2 * c2 + (u % 2)
                    for jj, kb in enumerate(kbs):
                        ks = tsz(kb)
                        c0 = 16 * jj
                        PE(lambda e, c2=c2, pt=pt, kb=kb, ks=ks, c0=c0: e.matmul(
                            ps[4 + c2][:, 0:16], Vv[:ks, kb, hh * 128:(hh + 1) * 128], Ht[pt][:ks, c0:c0 + 16],
                            start=(kb == 0), stop=(kb == 16)),
                            r=[("V", kb), ("H", pt), "B"], w=[("ps", 4 + c2)])
                        PE(lambda e, c2=c2, pt=pt, kb=kb, ks=ks, c0=c0: e.matmul(
                            ps[6 + c2][:, 0:16], ones_b[:ks, :], Ht[pt][:ks, c0:c0 + 16],
                            start=(kb == 0), stop=(kb == 16)),
                            r=["cst", ("H", pt)], w=[("ps", 6 + c2)])

            if g == 4:
                nun = len(TAIL_UNITS)
                s_unit(0)
                for u in range(nun):
                    if u + 1 < nun:
                        s_unit(u + 1)
                    rest_unit(u)
            else:
                s_mm(0)
                for kb in range(nkb):
                    if kb + 1 < nkb:
                        s_mm(kb + 1)
                    rest(kb)
            comb = 4 + hh
            for c2 in range(2):
                ACT(lambda e, c2=c2: e.activation(out=Ft[c2][:, 0:gs], in_=ps[6 + c2][:, 0:gs], func=AF.Ln),
                    r=[("ps", 6 + c2)], w=[("F", c2)])
                DVE(lambda e, c2=c2: e.tensor_copy(out=Ft[2 + c2][:, 0:gs], in_=ps[4 + c2][:, 0:gs]),
                    r=[("ps", 4 + c2)], w=[("F", 2 + c2)])
            for c2 in range(2):
                ACT(lambda e, c2=c2: e.activation(out=Ft[c2][:, 0:gs], in_=Ft[c2][:, 0:gs], func=AF.Exp, scale=-1.0),
                    r=[("F", c2)], w=[("F", c2)])
                DVE(lambda e, c2=c2: e.tensor_tensor(out=Ft[2 + c2][:, 0:gs], in0=Ft[2 + c2][:, 0:gs], in1=Ft[c2][:, 0:gs],
                                                     op=ALU.mult),
                    r=[("F", 2 + c2), ("F", c2)], w=[("F", 2 + c2)])
            DVE(lambda e: e.scalar_tensor_tensor(out=Ft[comb][:, 0:gs], in0=Ft[3][:, 0:gs], scalar=lamt[:, 1:2],
                                                 in1=Ft[2][:, 0:gs], op0=ALU.mult, op1=ALU.add),
                r=[("F", 2), ("F", 3), ("lamt", 1)], w=[("F", comb)])

            def part2():
                sq = 6 + hh
                ACT(lambda e: e.activation(out=Ft[sq][:, 0:gs], in_=Ft[comb][:, 0:gs], func=AF.Square),
                    r=[("F", comb)], w=[("F", sq)])
                PE(lambda e: e.matmul(ps[hh][:, 0:gs], ones_f[:, :], Ft[sq][:, 0:gs], start=True, stop=True),
                   r=["onesf", ("F", sq)], w=[("ps", hh)])
                ACT(lambda e: e.activation(out=Ft[sq][:, 0:gs], in_=ps[hh][:, 0:gs], func=AF.Ln, bias=EPS, scale=1.0 / 128),
                    r=[("ps", hh)], w=[("F", sq)])
                ACT(lambda e: e.activation(out=Ft[sq][:, 0:gs], in_=Ft[sq][:, 0:gs], func=AF.Exp, scale=-0.5),
                    r=[("F", sq)], w=[("F", sq)])
                DVE(lambda e: e.scalar_tensor_tensor(out=OG[g % 2][:, hh, 0:gs], in0=Ft[comb][:, 0:gs],
                                                     scalar=lamt[:, 2:3], in1=Ft[sq][:, 0:gs], op0=ALU.mult, op1=ALU.mult),
                    r=[("F", comb), ("F", sq), ("lamt", 2)], w=[("OG", g % 2, hh)])
            return part2

        def load_WA_mla(l):
            for c in range(8):
                DMA("pool", "WA", lambda e, c=c: e.dma_start(out=WA[:, c, 0:448], in_=win_d[l, c * 128:(c + 1) * 128, 0:448]),
                    w=[("WA", 0)])

        def load_WA_diff(l, Pd):
            for c in range(8):
                DMA("pool", "WA", lambda e, c=c: e.dma_start(
                    out=WA[:, c, 0:768].rearrange("p (a n) -> p a n", a=3),
                    in_=win_d[l, c * 128:(c + 1) * 128, 448:1984].rearrange("p (a n) -> p a n", a=3)[:, :, 256 * Pd:256 * Pd + 256]),
                    w=[("WA", 0), ("WA", 1), ("WA", 2)])

        def load_wo_rows(l, P):
            for kc in range(2):
                DMA("pool", "WA", lambda e, kc=kc: e.dma_start(
                    out=WA[:, kc * 4:kc * 4 + 4, 768:1024],
                    in_=wo_d[l, 256 * P + 128 * kc:256 * P + 128 * kc + 128, :].rearrange("p (a n) -> p a n", a=4)),
                    w=[("WA", 3)])

        def wo_partial(l, P, g):
            og = OG[g % 2]
            for i in gtiles(g):
                ts = tsz(i)
                j = i - 4 * g
                for nb in range(2):
                    bank = (2 * i + nb) % 3
                    for half in range(2):
                        q4 = 2 * nb + half
                        for kc in range(2):
                            PE(lambda e, kc=kc, q4=q4, half=half, bank=bank, ts=ts, j=j: e.matmul(
                                ps[bank][:ts, half * 256:(half + 1) * 256], og[:, kc, j * 128:j * 128 + ts],
                                WA[:, kc * 4 + q4, 768:1024], start=(kc == 0), stop=(kc == 1)),
                                r=[("OG", g % 2, 0), ("OG", g % 2, 1), ("WA", 3)], w=[("ps", bank)])
                    DVE(lambda e, nb=nb, bank=bank, i=i, ts=ts: e.tensor_tensor(
                        out=h[:ts, i, nb * 512:(nb + 1) * 512], in0=ps[bank][:ts, :], in1=h[:ts, i, nb * 512:(nb + 1) * 512],
                        op=ALU.add), r=[("ps", bank), ("h", i)], w=[("h", i)])

        def load_WS(l):
            for c in range(2):
                DMA("pool", "WS", lambda e, c=c: e.dma_start(out=WS[:, c * 768:(c + 1) * 768],
                                                             in_=wq_d[l, c * 128:(c + 1) * 128, :]),
                    w=["WS"])
            DMA("pool", "WS", lambda e: e.dma_start(out=WS[:, 1536:2560], in_=wkv_d[l, :, :]), w=["WS"])

        def load_ffn(l, sl, slot):
            Wg, Wu, Wd = ffn_views(slot)
            base = slot * 6144
            gu = B[:, base:base + 4096].rearrange("p (a c n) -> p a c n", a=2, c=8)
            for c in range(8):
                DMA("pool", ("ffn", slot), lambda e, c=c: e.dma_start(
                    out=gu[:, :, c, :],
                    in_=wgu_d[l, c * 128:(c + 1) * 128, :].rearrange("p (a n) -> p a n", a=2)[:, :, 256 * sl:256 * sl + 256]),
                    r=["B"], w=[("fw", slot)])
            for jc in range(2):
                DMA("pool", ("ffn", slot), lambda e, jc=jc: e.dma_start(
                    out=Wd[:, jc, :], in_=wdn_d[l, 256 * sl + 128 * jc:256 * sl + 128 * jc + 128, :]),
                    r=["B"], w=[("fw", slot)])


        def interleaved(fns):
            main = S.ops
            chains = []
            for fn in fns:
                S.ops = []
                fn()
                chains.append(S.ops)
            S.ops = main
            for k in range(max(len(c) for c in chains)):
                for c in chains:
                    if k < len(c):
                        main.append(c[k])

        def prep_group(fn, l, P, g):
            tl = gtiles(g)
            pairs = [tl[a:a + 2] for a in range(0, len(tl), 2)]
            for pk, pair in enumerate(pairs):
                interleaved([(lambda i=i, pk=pk: fn(l, P, g, i, "win", pk)) for i in pair])
            for pk, pair in enumerate(pairs):
                interleaved([(lambda i=i, pk=pk: fn(l, P, g, i, "rest", pk)) for i in pair])

        for s in range(n_seq):
            DMA("pool", ("x", 0), lambda e: e.dma_start(out=h[0:16, 0, :], in_=meta_d), w=[("h", 0)])
            DMA("pool", ("x", 0), lambda e, s=s: e.dma_start(out=h[16:128, 0, :], in_=x_d[s, 0:112, :]), w=[("h", 0)])
            for i in range(1, 16):
                DMA("pool", ("x", i), lambda e, s=s, i=i: e.dma_start(out=h[:, i, :], in_=x_d[s, 128 * i - 16:128 * i + 112, :]),
                    w=[("h", i)])
            DMA("pool", ("x", 16), lambda e, s=s: e.dma_start(out=h[0:16, 16, :], in_=x_d[s, 2032:2048, :]), w=[("h", 16)])

            for l in layers:
                lam_init = 0.8 - 0.6 * math.exp(-0.3 * l)
                DVE(lambda e, l=l: e.tensor_tensor(out=Ft[0][:, 0:64], in0=pv[:, l, 532:596], in1=pv[:, l, 596:660], op=ALU.mult),
                    r=["pv"], w=[("F", 0)])
                DVE(lambda e, l=l: e.tensor_tensor(out=Ft[0][:, 256:320], in0=pv[:, l, 660:724], in1=pv[:, l, 724:788], op=ALU.mult),
                    r=["pv"], w=[("F", 0)])
                DVE(lambda e: e.reduce_sum(out=lamt[:, 4:5], in_=Ft[0][:, 0:64], axis=AX.X), r=[("F", 0)], w=[("lamt", 4)])
                DVE(lambda e: e.reduce_sum(out=lamt[:, 5:6], in_=Ft[0][:, 256:320], axis=AX.X), r=[("F", 0)], w=[("lamt", 5)])
                ACT(lambda e: e.activation(out=lamt[:, 6:8], in_=lamt[:, 4:6], func=AF.Exp),
                    r=[("lamt", 4), ("lamt", 5)], w=[("lamt", 6)])
                DVE(lambda e: e.tensor_tensor(out=lamt[:, 0:1], in0=lamt[:, 7:8], in1=lamt[:, 6:7], op=ALU.subtract),
                    r=[("lamt", 6)], w=[("lamt", 0)])
                DVE(lambda e, li=lam_init: e.tensor_scalar_add(out=lamt[:, 1:2], in0=lamt[:, 0:1], scalar1=-li),
                    r=[("lamt", 0)], w=[("lamt", 1)])
                DVE(lambda e, li=lam_init, l=l: e.tensor_scalar_mul(out=lamt[:, 2:3], in0=pv[:, l, 19:20], scalar1=1.0 - li),
                    r=["pv"], w=[("lamt", 2)])

                fenceB()
                load_WS(l)
                load_WA_mla(l)
                load_wo_rows(l, 0)

                def norm1(i):
                    ts = tsz(i)
                    norm_to_T(l, i, i % 2, 0, 4 * (i % 2), XT[:, :, i * 128:i * 128 + ts], [("XT", c, i) for c in range(8)],
                              jx=i % 2, defer=True, rs_ap=RS[:ts, i, :], rs_key=("RS", i))

                for a in range(0, NT, 2):
                    interleaved([(lambda i=i: norm1(i)) for i in range(a, min(a + 2, NT))])

                for P in range(2 if (dbg & 1) else 0):
                    if P > 0:
                        load_wo_rows(l, P)
                    pend = None
                    for g in range(dbg_groups):
                        prep_group(mla_prep, l, P, g)
                        if pend is not None:
                            wo_partial(l, P, pend)
                        for hh in range(2 if dbg_attn else 0):
                            attn_head("mla", l, P, g, hh)
                        pend = g if dbg_attn else None
                    if pend is not None:
                        wo_partial(l, P, pend)
                for P in range(2, 4 if (dbg & 2) else 2):
                    load_WA_diff(l, P - 2)
                    load_wo_rows(l, P)
                    pend = None
                    for g in range(dbg_groups):
                        prep_group(diff_prep, l, P, g)
                        if pend is not None:
                            wo_partial(l, P, pend)
                        tails = [diff_attn(l, P, g, hh, lam_init) for hh in range(2 if dbg_attn else 0)]
                        if tails:
                            interleaved(tails)
                        pend = g if dbg_attn else None
                    if pend is not None:
                        wo_partial(l, P, pend)
                fenceB()
                load_ffn(l, 0, 0)
                load_ffn(l, 1, 1)

                def norm2(i):
                    ts = tsz(i)
                    norm_to_T(l, i, i % 2, 8, 4 * (i % 2), XT[:, :, i * 128:i * 128 + ts], [("XT", c, i) for c in range(8)],
                              jx=i % 2)

                for a in range(0, NT if (dbg & 4) else 0, 2):
                    interleaved([(lambda i=i: norm2(i)) for i in range(a, min(a + 2, NT))])
                NSL = DFF // 256
                steps = [(sl, g) for sl in range(NSL if (dbg & 8) else 0) for g in range(NG)]
                cnt_gu = [0]

                def ffn_gu(sl, g):
                    slot = sl % 2
                    Wg, Wu, Wd = ffn_views(slot)
                    gs = gsz(g)
                    xkeys = [("XT", c, i) for c in range(8) for i in gtiles(g)]
                    aset = (sl * NG + g) % 2
                    for jc in range(2):
                        gb = 2 * (cnt_gu[0] % 2)
                        ub = gb + 1
                        fa = cnt_gu[0] % 4
                        cnt_gu[0] += 1
                        at = Ht[aset * 2 + jc]
                        for c in range(8):
                            PE(lambda e, c=c, jc=jc, gb=gb: e.matmul(
                                ps[gb][:, 0:gs], Wg[:, c, jc * 128:(jc + 1) * 128], XT[:, c, g * 512:g * 512 + gs],
                                start=(c == 0), stop=(c == 7)), r=xkeys + [("fw", slot), "B"], w=[("ps", gb)])
                        for c in range(8):
                            PE(lambda e, c=c, jc=jc, ub=ub: e.matmul(
                                ps[ub][:, 0:gs], Wu[:, c, jc * 128:(jc + 1) * 128], XT[:, c, g * 512:g * 512 + gs],
                                start=(c == 0), stop=(c == 7)), r=xkeys + [("fw", slot), "B"], w=[("ps", ub)])
                        ACT(lambda e, gb=gb, fa=fa: e.activation(out=Ft[fa][:, 0:gs], in_=ps[gb][:, 0:gs], func=AF.Silu),
                            r=[("ps", gb)], w=[("F", fa)])
                        DVE(lambda e, ub=ub, fa=fa, at=at: e.tensor_tensor(out=at[:, 0:gs], in0=ps[ub][:, 0:gs],
                                                                           in1=Ft[fa][:, 0:gs], op=ALU.mult),
                            r=[("ps", ub), ("F", fa)], w=[("H", aset * 2 + jc)])

                def ffn_down(sl, g):
                    slot = sl % 2
                    Wg, Wu, Wd = ffn_views(slot)
                    aset = (sl * NG + g) % 2
                    for i in gtiles(g):
                        ts = tsz(i)
                        j = i - 4 * g
                        for nb in range(2):
                            bank = 4 + nb + 2 * (i % 2)
                            for jc in range(2):
                                at = Ht[aset * 2 + jc]
                                PE(lambda e, jc=jc, nb=nb, bank=bank, ts=ts, j=j, at=at: e.matmul(
                                    ps[bank][:ts, :], at[:, j * 128:j * 128 + ts], Wd[:, jc, nb * 512:(nb + 1) * 512],
                                    start=(jc == 0), stop=(jc == 1)),
                                    r=[("H", aset * 2 + jc), ("fw", slot), "B"], w=[("ps", bank)])
                            DVE(lambda e, nb=nb, bank=bank, i=i, ts=ts: e.tensor_tensor(
                                out=h[:ts, i, nb * 512:(nb + 1) * 512], in0=ps[bank][:ts, :],
                                in1=h[:ts, i, nb * 512:(nb + 1) * 512], op=ALU.add),
                                r=[("ps", bank), ("h", i)], w=[("h", i)])

                if steps:
                    ffn_gu(*steps[0])
                for k, (sl, g) in enumerate(steps):
                    if k + 1 < len(steps):
                        ffn_gu(*steps[k + 1])
                    ffn_down(sl, g)
                    if g == NG - 1 and sl + 2 < NSL:
                        load_ffn(l, sl + 2, sl % 2)

            DMA("sp", ("out", 0), lambda e, s=s: e.dma_start(out=out_d[s, 0:112, :], in_=h[16:128, 0, :]), r=[("h", 0)])
            for i in range(1, 16):
                DMA("sp", ("out", i), lambda e, s=s, i=i: e.dma_start(out=out_d[s, 128 * i - 16:128 * i + 112, :], in_=h[:, i, :]),
                    r=[("h", i)])
            DMA("sp", ("out", 16), lambda e, s=s: e.dma_start(out=out_d[s, 2032:2048, :], in_=h[0:16, 16, :]), r=[("h", 16)])

        sem_keys = S.resolve()
        sems = {k: es.enter_context(nc.semaphore("s_" + str(i))) for i, k in enumerate(sem_keys)}
        streams = {}
        for op in S.ops:
            streams.setdefault(op.eng, []).append(op)

        def runner(name):
            def f(e):
                for op in streams.get(name, []):
                    for k, v in op.waits:
                        e.wait_ge(sems[k], v)
                    ins = op.fn(e)
                    if op.dma is not None:
                        ins.then_inc(sems[("dma", op.dma)], 16)
                    elif op.needs_inc:
                        ins.then_inc(sems[("eng", op.eng)], 1)
                if name == "sp":
                    for grp, n in S.dma_cnt.items():
                        if isinstance(grp, tuple) and grp[0] == "out":
                            e.wait_ge(sems[("dma", grp)], 16 * n)
            return f

        with nc.Block() as block:
            block.tensor(runner("pe"))
            block.scalar(runner("act"))
            block.vector(runner("dve"))
            block.gpsimd(runner("pool"))
            block.sync(runner("sp"))
    return nc


def _host_consts():
    cst = np.zeros((128, 384), np.float32)
    cst[:, 0:128] = np.eye(128, dtype=np.float32)
    k = np.arange(128)[:, None]
    q = np.arange(128)[None, :]
    cst[:, 128:256] = np.where(k <= q, 0.0, NEG).astype(np.float32)
    cst[:, 256:384] = 1.0
    pos = (np.arange(NT)[None, :] * 128 + np.arange(128)[:, None]).astype(np.float32)
    inv = (1.0 / (np.float32(10000.0) ** (np.arange(0, 64, 2, dtype=np.float32) / np.float32(64)))).astype(np.float32)
    ang = pos[:, :, None] * inv[None, None, :]
    emb = np.concatenate([ang, ang], axis=-1).astype(np.float32)
    cos = np.cos(emb).astype(np.float32)
    sin = np.sin(emb).astype(np.float32)
    sinr = sin.copy()
    sinr[:, :, 0:32] = -sin[:, :, 0:32]
    rope = np.concatenate([cos.reshape(128, NT * 64), sinr.reshape(128, NT * 64)], axis=1).astype(np.float32)
    return cst, np.ascontiguousarray(rope)


def _pack_pv(inp):
    pv = np.zeros((DEPTH, 128, NPV), np.float32)
    bc = lambda v: np.broadcast_to(np.asarray(v, np.float32)[None, :], (128, len(v)))
    for l in range(DEPTH):
        pv[l, :, 0:8] = np.asarray(inp["attn_norm"][l]).reshape(8, 128).T
        pv[l, :, 8:16] = np.asarray(inp["ffn_norm"][l]).reshape(8, 128).T
        pv[l, :, 16:18] = np.asarray(inp["mla_q_a_norm"][l]).reshape(2, 128).T
        pv[l, :, 18:19] = np.asarray(inp["mla_kv_a_norm"][l]).reshape(1, 128).T
        pv[l, :, 19:20] = np.asarray(inp["diff_subln"][l]).reshape(1, 128).T
        pv[l, :, 20:212] = bc(inp["mla_q_norm"][l])
        pv[l, :, 212:404] = bc(inp["mla_k_norm"][l])
        pv[l, :, 404:468] = bc(inp["diff_q_norm"][l])
        pv[l, :, 468:532] = bc(inp["diff_k_norm"][l])
        pv[l, :, 532:596] = bc(inp["lambda_q1"][l])
        pv[l, :, 596:660] = bc(inp["lambda_k1"][l])
        pv[l, :, 660:724] = bc(inp["lambda_q2"][l])
        pv[l, :, 724:788] = bc(inp["lambda_k2"][l])
    return pv


def _prep_shared(inp):
    cst, rope = _host_consts()
    wq = np.asarray(inp["w_q_up"], np.float32).reshape(DEPTH, 256, 4, 192)
    wq_p = np.concatenate([wq[..., :128].reshape(DEPTH, 256, 512), wq[..., 128:].reshape(DEPTH, 256, 256)], axis=-1)
    wkv = np.asarray(inp["w_kv_up"], np.float32).reshape(DEPTH, 128, 4, 256)
    wkv_p = np.concatenate([wkv[..., :128].reshape(DEPTH, 128, 512), wkv[..., 128:].reshape(DEPTH, 128, 512)], axis=-1)
    return {
        "meta": np.ascontiguousarray(np.asarray(inp["meta_tokens"], np.float32)),
        "pv": _pack_pv(inp),
        "cst": cst,
        "rope": rope,
        "w_in": np.ascontiguousarray(np.asarray(inp["w_in"], np.float32)),
        "w_qup": np.ascontiguousarray(wq_p),
        "w_kvup": np.ascontiguousarray(wkv_p),
        "w_o": np.ascontiguousarray(np.asarray(inp["w_o"], np.float32)),
        "w_gu": np.ascontiguousarray(np.asarray(inp["w_gate_up"], np.float32)),
        "w_dn": np.ascontiguousarray(np.asarray(inp["w_down"], np.float32)),
    }


_NC_CACHE = {}


def kernel(**inputs):
    x = np.asarray(inputs["x"], np.float32)
    shared = _prep_shared(inputs)
    key = (SEQ_PER_CORE, (0, 1))
    if key not in _NC_CACHE:
        _NC_CACHE[key] = build(SEQ_PER_CORE, (0, 1))
    nc = _NC_CACHE[key]
    in_maps = []
    for c in range(N_CORES):
        m = dict(shared)
        m["x"] = np.ascontiguousarray(x[c * SEQ_PER_CORE:(c + 1) * SEQ_PER_CORE])
        in_maps.append(m)
    res = run_bass_kernel_spmd(nc, in_maps, core_ids=list(range(N_CORES)))
    out = np.concatenate([np.asarray(r["out"], np.float32) for r in res.results], axis=0)
    return out
```
